# Optimizing a Trainium2 kernel written in Bass

```python
import math
import jax, jax.numpy as jnp
from jax import lax
import numpy as np

D_MODEL = 1024
BATCH = 16
SEQ = 256
DEPTH = 4
DEC_BATCH = 8
DEC_SEQ = 2048
PAST_LEN = 256

GRID_W = 64
MIX = D_MODEL
SSD_INNER = MIX // 2
SSD_HEAD_DIM = 64
SSD_HEADS = SSD_INNER // SSD_HEAD_DIM
SSD_GROUPS = 2
SSD_STATE = 128
SSD_CONV = 5
SSD_CHUNK = 128
SSD_CONV_CH = SSD_INNER + 2 * SSD_GROUPS * SSD_STATE
SC_WIDTH = MIX // 4
SC_CONV = 3
DIFF_WIDTH = MIX // 4
DIFF_HEADS = 4
DIFF_HD = DIFF_WIDTH // DIFF_HEADS // 2
DIFF_VD = 2 * DIFF_HD
Q_BLOCK = 128
ROPE_BASE = 10000.0
ROPE_F = DIFF_HD // 4
D_FF = 4 * D_MODEL
IN_COLS = SSD_INNER + SSD_CONV_CH + 2 * SSD_HEADS + 3 * SC_WIDTH + 3 * DIFF_WIDTH
ALPHA = (2 * DEPTH) ** 0.25
BETA = (8 * DEPTH) ** -0.25
LN_EPS = 1e-5
RMS_EPS = 1e-6

kernel_name = "hybrid_ssd_conv_diffattn_dit_step"


def _in_split_points():
    sizes = (SSD_INNER, SSD_CONV_CH, 2 * SSD_HEADS, SC_WIDTH, SC_WIDTH, SC_WIDTH, DIFF_WIDTH, DIFF_WIDTH)
    pts = []
    acc = 0
    for s in sizes:
        acc += s
        pts.append(acc)
    return pts


def layer_norm(x, g, b):
    xf = x.astype(jnp.float32)
    mu = jnp.mean(xf, axis=-1, keepdims=True)
    var = jnp.mean(jnp.square(xf - mu), axis=-1, keepdims=True)
    return ((xf - mu) * lax.rsqrt(var + LN_EPS) * g.astype(jnp.float32) + b.astype(jnp.float32)).astype(x.dtype)


def rms_norm(x, g):
    xf = x.astype(jnp.float32)
    y = xf * lax.rsqrt(jnp.mean(jnp.square(xf), axis=-1, keepdims=True) + RMS_EPS)
    return (y * g.astype(jnp.float32)).astype(x.dtype)


def dwconv(x, w):
    width = w.shape[0]
    return lax.conv_general_dilated(
        x, w[:, None, :].astype(x.dtype), window_strides=(1,),
        padding=[(width // 2, width // 2)], dimension_numbers=('NWC', 'WIO', 'NWC'),
        feature_group_count=x.shape[-1])


def axial_angles(rows):
    row = jnp.repeat(jnp.arange(rows, dtype=jnp.float32), GRID_W)
    col = jnp.tile(jnp.arange(GRID_W, dtype=jnp.float32), rows)
    inv = ROPE_BASE ** (-jnp.arange(ROPE_F, dtype=jnp.float32) / ROPE_F)
    return jnp.stack([row[:, None] * inv, col[:, None] * inv], axis=1)


def rope2d(x, theta):
    b, n, h, d = x.shape
    xf = x.astype(jnp.float32).reshape(b, n, h, 2, 2, ROPE_F)
    x1, x2 = xf[..., 0, :], xf[..., 1, :]
    cos = jnp.cos(theta)[None, :, None]
    sin = jnp.sin(theta)[None, :, None]
    out = jnp.stack([x1 * cos - x2 * sin, x2 * cos + x1 * sin], axis=-2)
    return out.reshape(b, n, h, d).astype(x.dtype)


def diff_attention(q, k, v, lam):
    b, n = q.shape[0], q.shape[1]
    nb = n // Q_BLOCK
    qh = jnp.moveaxis(q.reshape(b, nb, Q_BLOCK, DIFF_HEADS, 2, DIFF_HD), 1, 0)
    kh = k.reshape(b, k.shape[1], DIFF_HEADS, 2, DIFF_HD)
    scale = DIFF_HD ** -0.5

    def block(qb):
        s = jnp.einsum('bqhsd,bkhsd->bhsqk', qb, kh).astype(jnp.float32) * scale
        pr = jax.nn.softmax(s, axis=-1)
        pd = pr[:, :, 0] - lam * pr[:, :, 1]
        return jnp.einsum('bhqk,bkhv->bqhv', pd.astype(v.dtype), v)

    o = lax.map(block, qh)
    return jnp.moveaxis(o, 0, 1).reshape(b, n, DIFF_HEADS, DIFF_VD)


def ssd_scan(x, dt, a, bm, cm, s0):
    b, n = x.shape[0], x.shape[1]
    nc = n // SSD_CHUNK
    e = SSD_HEADS // SSD_GROUPS
    xs = x.astype(jnp.float32).reshape(b, nc, SSD_CHUNK, SSD_GROUPS, e, SSD_HEAD_DIM)
    dts = dt.astype(jnp.float32).reshape(b, nc, SSD_CHUNK, SSD_GROUPS, e)
    bs = bm.astype(jnp.float32).reshape(b, nc, SSD_CHUNK, SSD_GROUPS, SSD_STATE)
    cs = cm.astype(jnp.float32).reshape(b, nc, SSD_CHUNK, SSD_GROUPS, SSD_STATE)
    acum = jnp.cumsum(dts * a.astype(jnp.float32).reshape(SSD_GROUPS, e), axis=2)
    xdt = xs * dts[..., None]
    mask = jnp.tril(jnp.ones((SSD_CHUNK, SSD_CHUNK), dtype=bool))
    seg = acum[:, :, :, None] - acum[:, :, None, :]
    decay = jnp.exp(jnp.where(mask[:, :, None, None], seg, -jnp.inf))
    cb = jnp.einsum('bcign,bcjgn->bcijg', cs, bs)
    y_diag = jnp.einsum('bcijge,bcjgep->bcigep', cb[..., None] * decay, xdt)
    decay_to_end = jnp.exp(acum[:, :, -1:] - acum)
    states = jnp.einsum('bcjgn,bcjge,bcjgep->bcgepn', bs, decay_to_end, xdt)
    chunk_decay = jnp.exp(acum[:, :, -1])

    def step(s, inp):
        st, dec = inp
        return dec[..., None, None] * s + st, s

    s_init = s0.astype(jnp.float32).reshape(b, SSD_GROUPS, e, SSD_HEAD_DIM, SSD_STATE)
    s_fin, s_prev = lax.scan(step, s_init, (jnp.moveaxis(states, 1, 0), jnp.moveaxis(chunk_decay, 1, 0)))
    s_prev = jnp.moveaxis(s_prev, 0, 1)
    y_off = jnp.einsum('bcign,bcgepn,bcige->bcigep', cs, s_prev, jnp.exp(acum))
    y = (y_diag + y_off).reshape(b, n, SSD_HEADS, SSD_HEAD_DIM)
    return y, s_fin.reshape(b, SSD_HEADS, SSD_HEAD_DIM, SSD_STATE)


def mixer(u, p, lam_init, theta, prefix_k, prefix_v, s0_f, s0_b):
    b, n = u.shape[0], u.shape[1]
    proj = u @ p['w_in']
    z, xbc, dt, sc_b, sc_c, sc_h, q, k, v = jnp.split(proj, _in_split_points(), axis=-1)

    xbc = jax.nn.silu(dwconv(xbc, p['ssd_conv_w']) + p['ssd_conv_b'])
    xs, bm, cm = jnp.split(xbc, [SSD_INNER, SSD_INNER + SSD_GROUPS * SSD_STATE], axis=-1)
    xs = xs.reshape(b, n, SSD_HEADS, SSD_HEAD_DIM)
    bm = bm.reshape(b, n, SSD_GROUPS, SSD_STATE)
    cm = cm.reshape(b, n, SSD_GROUPS, SSD_STATE)
    dt = jax.nn.softplus(dt.astype(jnp.float32).reshape(b, n, 2, SSD_HEADS) + p['ssd_dt_bias'].astype(jnp.float32))
    a = -jnp.exp(p['ssd_a_log'].astype(jnp.float32))
    flip = lambda t: jnp.flip(t, axis=1)
    y_f, s_f = ssd_scan(xs, dt[:, :, 0], a[0], bm, cm, s0_f)
    y_b, s_b = ssd_scan(flip(xs), flip(dt[:, :, 1]), a[1], flip(bm), flip(cm), s0_b)
    y_ssd = y_f + flip(y_b) + p['ssd_d'].astype(jnp.float32)[:, None] * xs.astype(jnp.float32)
    y_ssd = y_ssd.reshape(b, n, SSD_INNER) * jax.nn.silu(z.astype(jnp.float32))
    y_ssd = rms_norm(y_ssd, p['ssd_norm_w']).astype(u.dtype)

    y_sc = sc_b * dwconv(sc_c * sc_h, p['sc_conv_w'])

    q = q.reshape(b, n, 2 * DIFF_HEADS, DIFF_HD)
    k = k.reshape(b, n, 2 * DIFF_HEADS, DIFF_HD)
    v = v.reshape(b, n, DIFF_HEADS, DIFF_VD)
    if theta is None:
        keys, vals = k, v
    else:
        q = rope2d(q, theta)
        k = rope2d(k, theta)
        keys = jnp.concatenate([prefix_k.astype(k.dtype), k], axis=1)
        vals = jnp.concatenate([prefix_v.astype(v.dtype), v], axis=1)
    lp = p['diff_lambda'].astype(jnp.float32)
    lam = jnp.exp(jnp.sum(lp[0] * lp[1])) - jnp.exp(jnp.sum(lp[2] * lp[3])) + lam_init
    o = diff_attention(q, keys, vals, lam)
    o = rms_norm(o, p['diff_norm_w']) * (1.0 - lam_init)
    y_attn = o.reshape(b, n, DIFF_WIDTH).astype(u.dtype)

    mixed = jnp.concatenate([y_ssd, y_sc, y_attn], axis=-1) @ p['w_out']
    return mixed, k, v, s_f.astype(u.dtype), s_b.astype(u.dtype)


def trunk_layer(x, mod, p, lam_init, theta, prefix_k, prefix_v, s0_f, s0_b):
    sh1, sc1, g1, sh2, sc2, g2 = jnp.split(mod, 6, axis=-1)
    u = x * (1.0 + sc1) + sh1
    m, k, v, s_f, s_b = mixer(u, p, lam_init, theta, prefix_k, prefix_v, s0_f, s0_b)
    x = layer_norm(ALPHA * x + g1 * m, p['ln1_g'], p['ln1_b'])
    h = x * (1.0 + sc2) + sh2
    f = jnp.square(jax.nn.relu(h @ p['w_up'])) @ p['w_down']
    x = layer_norm(ALPHA * x + g2 * f, p['ln2_g'], p['ln2_b'])
    return x, k, v, s_f, s_b


def setup_inputs(seed: int = 0) -> dict:
    key = jax.random.key(seed)
    ks = jax.random.split(key, 32)
    nrm = lambda k_, shape, s: jax.random.normal(k_, shape, jnp.float32) * s
    u_dt = jax.random.uniform(ks[12], (DEPTH, 2, SSD_HEADS), jnp.float32)
    dt0 = jnp.exp(u_dt * (math.log(0.1) - math.log(0.001)) + math.log(0.001))
    return {
        'x_prompt': nrm(ks[0], (BATCH, SEQ, D_MODEL), 1.0),
        'x_sample': nrm(ks[1], (DEC_BATCH, DEC_SEQ, D_MODEL), 1.0),
        'cache_k': nrm(ks[2], (DEC_BATCH, DEPTH, PAST_LEN, 2 * DIFF_HEADS, DIFF_HD), 1.0),
        'cache_v': nrm(ks[3], (DEC_BATCH, DEPTH, PAST_LEN, DIFF_HEADS, DIFF_VD), 1.0),
        'state_ssm_fwd': nrm(ks[4], (DEC_BATCH, DEPTH, SSD_HEADS, SSD_HEAD_DIM, SSD_STATE), 0.1),
        'state_ssm_bwd': nrm(ks[5], (DEC_BATCH, DEPTH, SSD_HEADS, SSD_HEAD_DIM, SSD_STATE), 0.1),
        'c': nrm(ks[6], (DEC_BATCH, D_MODEL), 1.0),
        'c_ctx': nrm(ks[7], (D_MODEL,), 1.0),
        'w_mod': nrm(ks[8], (DEPTH, D_MODEL, 6 * D_MODEL), D_MODEL ** -0.5),
        'b_mod': nrm(ks[9], (DEPTH, 6 * D_MODEL), 0.01),
        'w_in': nrm(ks[10], (DEPTH, D_MODEL, IN_COLS), D_MODEL ** -0.5),
        'ssd_conv_w': nrm(ks[11], (DEPTH, SSD_CONV, SSD_CONV_CH), SSD_CONV ** -0.5),
        'ssd_conv_b': nrm(ks[13], (DEPTH, SSD_CONV_CH), 0.01),
        'ssd_dt_bias': dt0 + jnp.log(-jnp.expm1(-dt0)),
        'ssd_a_log': jnp.log(jax.random.uniform(ks[14], (DEPTH, 2, SSD_HEADS), jnp.float32, 1.0, 16.0)),
        'ssd_d': 1.0 + nrm(ks[15], (DEPTH, SSD_HEADS), 0.01),
        'ssd_norm_w': 1.0 + nrm(ks[16], (DEPTH, SSD_INNER), 0.01),
        'sc_conv_w': nrm(ks[17], (DEPTH, SC_CONV, SC_WIDTH), SC_CONV ** -0.5),
        'diff_lambda': nrm(ks[18], (DEPTH, 4, DIFF_HD), 0.1),
        'diff_norm_w': 1.0 + nrm(ks[19], (DEPTH, DIFF_VD), 0.01),
        'w_out': nrm(ks[20], (DEPTH, MIX, D_MODEL), BETA * MIX ** -0.5),
        'ln1_g': 1.0 + nrm(ks[21], (DEPTH, D_MODEL), 0.01),
        'ln1_b': nrm(ks[22], (DEPTH, D_MODEL), 0.01),
        'w_up': nrm(ks[23], (DEPTH, D_MODEL, D_FF), D_MODEL ** -0.5),
        'w_down': nrm(ks[24], (DEPTH, D_FF, D_MODEL), BETA * D_FF ** -0.5),
        'ln2_g': 1.0 + nrm(ks[25], (DEPTH, D_MODEL), 0.01),
        'ln2_b': nrm(ks[26], (DEPTH, D_MODEL), 0.01),
    }


def reference(x_prompt, x_sample, cache_k, cache_v, state_ssm_fwd, state_ssm_bwd, c, c_ctx,
              w_mod, b_mod, w_in, ssd_conv_w, ssd_conv_b, ssd_dt_bias, ssd_a_log, ssd_d,
              ssd_norm_w, sc_conv_w, diff_lambda, diff_norm_w, w_out, ln1_g, ln1_b,
              w_up, w_down, ln2_g, ln2_b):
    rows = x_sample.shape[1] // GRID_W
    theta = axial_angles(rows)
    zero_state = jnp.zeros((x_prompt.shape[0], SSD_HEADS, SSD_HEAD_DIM, SSD_STATE), x_prompt.dtype)
    hp, hs = x_prompt, x_sample
    ks_, vs_, sfs_, sbs_ = [], [], [], []
    for l in range(DEPTH):
        p = {
            'w_in': w_in[l], 'ssd_conv_w': ssd_conv_w[l], 'ssd_conv_b': ssd_conv_b[l],
            'ssd_dt_bias': ssd_dt_bias[l], 'ssd_a_log': ssd_a_log[l], 'ssd_d': ssd_d[l],
            'ssd_norm_w': ssd_norm_w[l], 'sc_conv_w': sc_conv_w[l], 'diff_lambda': diff_lambda[l],
            'diff_norm_w': diff_norm_w[l], 'w_out': w_out[l], 'ln1_g': ln1_g[l], 'ln1_b': ln1_b[l],
            'w_up': w_up[l], 'w_down': w_down[l], 'ln2_g': ln2_g[l], 'ln2_b': ln2_b[l],
        }
        lam_init = 0.8 - 0.6 * math.exp(-0.3 * l)
        mod_ctx = (jax.nn.silu(c_ctx) @ w_mod[l] + b_mod[l])[None, None, :]
        hp, k_l, v_l, sf_l, sb_l = trunk_layer(hp, mod_ctx, p, lam_init, None, None, None, zero_state, zero_state)
        ks_.append(k_l)
        vs_.append(v_l)
        sfs_.append(sf_l)
        sbs_.append(sb_l)
        mod_lat = (jax.nn.silu(c) @ w_mod[l] + b_mod[l])[:, None, :]
        hs = trunk_layer(hs, mod_lat, p, lam_init, theta, cache_k[:, l], cache_v[:, l],
                         state_ssm_fwd[:, l], state_ssm_bwd[:, l])[0]
    y_prompt = hp
    y_sample = hs
    new_cache_k = jnp.stack(ks_, axis=1)
    new_cache_v = jnp.stack(vs_, axis=1)
    new_state_ssm_fwd = jnp.stack(sfs_, axis=1)
    new_state_ssm_bwd = jnp.stack(sbs_, axis=1)
    return (y_prompt, y_sample, new_cache_k, new_cache_v, new_state_ssm_fwd, new_state_ssm_bwd)
```

```python
import math
from contextlib import ExitStack

import numpy as np
import concourse.bass as bass
import concourse.mybir as mybir
from concourse.bass_utils import run_bass_kernel_spmd

F32 = mybir.dt.float32
BF16 = mybir.dt.bfloat16
AF = mybir.ActivationFunctionType
ALU = mybir.AluOpType
AX = mybir.AxisListType

ENGS = ["pe", "act", "dve", "pool", "sp"]
NDMASEM = 24

D = 1024
DEPTH = 4
NP_SEQ, P_LEN = 2, 256
S_LEN = 2048
PAST = 256
IN_COLS = 3088
ALPHA = (2 * DEPTH) ** 0.25
LN_EPS = 1e-5
RMS_EPS = 1e-6
GRID_W = 64


def _dsize(dt):
    return 2 if dt == BF16 else 4


def region(ap):
    name = ap.tensor.name
    isps = name.startswith("ps")
    if not (isps or name.startswith("sb")):
        return None
    pat = list(ap.ap)
    sz = _dsize(ap.dtype)
    pstep = pat[0][0]
    pcnt = pat[0][1]
    p0 = ap.offset // pstep
    e0 = ap.offset % pstep
    span = 1
    for st, cnt in pat[1:]:
        span += (cnt - 1) * abs(st)
    b0, b1 = e0 * sz, (e0 + span) * sz
    if isps:
        return (name, 0, 128, (b0 // 2048) * 2048, ((b1 - 1) // 2048 + 1) * 2048)
    return (name, p0, p0 + pcnt, b0, b1)


def _ov(a, b):
    return a[1] < b[2] and b[1] < a[2] and a[3] < b[4] and b[3] < a[4]


def _cov(a, b):
    return a[1] <= b[1] and a[2] >= b[2] and a[3] <= b[3] and a[4] >= b[4]


class Prog:
    def __init__(self):
        self.ops = []
        self.writes = {}
        self.reads = {}
        self.eng_ops = {e: [] for e in ENGS}

    def add(self, eng, fn, reads=(), writes=(), dma=False, extra_deps=(), cost=300.0, lat=0.0):
        oid = len(self.ops)
        deps = set(extra_deps)
        odeps = set()
        rr = list(dict.fromkeys(r for r in (region(a) for a in reads) if r is not None))
        ww = list(dict.fromkeys(r for r in (region(a) for a in writes) if r is not None))
        ww = list(dict.fromkeys(ww + [r for r in rr if r[0].startswith("ps")]))
        for r in rr:
            for (w, wid) in self.writes.get(r[0], ()):
                if _ov(r, w):
                    deps.add(wid)
        for w in ww:
            for (w2, wid) in self.writes.get(w[0], ()):
                if _ov(w, w2):
                    deps.add(wid)
            for (r2, rid) in self.reads.get(w[0], ()):
                if _ov(w, r2):
                    deps.add(rid)
        ops = self.ops
        for w in ww:
            lst = self.writes.setdefault(w[0], [])
            lst[:] = [x for x in lst if not _cov(w, x[0])]
            lst.append((w, oid))
            rl = self.reads.setdefault(w[0], [])
            rl[:] = [x for x in rl if not _cov(w, x[0])]
        for r in rr:
            rl = self.reads.setdefault(r[0], [])
            if not dma:
                keep = []
                for x in rl:
                    if x[0] == r and ops[x[1]][0] == eng and not ops[x[1]][3]:
                        odeps.add(x[1])
                    else:
                        keep.append(x)
                rl[:] = keep
            rl.append((r, oid))
        deps.discard(oid)
        self.ops.append([eng, fn, deps, dma, odeps, cost, lat, _TAG[0]])
        self.eng_ops[eng].append(oid)
        return oid

    def schedule(self, window=48):
        ops = self.ops
        n = len(ops)
        succ = [[] for _ in range(n)]
        npred = [0] * n
        for oid in range(n):
            ds = ops[oid][2] | ops[oid][4]
            npred[oid] = len(ds)
            for d in ds:
                succ[d].append(oid)
        finish = [0.0] * n
        self.start_t = [0.0] * n
        ready_t = [0.0] * n
        done = [False] * n
        pend = {e: list(self.eng_ops[e]) for e in ENGS}
        head = {e: 0 for e in ENGS}
        free_t = {e: 0.0 for e in ENGS}
        new_order = {e: [] for e in ENGS}
        order = []
        best = {e: None for e in ENGS}
        dirty = set(ENGS)
        remaining = n
        while remaining:
            for e in list(dirty):
                lst = pend[e]
                h = head[e]
                while h < len(lst) and done[lst[h]]:
                    h += 1
                head[e] = h
                b = None
                cnt = 0
                i = h
                ft = free_t[e]
                while i < len(lst) and cnt < window:
                    o = lst[i]
                    i += 1
                    if done[o]:
                        continue
                    cnt += 1
                    if npred[o] == 0:
                        st = ready_t[o] if ready_t[o] > ft else ft
                        if b is None or st < b[0]:
                            b = (st, o)
                            if st <= ft:
                                break
                best[e] = b
            dirty.clear()
            pick = None
            for e in ENGS:
                b = best[e]
                if b is not None and (pick is None or b[0] < pick[0]):
                    pick = (b[0], b[1], e)
            assert pick is not None, "scheduler deadlock"
            st, o, e = pick
            eng, fn, deps, dma, odeps, cost, lat = ops[o][:7]
            self.start_t[o] = st
            free_t[e] = st + cost
            finish[o] = st + cost + lat
            done[o] = True
            remaining -= 1
            new_order[e].append(o)
            order.append(o)
            dirty.add(e)
            for s2 in succ[o]:
                npred[s2] -= 1
                if finish[o] > ready_t[s2]:
                    ready_t[s2] = finish[o]
                dirty.add(ops[s2][0])
        self.eng_ops = new_order
        self.sched_order = order
        self.est_time = max(finish) if n else 0.0

    def plan(self):
        ops = self.ops
        n = len(ops)
        order = getattr(self, "sched_order", None) or list(range(n))
        pos = {}
        for e in ENGS:
            for i, oid in enumerate(self.eng_ops[e]):
                pos[oid] = i
        gpos = {o: i for i, o in enumerate(order)}
        clock = {e: {e2: -1 for e2 in ENGS} for e in ENGS}
        snap = {}
        dma_known = {e: set() for e in ENGS}
        waits = {}
        marked = set()
        for oid in order:
            eng, fn, deps, dma = ops[oid][:4]
            w = []
            ck = clock[eng]
            for d in sorted(deps, key=lambda x: gpos[x], reverse=True):
                de, ddma = ops[d][0], ops[d][3]
                if ddma:
                    if d in dma_known[eng]:
                        continue
                    dma_known[eng].add(d)
                    w.append(d)
                    marked.add(d)
                else:
                    if de == "pe" and eng == "pe":
                        continue
                    p = pos[d]
                    if ck[de] >= p:
                        continue
                    w.append(d)
                    marked.add(d)
                    ck[de] = p
                s = snap.get(d)
                if s is not None:
                    for e2 in ENGS:
                        if s[e2] > ck[e2]:
                            ck[e2] = s[e2]
            waits[oid] = w
            snap[oid] = tuple(ck[e2] for e2 in ENGS)
            snap[oid] = dict(zip(ENGS, snap[oid]))
        cnt = {e: 0 for e in ENGS}
        semval = {}
        dma_n = {e: 0 for e in ENGS}
        dma_prev = {}
        dma_hist = {e: [] for e in ENGS}
        for oid in order:
            eng, fn, deps, dma = ops[oid][:4]
            if dma:
                k = dma_n[eng]
                dma_n[eng] += 1
                semval[oid] = (("dma", eng, k % NDMASEM), 16 * (k // NDMASEM + 1))
                if k >= NDMASEM:
                    dma_prev[oid] = dma_hist[eng][k - NDMASEM]
                dma_hist[eng].append(oid)
            elif oid in marked:
                cnt[eng] += 1
                semval[oid] = (("eng", eng), cnt[eng])
        self.sem_counts = cnt
        self._plan = (waits, semval, dma_prev)

    def emit_engine(self, ename, eng, sems):
        waits, semval, dma_prev = self._plan
        ops = self.ops
        for oid in self.eng_ops[ename]:
            e, fn, deps, dma = ops[oid][:4]
            if dma and oid in dma_prev:
                pk, pv = semval[dma_prev[oid]]
                eng.wait_ge(sems[pk], pv)
            for d in waits[oid]:
                k, v = semval[d]
                eng.wait_ge(sems[k], v)
            ins = fn(eng)
            if oid in semval:
                k, v = semval[oid]
                ins.then_inc(sems[k], 16 if dma else 1)


Z0, XBC0, DT0, SCB0, SCC0, SCH0, Q0, K0, V0 = 0, 512, 1536, 1552, 1808, 2064, 2320, 2576, 2832


def _rope_perm():
    idx = np.arange(256).reshape(8, 2, 2, 8)
    return idx[:, :, ::-1, :].reshape(256)


def _win_groups():
    perm = _rope_perm()
    g = []
    g.append(np.arange(XBC0, XBC0 + 512))
    g.append(np.arange(XBC0 + 512, XBC0 + 1024))
    g.append(np.concatenate([Q0 + np.arange(256), Q0 + perm]))
    g.append(np.concatenate([K0 + np.arange(256), K0 + perm]))
    g.append(np.concatenate([V0 + np.arange(256), DT0 + np.arange(16), K0 + np.arange(240)]))
    g.append(np.concatenate([SCC0 + np.arange(256), SCH0 + np.arange(256)]))
    g.append(np.concatenate([SCB0 + np.arange(256), K0 + np.arange(256)]))
    g.append(np.arange(Z0, Z0 + 512))
    return np.concatenate(g)


NG = 8
C_ID, C_UF, C_LF, C_UB, C_LB, C_ONE, C_HM, C_MF, C_MB = 0, 128, 256, 384, 512, 640, 768, 772, 900
NCONST = 1028


def _consts():
    c = np.zeros((128, NCONST), np.float32)
    t = np.arange(128)[:, None]
    i = np.arange(128)[None, :]
    c[:, C_ID:C_ID + 128] = (t == i)
    c[:, C_UF:C_UF + 128] = (t <= i)
    c[:, C_LF:C_LF + 128] = (t > i)
    c[:, C_UB:C_UB + 128] = (t >= i)
    c[:, C_LB:C_LB + 128] = (t < i)
    c[:, C_ONE:C_ONE + 128] = 1.0
    for h in range(4):
        c[32 * h:32 * h + 32, C_HM + h] = 1.0
    c[:, C_MF:C_MF + 128] = (i >= t)
    c[:, C_MB:C_MB + 128] = (i <= t)
    return c


def _rope_tables():
    p = np.arange(128)
    d = p % 32
    axis, half, f = d // 16, (d % 16) // 8, d % 8
    inv = 10000.0 ** (-np.arange(8, dtype=np.float32) / 8)
    tt = np.arange(S_LEN)
    row = (tt // GRID_W).astype(np.float32)
    col = (tt % GRID_W).astype(np.float32)
    posv = np.where(axis[:, None] == 0, row[None, :], col[None, :]).astype(np.float32)
    ang = (posv * inv[f][:, None].astype(np.float32)).astype(np.float32)
    cos = np.cos(ang).astype(np.float32)
    sin = np.sin(ang).astype(np.float32)
    sgn = np.where(half == 0, -1.0, 1.0).astype(np.float32)[:, None]
    return cos, (sin * sgn).astype(np.float32)


PV_BMOD, PV_CW, PV_CB, PV_SCW, PV_L1G, PV_L1B, PV_L2G, PV_L2B, PV_NW = 0, 48, 88, 96, 102, 110, 118, 126, 134
NPV = 138
BR_DTB, BR_ALOG, BR_D, BR_DNW, BR_LAM = 0, 16, 32, 40, 104
NBR = 232


def _fm(v):
    v = np.asarray(v)
    return v.reshape(-1, 128).T


class _Stop(Exception):
    pass


_LIMIT = [10 ** 9]
_SKIP = set()


_PASSOFF = [0]
_TAG = [0]


def chk(n):
    _TAG[0] = n + _PASSOFF[0]
    if n + _PASSOFF[0] > _LIMIT[0]:
        raise _Stop()


def build():
    nc = bass.Bass("TRN2", target_bir_lowering=False)
    P = Prog()
    es = ExitStack()
    dr = lambda name, shape, kind="ExternalInput": nc.dram_tensor(name, list(shape), F32, kind=kind).ap()
    xp_d = dr("xp", [NP_SEQ * P_LEN, D])
    xs_d = dr("xs", [S_LEN, D])
    ck_d = dr("ck", [DEPTH, PAST, 256])
    cv_d = dr("cv", [DEPTH, PAST, 256])
    s0f_d = dr("s0f", [DEPTH, 8, 64, 128])
    s0b_d = dr("s0b", [DEPTH, 8, 64, 128])
    cvec_d = dr("cvec", [128, 16])
    wmod_d = dr("w_mod", [DEPTH, D, 6 * D])
    win_d = dr("w_in_r", [DEPTH, D, NG * 512])
    wout_d = dr("w_out", [DEPTH, D, D])
    wup_d = dr("w_up", [DEPTH, D, 4 * D])
    wdn_d = dr("w_down", [DEPTH, 4 * D, D])
    pvec_d = dr("pvec", [DEPTH, 128, NPV])
    brow_d = dr("brow", [DEPTH, NBR])
    const_d = dr("consts", [128, NCONST])
    cos_d = dr("ropecos", [128, S_LEN])
    sin_d = dr("ropesin", [128, S_LEN])
    yp_d = dr("yp", [NP_SEQ * P_LEN, D], "ExternalOutput")
    ys_d = dr("ys", [S_LEN, D], "ExternalOutput")
    nk_d = dr("nk", [NP_SEQ * DEPTH * P_LEN, 256], "ExternalOutput")
    nv_d = dr("nv", [NP_SEQ * DEPTH * P_LEN, 256], "ExternalOutput")
    nsf_d = dr("nsf", [NP_SEQ * DEPTH * 512, 128], "ExternalOutput")
    nsb_d = dr("nsb", [NP_SEQ * DEPTH * 512, 128], "ExternalOutput")
    out_dmas = []

    sb = lambda name, cols: es.enter_context(nc.sbuf_tensor(name, [128, cols], F32))
    TMAX = S_LEN
    sbX = sb("sbX", 8 * TMAX)
    sbU = sb("sbU", 4 * TMAX)
    sbA = sb("sbA", 4 * TMAX)
    sbB = sb("sbB", 4 * (TMAX + 8))
    sbW = sb("sbW", 4096)
    sbC = sb("sbC", C_MF)
    sbCb = sb("sbCb", NCONST // 2)
    sbPV = sb("sbPV", NPV)
    sbBR = sb("sbBR", NBR)
    sbMOD = sb("sbMOD", DEPTH * 2 * 48)
    sbDER = sb("sbDER", 2 * 48 + 16)
    sbT = sb("sbT", 2560)
    sbT2 = sb("sbT2", 1664)
    sbS = sb("sbS", 2 * 512 + 2 * 256)
    ps = es.enter_context(nc.psum_tensor("ps", [128, 8, 512], F32))

    cst = lambda off, n=128: sbC[:, off:off + n]
    cstb = lambda off, n=128: sbCb[:, :].bitcast(BF16)[:, off:off + n]

    def fsz(ap):
        return float(ap.free_size())

    def ecost(eng, out):
        f = fsz(out)
        if eng == "act":
            return 220.0 + f / 1.4
        if eng == "dve":
            return 100.0 + f / 0.96
        return 300.0 + 2.0 * f

    def mm(out, lhsT, rhs, start=True, stop=True, skip=False, tpos=None):
        passes = 4.0 if lhsT.dtype == F32 else 1.0
        c = 30.0 + fsz(rhs) * passes / 2.0
        if tpos is not None:
            c = c / 2.0 + 10.0
            fn = lambda e: e.matmul(out, lhsT, rhs, start=start, stop=stop, tile_position=tpos)
        elif skip:
            fn = lambda e: e.matmul(out, lhsT, rhs, start=start, stop=stop, skip_group_check=True)
        else:
            fn = lambda e: e.matmul(out, lhsT, rhs, start=start, stop=stop)
        return P.add("pe", fn, reads=[lhsT, rhs], writes=[out], cost=c)

    def tp(out, in_, ident):
        c = 40.0 + 128.0 * (4.0 if in_.dtype == F32 else 1.0) / 2.0
        return P.add("pe", lambda e: e.transpose(out, in_, ident), reads=[in_, ident], writes=[out], cost=c)

    def act(out, in_, func, bias=None, scale=None, accum=None):
        kw = {}
        rd = [in_]
        if bias is not None:
            kw["bias"] = bias
            if not isinstance(bias, float):
                rd.append(bias)
        if scale is not None:
            kw["scale"] = scale
            if not isinstance(scale, float):
                rd.append(scale)
        wr = [out]
        if accum is not None:
            kw["accum_out"] = accum
            wr.append(accum)
        return P.add("act", lambda e: e.activation(out=out, in_=in_, func=func, **kw), reads=rd, writes=wr, cost=ecost("act", out))

    def tt(eng, out, in0, in1, op):
        return P.add(eng, lambda e: e.tensor_tensor(out=out, in0=in0, in1=in1, op=op), reads=[in0, in1], writes=[out], cost=ecost(eng, out))

    def ts(eng, out, in0, s1, s2=None, op0=ALU.mult, op1=None):
        rd = [in0] + [s for s in (s1, s2) if s is not None and not isinstance(s, float)]
        if op1 is None:
            fn = lambda e: e.tensor_scalar(out=out, in0=in0, scalar1=s1, scalar2=None, op0=op0)
        else:
            fn = lambda e: e.tensor_scalar(out=out, in0=in0, scalar1=s1, scalar2=s2, op0=op0, op1=op1)
        return P.add(eng, fn, reads=rd, writes=[out], cost=ecost(eng, out))

    def stt(eng, out, in0, scalar, in1, op0, op1):
        rd = [in0, in1] + ([] if isinstance(scalar, float) else [scalar])
        return P.add(eng, lambda e: e.scalar_tensor_tensor(out=out, in0=in0, scalar=scalar, in1=in1, op0=op0, op1=op1),
                     reads=rd, writes=[out], cost=ecost(eng, out))

    def cp(eng, out, in_):
        if eng == "act":
            return act(out, in_, AF.Copy)
        return P.add(eng, lambda e: e.tensor_copy(out=out, in_=in_), reads=[in_], writes=[out], cost=ecost(eng, out))

    def mset(eng, out, val):
        return P.add(eng, lambda e: e.memset(out, val), writes=[out], cost=150.0)

    def dma(q, out, in_, is_out=False):
        nb_ = fsz(out) * out.partition_size() * (2 if out.dtype == BF16 else 4)
        if q == "pool":
            o = P.add(q, lambda e: e.dma_start(out=out, in_=in_), reads=[in_], writes=[out], dma=True, cost=1000.0, lat=3000.0 + nb_ / 100.0)
        else:
            o = P.add(q, lambda e: e.dma_start(out=out, in_=in_), reads=[in_], writes=[out], dma=True, cost=100.0, lat=2500.0 + nb_ / 150.0)
        if is_out:
            out_dmas.append(o)
        return o

    def recip(out, in_):
        return P.add("dve", lambda e: e.reciprocal(out=out, in_=in_), reads=[in_], writes=[out], cost=ecost("dve", out))

    def rsqrt(buf, eps):
        act(buf, buf, AF.Sqrt, bias=cst_eps[eps], scale=1.0)
        recip(buf, buf)

    bank_rr = {"A": [0, [0, 1, 2, 3]], "B": [0, [4, 5]], "C": [0, [6, 7]], "S": [0, [0, 1, 2, 3, 4, 5, 6, 7]]}
    tdcount = [0]
    pairc = [0]

    def bank(cls):
        st = bank_rr[cls]
        b = st[1][st[0] % len(st[1])]
        st[0] += 1
        return ps[:, b, :]

    def bf(ap):
        return ap.bitcast(BF16)

    dma("sp", sbC[:, :], const_d[:, 0:C_MF])
    dma("pool", sbCb[:, :].bitcast(BF16), const_d)
    identb = cstb(C_ID)
    identf = cst(C_ID)
    sbE = sb("sbE", 4)
    sbQ = sb("sbQ", 32)
    mset("pool", sbE[:, 0:1], LN_EPS / (ALPHA * ALPHA))
    mset("pool", sbE[:, 1:2], RMS_EPS)
    cst_eps = {"ln": sbE[:, 0:1], "rms": sbE[:, 1:2]}

    sbCS = sb("sbCS", 24 + 48)
    cvb = sbCS[:, 0:16]
    dma("sp", cvb, cvec_d)
    csb = bf(sbCS[:, 16:24])
    act(csb, cvb, AF.Silu)
    sbBM = sbCS[:, 24:72]
    MOD = sbMOD[:, :].rearrange("p (l c w) -> p l c w", l=DEPTH, c=2)
    wslot = [bf(sbW[:, 0:2048]).rearrange("p (k n) -> p k n", k=8), bf(sbW[:, 2048:4096]).rearrange("p (k n) -> p k n", k=8)]
    wcnt = [0]

    def wload(src2d, ncols):
        s = wslot[wcnt[0] % 2]
        wcnt[0] += 1
        dma("pool", s[:, :, 0:ncols], src2d.rearrange("(k p) n -> p k n", p=128))
        return s

    mslot = [bf(t_[:, i * 2048:(i + 1) * 2048]).rearrange("p (k n) -> p k n", k=8) for t_ in (sbA, sbB) for i in range(4)]
    mcnt = [0]

    def emit_mod(l, first=False):
        dma("sp", sbBM, pvec_d[l][:, PV_BMOD:PV_BMOD + 48])
        for g in range(12):
            if first:
                wt = mslot[mcnt[0] % 8]
                mcnt[0] += 1
                dma("pool", wt, wmod_d[l][:, g * 512:(g + 1) * 512].rearrange("(k p) n -> p k n", p=128))
            else:
                wt = wload(wmod_d[l][:, g * 512:(g + 1) * 512], 512)
            pb = bank("A")
            for m in range(4):
                for k in range(8):
                    mm(pb[:, m * 2:m * 2 + 2], wt[:, k, m * 128:(m + 1) * 128], csb[:, k * 2:k * 2 + 2],
                       start=(k == 0), stop=(k == 7))
            for m in range(4):
                oc = g * 4 + m
                for c in range(2):
                    ts("dve", MOD[:, l, c, oc:oc + 1], pb[:, m * 2 + c:m * 2 + c + 1], sbBM[:, oc:oc + 1],
                       op0=ALU.add)

    emit_mod(0, first=True)

    def run_pass(seqs, T, x_d, y_d):
        chk(1)
        NT = T // 128
        X = sbX[:, 0:8 * T].rearrange("p (k t) -> p k t", k=8)
        U = bf(sbU[:, 0:4 * T]).rearrange("p (k t) -> p k t", k=8)
        XBC = bf(sbA[:, 0:4 * T]).rearrange("p (k t) -> p k t", k=8)
        TB = T + 4 * len(seqs)
        Bv = bf(sbB[:, 0:4 * TB]).rearrange("p (k t) -> p k t", k=8)
        for si, s in enumerate(seqs):
            s["t0"] = sum(q["n"] for q in seqs[:si])
            s["b0"] = s["t0"] + 4 * si + 2
            s["nk"] = s["n"] + (PAST if s["sample"] else 0)
        blocks = []
        for s in seqs:
            for o in range(0, s["n"], 512):
                blocks.append((s, o, min(512, s["n"] - o)))
        dblocks = []
        for (s_, o_, nb_) in blocks:
            t0_ = s_["t0"] + o_
            if dblocks and dblocks[-1][2] == s_["cond"] and dblocks[-1][0] + dblocks[-1][1] == t0_ and dblocks[-1][1] + nb_ <= 512:
                dblocks[-1] = (dblocks[-1][0], dblocks[-1][1] + nb_, s_["cond"])
            else:
                dblocks.append((t0_, nb_, s_["cond"]))
        NKMAX = max(s["nk"] for s in seqs)
        NKMAX = max(s["nk"] for s in seqs)
        Kf = bf(sbA[:, 0:NKMAX]).rearrange("p (c t) -> p c t", c=2)
        vofs = NKMAX
        Vt = bf(sbA[:, vofs:vofs + (NKMAX // 128) * 132]).rearrange("p (t h e) -> p t h e", h=4, e=66)
        AT0 = 4680
        HID = bf(sbB[:, 0:4096]).rearrange("p (b k t) -> p b k t", b=2, k=8)
        YBW = bf(sbU[:, 0:NT * 256]).rearrange("p (t c) -> p t c", c=512)
        TSETS = [sbT, sbU[:, 4096:6656]]
        EXS = [sbT2[:, 96:120], sbU[:, 6656:6680]]
        SSQS = [sbT2[:, 120:121], sbU[:, 6680:6681]]

        for ti in range(NT):
            xt = sbT[:, 0:1024]
            dma("sp", xt, x_d[ti * 128:(ti + 1) * 128, :])
            for half in range(2):
                pb = bank("A")
                for k in range(4):
                    tp(pb[:, k * 128:(k + 1) * 128], xt[:, (half * 4 + k) * 128:(half * 4 + k + 1) * 128], identf)
                cp("act" if half else "dve", X[:, half * 4:half * 4 + 4, ti * 128:(ti + 1) * 128],
                   pb.rearrange("p (k t) -> p k t", k=4))

        chk(2)

        def _layer(l):
            lam_init = 0.8 - 0.6 * math.exp(-0.3 * l)
            dma("sp", sbPV[:, :], pvec_d[l])
            dma("sp", sbBR[:, :], brow_d[l].partition_broadcast(128))
            pv = lambda off, n=1: sbPV[:, off:off + n]
            DER = sbDER[:, 0:96].rearrange("p (c w) -> p c w", c=2)
            for c in range(2):
                ts("pool", DER[:, c, 0:8], MOD[:, l, c, 8:16], 1.0, op0=ALU.add)
                ts("pool", DER[:, c, 8:16], MOD[:, l, c, 16:24], 1.0 / ALPHA, op0=ALU.mult)
                ts("pool", DER[:, c, 40:48], MOD[:, l, c, 32:40], 1.0, op0=ALU.add)
                tt("pool", DER[:, c, 16:24], DER[:, c, 40:48], pv(PV_L1G, 8), ALU.mult)
                tt("pool", DER[:, c, 24:32], DER[:, c, 40:48], pv(PV_L1B, 8), ALU.mult)
                tt("pool", DER[:, c, 24:32], DER[:, c, 24:32], MOD[:, l, c, 24:32], ALU.add)
                ts("pool", DER[:, c, 32:40], MOD[:, l, c, 40:48], 1.0 / ALPHA, op0=ALU.mult)
            LT = sbDER[:, 96:112]
            lp = sbBR[:, BR_LAM:BR_LAM + 128].rearrange("p (a d) -> p a d", a=4)
            prod = sbT2[:, 0:64].rearrange("p (a d) -> p a d", a=2)
            tt("dve", prod[:, 0, :], lp[:, 0, :], lp[:, 1, :], ALU.mult)
            tt("dve", prod[:, 1, :], lp[:, 2, :], lp[:, 3, :], ALU.mult)
            P.add("dve", lambda e: e.reduce_sum(out=LT[:, 0:2], in_=prod, axis=AX.X), reads=[prod], writes=[LT[:, 0:2]])
            act(LT[:, 2:4], LT[:, 0:2], AF.Exp)
            tt("dve", LT[:, 4:5], LT[:, 3:4], LT[:, 2:3], ALU.subtract)
            ts("dve", LT[:, 5:6], LT[:, 4:5], -lam_init, op0=ALU.add)
            NLAM = LT[:, 5:6]
            Abc = sbDER[:, 112:112] if False else None
            ABR = sbT2[:, 64:80]
            act(ABR, sbBR[:, BR_ALOG:BR_ALOG + 16], AF.Exp)
            ts("dve", ABR, ABR, -1.0, op0=ALU.mult)

            for (t0, nb, c) in dblocks:
                for k in range(8):
                    if k % 2:
                        ts("dve", U[:, k, t0:t0 + nb], X[:, k, t0:t0 + nb], DER[:, c, k:k + 1], MOD[:, l, c, k:k + 1], op0=ALU.mult, op1=ALU.add)
                    else:
                        act(U[:, k, t0:t0 + nb], X[:, k, t0:t0 + nb], AF.Identity, bias=MOD[:, l, c, k:k + 1], scale=DER[:, c, k:k + 1])

            chk(3)
            wg = lambda g: wload(win_d[l][:, g * 512:(g + 1) * 512], 512)

            def proj_fm(wt, ncol0, nch, evac):
                for (s, o, nb) in blocks:
                    t0 = s["t0"] + o
                    for m in range(nch):
                        pb = bank("A")
                        for k in range(8):
                            mm(pb[:, 0:nb], wt[:, k, ncol0 + m * 128:ncol0 + (m + 1) * 128], U[:, k, t0:t0 + nb],
                               start=(k == 0), stop=(k == 7))
                        evac(s, o, nb, m, pb[:, 0:nb])

            def proj_tm(wt, ncol0, ncols, evac):
                for s in seqs:
                    for ti in range(s["n"] // 128):
                        t0 = s["t0"] + ti * 128
                        pb = bank("A")
                        for k in range(8):
                            mm(pb[:, 0:ncols], U[:, k, t0:t0 + 128], wt[:, k, ncol0:ncol0 + ncols], start=(k == 0), stop=(k == 7))
                        evac(s, ti, pb[:, 0:ncols])

            DTR = sbT2[:, 1152:1152 + NT * 16].rearrange("p (t c) -> p t c", c=16)
            for s in seqs:
                wt4 = wg(4)
                koff = PAST // 128 if s["sample"] else 0
                if s["sample"]:
                    for kt in range(2):
                        kst = bf(sbT[:, 2048:2176])
                        dma("pool", kst, ck_d[l, kt * 128:(kt + 1) * 128, :])
                        pb = bank("C")
                        pbb = pb.bitcast(BF16)
                        for c2 in range(2):
                            tp(pbb[:, c2 * 128:(c2 + 1) * 128], kst[:, c2 * 128:(c2 + 1) * 128], identb)
                        cp("act", Kf[:, :, kt * 128:(kt + 1) * 128], pbb[:, 0:256].rearrange("p (c t) -> p c t", c=2))
                        dma("pool", Vt[:, kt, :, 0:64], cv_d[l, kt * 128:(kt + 1) * 128, :].rearrange("t (h e) -> t h e", h=4))
                for ti in range(s["nk"] // 128):
                    if "A" not in _SKIP:
                        mset("pool", Vt[:, ti, :, 64:66], 1.0)
                for ti in range(s["n"] // 128):
                    t0 = s["t0"] + ti * 128
                    pb = bank("A")
                    for k in range(8):
                        mm(pb[:, 0:272], U[:, k, t0:t0 + 128], wt4[:, k, 0:272], start=(k == 0), stop=(k == 7))
                    if "B" not in _SKIP:
                        cp("act", Vt[:, koff + ti, :, 0:64], pb[:, 0:256].rearrange("p (h e) -> p h e", h=4))
                    if "C" not in _SKIP:
                        tt("dve", DTR[:, s["t0"] // 128 + ti, :], pb[:, 256:272], sbBR[:, BR_DTB:BR_DTB + 16], ALU.add)
                    if not s["sample"] and "D" not in _SKIP:
                        st = sbT[:, 1024:1280]
                        cp("dve", st, pb[:, 0:256])
                        r0 = (s["idx"] * DEPTH + l) * P_LEN + ti * 128
                        dma("sp", nv_d[r0:r0 + 128, :], st, is_out=True)
                chk(3.1)
                wt3 = wg(3)
                koffc = PAST if s["sample"] else 0
                for o in range(0, s["n"], 512):
                    nb = min(512, s["n"] - o)
                    t0 = s["t0"] + o
                    if s["sample"]:
                        cs = sbT[:, 1024:1536]
                        sn = sbT[:, 1536:2048]
                        dma("sp", cs[:, 0:nb], cos_d[:, o:o + nb])
                        dma("sp", sn[:, 0:nb], sin_d[:, o:o + nb])
                    for m in range(2):
                        pk = bank("A")
                        for k in range(8):
                            mm(pk[:, 0:nb], wt3[:, k, m * 128:(m + 1) * 128], U[:, k, t0:t0 + nb], start=(k == 0), stop=(k == 7))
                        if s["sample"]:
                            pkp = bank("A")
                            for k in range(8):
                                mm(pkp[:, 0:nb], wt3[:, k, 256 + m * 128:256 + (m + 1) * 128], U[:, k, t0:t0 + nb], start=(k == 0), stop=(k == 7))
                            t1 = sbT[:, 0:nb]
                            t2 = sbT[:, 512:512 + nb]
                            tt("dve", t1, pk[:, 0:nb], cs[:, 0:nb], ALU.mult)
                            tt("dve", t2, pkp[:, 0:nb], sn[:, 0:nb], ALU.mult)
                            tt("pool", Kf[:, m, koffc + o:koffc + o + nb], t1, t2, ALU.add)
                        else:
                            cp("act", Kf[:, m, koffc + o:koffc + o + nb], pk[:, 0:nb])
                chk(3.2)
                wt2 = wg(2)
                nkt = s["nk"] // 128
                for o in range(0, s["n"], 512):
                    nb = min(512, s["n"] - o)
                    nq = nb // 128
                    t0 = s["t0"] + o
                    QR = bf(sbA[:, AT0:AT0 + 512]).rearrange("p (c t) -> p c t", c=2)
                    if s["sample"]:
                        cs = sbT[:, 1024:1536]
                        sn = sbT[:, 1536:2048]
                        dma("sp", cs[:, 0:nb], cos_d[:, o:o + nb])
                        dma("sp", sn[:, 0:nb], sin_d[:, o:o + nb])
                    for m in range(2):
                        pq = bank("A")
                        for k in range(8):
                            mm(pq[:, 0:nb], wt2[:, k, m * 128:(m + 1) * 128], U[:, k, t0:t0 + nb], start=(k == 0), stop=(k == 7))
                        if s["sample"]:
                            pqp = bank("A")
                            for k in range(8):
                                mm(pqp[:, 0:nb], wt2[:, k, 256 + m * 128:256 + (m + 1) * 128], U[:, k, t0:t0 + nb], start=(k == 0), stop=(k == 7))
                            t1 = sbT[:, 0:nb]
                            t2 = sbT[:, 512:512 + nb]
                            tt("dve", t1, pq[:, 0:nb], cs[:, 0:nb], ALU.mult)
                            tt("dve", t2, pqp[:, 0:nb], sn[:, 0:nb], ALU.mult)
                            tt("pool", QR[:, m, 0:nb], t1, t2, ALU.add)
                        else:
                            cp("act", QR[:, m, 0:nb], pq[:, 0:nb])
                    chk(3.3)
                    YT = bf(sbA[:, AT0 + 512:AT0 + 1024]).rearrange("p (q c) -> p q c", q=4)
                    for i in range(4):
                        OS = sbA[:, AT0 + 2560:AT0 + 2560 + 520].rearrange("p (q s e) -> p q s e", q=4, s=2)
                        cch = i // 2
                        accs = [ps[:, 4 + 2 * (i % 2) + sm_, :] for sm_ in range(2)]
                        for kt in range(nkt):
                            b0 = 2 * (pairc[0] % 2)
                            pairc[0] += 1
                            pst2 = ps[:, b0:b0 + 2, :]
                            for sm in range(2):
                                hrow = (2 * i + sm) % 4
                                mm(pst2[:, sm, 0:nb], Kf[32 * hrow:32 * hrow + 32, cch, kt * 128:(kt + 1) * 128],
                                   QR[32 * hrow:32 * hrow + 32, cch, 0:nb], tpos=(32 * hrow, 0))
                            pofs = AT0 + 1536 + (pairc[0] % 2) * 512
                            PT2 = bf(sbA[:, pofs:pofs + 512]).rearrange("p (u t) -> p u t", u=2)
                            act(PT2[:, :, 0:nb], pst2[:, :, 0:nb], AF.Exp, scale=float(32 ** -0.5))
                            for sm in range(2):
                                for qt in range(nq):
                                    mm(accs[sm][:, qt * 65:(qt + 1) * 65], PT2[:, sm, qt * 128:(qt + 1) * 128], Vt[:, kt, i, 0:65],
                                       start=(kt == 0 and qt == 0), stop=(kt == nkt - 1), skip=True)
                        for sm in range(2):
                            cp("act" if sm else "dve", OS[:, 0:nq, sm, :], accs[sm][:, 0:nq * 65].rearrange("p (q e) -> p q e", q=nq))
                        chk(3.4)
                        RD = sbT2[:, 80:88].rearrange("p (q s) -> p q s", q=4)
                        recip(RD[:, 0:nq, :], OS[:, 0:nq, :, 64])
                        ON = sbT[:, 0:512].rearrange("p (q s e) -> p q s e", q=4, s=2)
                        tt("dve", ON[:, 0:nq], OS[:, 0:nq, :, 0:64], RD[:, 0:nq, :].unsqueeze(3).to_broadcast([128, nq, 2, 64]), ALU.mult)
                        DF = sbT[:, 512:768].rearrange("p (q e) -> p q e", q=4)
                        stt("dve", DF[:, 0:nq], ON[:, 0:nq, 1, :], NLAM, ON[:, 0:nq, 0, :], ALU.mult, ALU.add)
                        SQ = sbT[:, 768:1024].rearrange("p (q e) -> p q e", q=4)
                        tt("pool", SQ[:, 0:nq], DF[:, 0:nq], DF[:, 0:nq], ALU.mult)
                        SSA = sbQ[:, 16:32].rearrange("p (q h) -> p q h", h=4)
                        SS = SSA[:, 0:nq, i]
                        P.add("dve", lambda e, SS=SS, SQ=SQ, nq=nq: e.reduce_sum(out=SS, in_=SQ[:, 0:nq], axis=AX.X),
                              reads=[SQ[:, 0:nq]], writes=[SS])
                        stt("dve", YT[:, 0:nq, i * 64:(i + 1) * 64], DF[:, 0:nq], float(1.0 - lam_init),
                            sbBR[:, BR_DNW:BR_DNW + 64].unsqueeze(1).to_broadcast([128, nq, 64]), ALU.mult, ALU.mult)
                    chk(3.5)
                    SSA = sbQ[:, 16:32].rearrange("p (q h) -> p q h", h=4)
                    ts("dve", SSA[:, 0:nq, :], SSA[:, 0:nq, :], 1.0 / 64, op0=ALU.mult)
                    rsqrt(SSA[:, 0:nq, :], "rms")
                    YT4 = YT.rearrange("p q (h e) -> p q h e", h=4)
                    tt("dve", YT4[:, 0:nq], YT4[:, 0:nq], SSA[:, 0:nq, :].unsqueeze(3).to_broadcast([128, nq, 4, 64]), ALU.mult)
                    for qt in range(nq):
                        pb = bank("C")
                        pbb = pb.bitcast(BF16)
                        for c2 in range(2):
                            tp(pbb[:, c2 * 128:(c2 + 1) * 128], YT[:, qt, c2 * 128:(c2 + 1) * 128], identb)
                        cc = s["b0"] + o + qt * 128
                        cp("act", Bv[:, 6:8, cc:cc + 128], pbb[:, 0:256].rearrange("p (c t) -> p c t", c=2))

            chk(4)
            for s in seqs:
                mset("pool", Bv[:, 0:4, s["b0"] - 2:s["b0"]], 0.0)
                mset("pool", Bv[:, 0:4, s["b0"] + s["n"]:s["b0"] + s["n"] + 2], 0.0)

            DG = bf(sbT2[:, 128:128 + 640]).rearrange("p (r k c) -> p r k c", r=2, k=5)
            for half in range(2):
                wt = wg(half)
                proj_fm(wt, 0, 4, lambda s, o, nb, m, pb: cp("act" if m % 2 else "dve",
                        Bv[:, m, s["b0"] + o:s["b0"] + o + nb], pb))
                for m in range(4):
                    ch = half * 4 + m
                    dg = DG[:, ch % 2]
                    for k in range(5):
                        ts("dve", dg[:, k, :], identb, pv(PV_CW + k * 8 + ch), op0=ALU.mult)
                    for (s, o, nb) in blocks:
                        pb = bank("A")
                        for k in range(5):
                            c0 = s["b0"] + o + k - 2
                            mm(pb[:, 0:nb], dg[:, k, :], Bv[:, m, c0:c0 + nb], start=(k == 0), stop=(k == 4))
                        act(XBC[:, ch, s["t0"] + o:s["t0"] + o + nb], pb[:, 0:nb], AF.Silu, bias=pv(PV_CB + ch))

            chk(5)
            for s in seqs:
                mset("pool", Bv[:, 0:2, s["b0"] - 2:s["b0"]], 0.0)
                mset("pool", Bv[:, 0:2, s["b0"] + s["n"]:s["b0"] + s["n"] + 2], 0.0)
            wt = wg(5)
            for (s, o, nb) in blocks:
                t0 = s["t0"] + o
                for m in range(2):
                    pc = bank("A")
                    ph = bank("A")
                    for k in range(8):
                        mm(pc[:, 0:nb], wt[:, k, m * 128:(m + 1) * 128], U[:, k, t0:t0 + nb], start=(k == 0), stop=(k == 7))
                    for k in range(8):
                        mm(ph[:, 0:nb], wt[:, k, 256 + m * 128:256 + (m + 1) * 128], U[:, k, t0:t0 + nb], start=(k == 0), stop=(k == 7))
                    tmp = sbT[:, 0:nb]
                    cp("act", tmp, ph[:, 0:nb])
                    tt("dve", Bv[:, m, s["b0"] + o:s["b0"] + o + nb], pc[:, 0:nb], tmp, ALU.mult)
            wt6 = wg(6)
            DG3 = bf(sbT2[:, 768:768 + 384]).rearrange("p (m k c) -> p m k c", m=2, k=3)
            for m in range(2):
                for k in range(3):
                    ts("dve", DG3[:, m, k, :], identb, pv(PV_SCW + k * 2 + m), op0=ALU.mult)
            for (s, o, nb) in blocks:
                t0 = s["t0"] + o
                for m in range(2):
                    pbb = bank("A")
                    pcc = bank("A")
                    for k in range(8):
                        mm(pbb[:, 0:nb], wt6[:, k, m * 128:(m + 1) * 128], U[:, k, t0:t0 + nb], start=(k == 0), stop=(k == 7))
                    for k in range(3):
                        c0 = s["b0"] + o + k - 1
                        mm(pcc[:, 0:nb], DG3[:, m, k, :], Bv[:, m, c0:c0 + nb], start=(k == 0), stop=(k == 2))
                    tmp = sbT[:, 512:512 + nb]
                    cp("act", tmp, pcc[:, 0:nb])
                    tt("dve", Bv[:, 4 + m, s["b0"] + o:s["b0"] + o + nb], pbb[:, 0:nb], tmp, ALU.mult)
            for s in seqs:
                if s["sample"]:
                    continue
                for ti in range(s["n"] // 128):
                    t0 = s["t0"] + ti * 128
                    pb = bank("A")
                    for k in range(8):
                        mm(pb[:, 0:256], U[:, k, t0:t0 + 128], wt6[:, k, 256:512], start=(k == 0), stop=(k == 7))
                    st = sbT[:, 1024:1280]
                    cp("act", st, pb[:, 0:256])
                    r0 = (s["idx"] * DEPTH + l) * P_LEN + ti * 128
                    dma("sp", nk_d[r0:r0 + 128, :], st, is_out=True)

            chk(6)
            wt7 = wg(7)

            def zs_view(s, ti):
                cc = s["b0"] + ti * 128
                return Bv[:, 0:4, cc:cc + 128]

            proj_tm(wt7, 0, 512, lambda s, ti, pb: act(zs_view(s, ti), pb.rearrange("p (c t) -> p c t", c=4), AF.Silu))

            DTf = sbT2[:, 1152:1152 + NT * 16]
            act(DTf, DTf, AF.Exp)
            ts("dve", DTf, DTf, 1.0, op0=ALU.add)
            act(DTf, DTf, AF.Ln)
            DAt = sbDER[:, 0:0] if False else None
            DA = sbS[:, 0:0] if False else None
            DAv = sbT2[:, 1408:1408 + NT * 16].rearrange("p (t c) -> p t c", c=16)
            tt("dve", DAv, DTR, ABR.unsqueeze(1).to_broadcast([128, NT, 16]), ALU.mult)

            chk(7)
            if seqs[0]["sample"] and l + 1 < DEPTH:
                _t = _TAG[0]
                _TAG[0] = 0
                emit_mod(l + 1)
                _TAG[0] = _t
            Sst = [sbS[:, 0:512], sbS[:, 512:1024]]
            Sbf = [bf(sbS[:, 1024:1280]), bf(sbS[:, 1280:1536])]
            for s in seqs:
                ntile = s["n"] // 128
                for d_ in range(2):
                    if s["sample"]:
                        src = (s0f_d if d_ == 0 else s0b_d)[l]
                        stg = sbT[:, 0:512].rearrange("p (a n) -> p a n", a=4)
                        dma("sp", stg, src.rearrange("(a h2) p n -> (h2 p) a n", h2=2))
                        for a in range(4):
                            pb = bank("C")
                            tp(pb[:, 0:128], stg[:, a, :], identf)
                            cp("act", Sst[d_][:, a * 128:(a + 1) * 128], pb[:, 0:128])
                        cp("dve", Sbf[d_], Sst[d_])
                    else:
                        mset("pool", Sst[d_], 0.0)
                        mset("pool", Sbf[d_], 0.0)
                order = []
                for k_ in range(ntile):
                    order.append((1, ntile - 1 - k_, k_ >= ntile // 2))
                    order.append((0, k_, k_ >= ntile // 2))
                for (d_, ti, comb) in order:
                    par = tdcount[0] % 2
                    tdcount[0] += 1
                    Tq = TSETS[par]
                    gti = s["t0"] // 128 + ti
                    tc0 = s["t0"] + ti * 128
                    da = DAv[:, gti, d_ * 8:(d_ + 1) * 8]
                    dt = DTR[:, gti, d_ * 8:(d_ + 1) * 8]
                    U_, L_, M_ = (C_UF, C_LF, C_MF) if d_ == 0 else (C_UB, C_LB, C_MB)
                    pe_ = bank("S")
                    mm(pe_[:, 0:8], cst(U_), da)
                    mm(pe_[:, 8:16], cst(L_), da)
                    mm(pe_[:, 16:24], cst(C_ONE), da)
                    EX = EXS[par]
                    act(EX, pe_[:, 0:24], AF.Exp)
                    pxs = bank("S")
                    pxsb = pxs.bitcast(BF16)
                    for c4 in range(4):
                        tp(pxsb[:, c4 * 128:(c4 + 1) * 128], XBC[:, c4, tc0:tc0 + 128], identb)
                    xsv = pxsb[:, 0:512].rearrange("p (h e) -> p h e", h=8)
                    XDT = bf(Tq[:, 0:256]).rearrange("p (h e) -> p h e", h=8)
                    XDE = bf(Tq[:, 256:512]).rearrange("p (h e) -> p h e", h=8)
                    tt("dve", XDT, xsv, dt.unsqueeze(2).to_broadcast([128, 8, 64]), ALU.mult)
                    tt("dve", XDE, XDT, EX[:, 8:16].unsqueeze(2).to_broadcast([128, 8, 64]), ALU.mult)
                    if comb:
                        XSD = Tq[:, 512:1024].rearrange("p (h e) -> p h e", h=8)
                        tt("dve", XSD, xsv, sbBR[:, BR_D:BR_D + 8].unsqueeze(2).to_broadcast([128, 8, 64]), ALU.mult)
                    pbt = bank("S")
                    pbtb = pbt.bitcast(BF16)
                    for g in range(2):
                        tp(pbtb[:, g * 128:(g + 1) * 128], XBC[:, 4 + g, tc0:tc0 + 128], identb)
                    BT = bf(Tq[:, 1024:1152])
                    cp("act", BT, pbtb[:, 0:256])
                    pcb = bank("S")
                    for g in range(2):
                        mm(pcb[:, g * 128:(g + 1) * 128], XBC[:, 4 + g, tc0:tc0 + 128], XBC[:, 6 + g, tc0:tc0 + 128])
                    CBM = bf(Tq[:, 1152:1280]).rearrange("p (g i) -> p g i", g=2)
                    tt("dve", CBM, pcb[:, 0:256].rearrange("p (g i) -> p g i", g=2),
                       cstb(M_).unsqueeze(1).to_broadcast([128, 2, 128]), ALU.mult)
                    MT = bf(Tq[:, 1536:2048]).rearrange("p (h i) -> p h i", h=8)
                    for g in range(2):
                        LD = bf(Tq[:, 2048 + g * 256:2048 + (g + 1) * 256]).rearrange("p (h j) -> p h j", h=4)
                        for h4 in range(4):
                            act(LD[:, h4, :], cstb(L_), AF.Copy, scale=da[:, g * 4 + h4:g * 4 + h4 + 1])
                        psg = bank("S")
                        for h4 in range(4):
                            mm(psg[:, h4 * 128:(h4 + 1) * 128], LD[:, h4, :], cstb(U_))
                        EE = bf(Tq[:, 1280:1536]).rearrange("p (h i) -> p h i", h=4)
                        act(EE, psg.rearrange("p (h i) -> p h i", h=4), AF.Exp)
                        tt("dve" if g == 0 else "pool", MT[:, g * 4:(g + 1) * 4, :], EE, CBM[:, g, :].unsqueeze(1).to_broadcast([128, 4, 128]), ALU.mult)
                    if comb:
                        XSDY = bf(Tq[:, 1280:1536])
                        tt("pool", XSDY, Tq[:, 512:1024], YBW[:, gti, :], ALU.add)
                    pyd = bank("S")
                    for h in range(8):
                        mm(pyd[:, h * 64:(h + 1) * 64], MT[:, h, :], XDT[:, h, :], start=(h == 0), stop=(not comb), skip=True)
                    if comb:
                        mm(pyd[:, 0:512], identb, XSDY, start=False, stop=True, skip=True)
                    pyo = bank("S")
                    for g in range(2):
                        mm(pyo[:, g * 256:(g + 1) * 256], XBC[:, 6 + g, tc0:tc0 + 128], Sbf[d_][:, g * 256:(g + 1) * 256])
                    YTMP = Tq[:, 2048:2560].rearrange("p (h e) -> p h e", h=8)
                    tt("dve", YTMP, pyo.rearrange("p (h e) -> p h e", h=8), EX[:, 0:8].unsqueeze(2).to_broadcast([128, 8, 64]), ALU.mult)
                    pst_ = bank("S")
                    for g in range(2):
                        mm(pst_[:, g * 256:(g + 1) * 256], BT[:, g * 128:(g + 1) * 128], bf(Tq[:, 256:512])[:, g * 256:(g + 1) * 256])
                    Sv = Sst[d_].rearrange("p (h e) -> p h e", h=8)
                    tt("pool", Sv, Sv, EX[:, 16:24].unsqueeze(2).to_broadcast([128, 8, 64]), ALU.mult)
                    tt("dve", Sst[d_], Sst[d_], pst_, ALU.add)
                    cp("act", Sbf[d_], Sst[d_])
                    ycol = YBW[:, gti, :]
                    if not comb:
                        tt("dve", ycol, pyd, YTMP.rearrange("p h e -> p (h e)"), ALU.add)
                    else:
                        YS = Tq[:, 2048:2560]
                        tt("dve", YS, pyd, YTMP.rearrange("p h e -> p (h e)"), ALU.add)
                        tt("dve", YS.rearrange("p (c t) -> p c t", c=4), YS.rearrange("p (c t) -> p c t", c=4), zs_view(s, ti), ALU.mult)
                        SSQ = sbQ[:, gti:gti + 1]
                        JK = Tq[:, 1536:2048]
                        mset("pool", SSQ, 0.0)
                        act(JK, YS, AF.Square, accum=SSQ)
                        cp("pool", zs_view(s, ti), YS.rearrange("p (c t) -> p c t", c=4))
                g0 = s["t0"] // 128
                RQ = sbQ[:, g0:g0 + ntile]
                ts("dve", RQ, RQ, 1.0 / 512, op0=ALU.mult)
                rsqrt(RQ, "rms")
                for ti in range(ntile):
                    YN = bf(sbT[:, (ti % 2) * 256:(ti % 2) * 256 + 256]).rearrange("p (c t) -> p c t", c=4)
                    ts("dve", YN, zs_view(s, ti), sbQ[:, g0 + ti:g0 + ti + 1], op0=ALU.mult)
                    pyt = bank("S")
                    pytb = pyt.bitcast(BF16)
                    for c4 in range(4):
                        tp(pytb[:, c4 * 128:(c4 + 1) * 128], YN[:, c4, :], identb)
                    for c4 in range(4):
                        act(zs_view(s, ti)[:, c4, :], pytb[:, c4 * 128:(c4 + 1) * 128], AF.Identity, scale=pv(PV_NW + c4))
                if not s["sample"]:
                    for d_ in range(2):
                        dst = (nsf_d if d_ == 0 else nsb_d)
                        r0 = (s["idx"] * DEPTH + l) * 512
                        for a in range(4):
                            pb = bank("C")
                            tp(pb[:, 0:128], Sst[d_][:, a * 128:(a + 1) * 128], identf)
                            stg = sbT[:, 0:128]
                            cp("act", stg, pb[:, 0:128])
                            dma("sp", dst[r0 + a * 128:r0 + (a + 1) * 128, :], stg, is_out=True)

            chk(8)
            WO = bf(sbU[:, 0:4096]).rearrange("p (k n) -> p k n", k=8) if False else None
            for half in range(2):
                wt = wload(wout_d[l][:, half * 512:(half + 1) * 512], 512)
                for (s, o, nb) in blocks:
                    t0 = s["t0"] + o
                    c = s["cond"]
                    for m in range(4):
                        mo = half * 4 + m
                        pb = bank("A")
                        for ki, k in enumerate((4, 5, 6, 7, 0, 1, 2, 3)):
                            mm(pb[:, 0:nb], wt[:, k, m * 128:(m + 1) * 128], Bv[:, k, s["b0"] + o:s["b0"] + o + nb],
                               start=(ki == 0), stop=(ki == 7))
                        stt("dve", X[:, mo, t0:t0 + nb], pb[:, 0:nb], DER[:, c, 8 + mo:9 + mo], X[:, mo, t0:t0 + nb], ALU.mult, ALU.add)

            def layer_norm(goff, boff, gh, bh, want_h):
                for (t0, nb, c) in dblocks:
                    VB = bf(sbB[:, 4096:4096 + 2048]).rearrange("p (k t) -> p k t", k=8)
                    pm = bank("C")
                    for k in range(8):
                        mm(pm[:, 0:nb], cst(C_ONE), X[:, k, t0:t0 + nb], start=(k == 0), stop=(k == 7))
                    for k in range(8):
                        act(VB[:, k, 0:nb], X[:, k, t0:t0 + nb], AF.Square)
                    pq = bank("C")
                    for k in range(8):
                        mm(pq[:, 0:nb], cstb(C_ONE), VB[:, k, 0:nb], start=(k == 0), stop=(k == 7))
                    MEAN = sbT[:, 0:nb]
                    RSTD = sbT[:, 512:512 + nb]
                    NMR = sbT[:, 1024:1024 + nb]
                    act(MEAN, pm[:, 0:nb], AF.Identity, scale=1.0 / D)
                    tt("pool", NMR, MEAN, MEAN, ALU.mult)
                    stt("dve", RSTD, pq[:, 0:nb], 1.0 / D, NMR, ALU.mult, ALU.subtract)
                    rsqrt(RSTD, "ln")
                    stt("dve", NMR, MEAN, -1.0, RSTD, ALU.mult, ALU.mult)
                    for k in range(8):
                        TT = sbT[:, 1536 + (k % 2) * 512:1536 + (k % 2) * 512 + nb]
                        tt("dve", TT, X[:, k, t0:t0 + nb], RSTD, ALU.mult)
                        tt("pool" if k % 8 in (0, 2, 3, 5, 6) else "dve", TT, TT, NMR, ALU.add)
                        if want_h or k % 4 != 3:
                            act(X[:, k, t0:t0 + nb], TT, AF.Identity, bias=pv(boff + k), scale=pv(goff + k))
                        else:
                            ts("dve", X[:, k, t0:t0 + nb], TT, pv(goff + k), pv(boff + k), op0=ALU.mult, op1=ALU.add)
                        if want_h:
                            if k % 4 == 0:
                                act(U[:, k, t0:t0 + nb], TT, AF.Identity, bias=DER[:, c, bh + k:bh + k + 1], scale=DER[:, c, gh + k:gh + k + 1])
                            else:
                                ts("dve", U[:, k, t0:t0 + nb], TT, DER[:, c, gh + k:gh + k + 1], DER[:, c, bh + k:bh + k + 1], op0=ALU.mult, op1=ALU.add)

            chk(9)
            layer_norm(PV_L1G, PV_L1B, 16, 24, True)
            chk(10)

            for q in range(8):
                wu = wload(wup_d[l][:, q * 512:(q + 1) * 512], 512)
                wd = wload(wdn_d[l][q * 512:(q + 1) * 512, :], 1024) if False else None
                WD = bf(sbA[:, (q % 2) * 2048:(q % 2) * 2048 + 2048]).rearrange("p (k n) -> p k n", k=4)
                dma("pool", WD, wdn_d[l][q * 512:(q + 1) * 512, :].rearrange("(k p) n -> p k n", p=128))
                for bi, (t0, nb, c) in enumerate(dblocks):
                    Hd = HID[:, bi % 2]
                    for hc in range(4):
                        pb = bank("A")
                        for k in range(8):
                            mm(pb[:, 0:nb], wu[:, k, hc * 128:(hc + 1) * 128], U[:, k, t0:t0 + nb], start=(k == 0), stop=(k == 7))
                        RL = sbT[:, (hc % 2) * 512:(hc % 2) * 512 + nb]
                        act(RL, pb[:, 0:nb], AF.Relu)
                        tt("pool", Hd[:, hc, 0:nb], RL, RL, ALU.mult)
                    for mo in range(8):
                        pb = bank("A")
                        for hc in range(4):
                            mm(pb[:, 0:nb], WD[:, hc, mo * 128:(mo + 1) * 128], Hd[:, hc, 0:nb], start=(hc == 0), stop=(hc == 3))
                        stt("dve", X[:, mo, t0:t0 + nb], pb[:, 0:nb], DER[:, c, 32 + mo:33 + mo], X[:, mo, t0:t0 + nb], ALU.mult, ALU.add)

            chk(11)
            layer_norm(PV_L2G, PV_L2B, 0, 0, False)
            chk(12)

        stopped = False
        try:
            for l in range(DEPTH):
                _layer(l)
        except _Stop:
            stopped = True
        for ti in range(NT):
            yt = sbT[:, 0:1024]
            for half in range(2):
                pb = bank("A")
                for k in range(4):
                    tp(pb[:, k * 128:(k + 1) * 128], X[:, half * 4 + k, ti * 128:(ti + 1) * 128], identf)
                cp("act" if half else "dve", yt[:, half * 512:(half + 1) * 512], pb)
            dma("sp", y_d[ti * 128:(ti + 1) * 128, :], yt, is_out=True)
        if stopped:
            raise _Stop()

    try:
        sseqs = [dict(n=S_LEN, cond=1, sample=True, idx=0)]
        _PASSOFF[0] = 100
        run_pass(sseqs, S_LEN, xs_d, ys_d)
        pseqs = [dict(n=P_LEN, cond=0, sample=False, idx=i) for i in range(NP_SEQ)]
        _PASSOFF[0] = 0
        run_pass(pseqs, NP_SEQ * P_LEN, xp_d, yp_d)
    except _Stop:
        pass

    P.add("sp", lambda e: e.nop(), extra_deps=out_dmas, cost=50.0)
    P.schedule(window=1024)
    P.plan()
    sems = {}
    keys = [("eng", e) for e in ENGS] + [("dma", e, s) for e in ("sp", "pool") for s in range(NDMASEM)]
    for k in keys:
        sems[k] = es.enter_context(nc.semaphore("_".join(map(str, k))))
    block = es.enter_context(nc.Block())

    @block.tensor
    def _(e):
        P.emit_engine("pe", e, sems)

    @block.scalar
    def _(e):
        P.emit_engine("act", e, sems)

    @block.vector
    def _(e):
        P.emit_engine("dve", e, sems)

    @block.gpsimd
    def _(e):
        P.emit_engine("pool", e, sems)

    @block.sync
    def _(e):
        P.emit_engine("sp", e, sems)

    es.close()
    return nc, P


def host_prep(inp):
    f = lambda a: np.ascontiguousarray(np.asarray(a, dtype=np.float32))
    cols = _win_groups()
    w_in_r = f(np.asarray(inp["w_in"])[:, :, cols])
    L = DEPTH
    pvec = np.zeros((L, 128, NPV), np.float32)
    brow = np.zeros((L, NBR), np.float32)
    for l in range(L):
        pvec[l, :, PV_BMOD:PV_BMOD + 48] = _fm(inp["b_mod"][l])
        cw = np.asarray(inp["ssd_conv_w"][l])
        for k in range(5):
            pvec[l, :, PV_CW + k * 8:PV_CW + k * 8 + 8] = _fm(cw[k])
        pvec[l, :, PV_CB:PV_CB + 8] = _fm(inp["ssd_conv_b"][l])
        sw = np.asarray(inp["sc_conv_w"][l])
        for k in range(3):
            pvec[l, :, PV_SCW + k * 2:PV_SCW + k * 2 + 2] = _fm(sw[k])
        pvec[l, :, PV_L1G:PV_L1G + 8] = _fm(inp["ln1_g"][l])
        pvec[l, :, PV_L1B:PV_L1B + 8] = _fm(inp["ln1_b"][l])
        pvec[l, :, PV_L2G:PV_L2G + 8] = _fm(inp["ln2_g"][l])
        pvec[l, :, PV_L2B:PV_L2B + 8] = _fm(inp["ln2_b"][l])
        brow[l, BR_DTB:BR_DTB + 16] = np.asarray(inp["ssd_dt_bias"][l]).reshape(16)
        brow[l, BR_ALOG:BR_ALOG + 16] = np.asarray(inp["ssd_a_log"][l]).reshape(16)
        brow[l, BR_D:BR_D + 8] = np.asarray(inp["ssd_d"][l])
        pvec[l, :, PV_NW:PV_NW + 4] = _fm(inp["ssd_norm_w"][l])
        brow[l, BR_DNW:BR_DNW + 64] = np.asarray(inp["diff_norm_w"][l])
        brow[l, BR_LAM:BR_LAM + 128] = np.asarray(inp["diff_lambda"][l]).reshape(128)
    cos, sin = _rope_tables()
    shared = {
        "w_mod": f(inp["w_mod"]), "w_in_r": w_in_r, "w_out": f(inp["w_out"]), "w_up": f(inp["w_up"]),
        "w_down": f(inp["w_down"]), "pvec": pvec, "brow": brow, "consts": _consts(),
        "ropecos": f(cos), "ropesin": f(sin),
    }
    return shared


def core_inputs(inp, i, shared):
    f = lambda a: np.ascontiguousarray(np.asarray(a, dtype=np.float32))
    m = dict(shared)
    m["xp"] = f(np.asarray(inp["x_prompt"])[2 * i:2 * i + 2].reshape(NP_SEQ * P_LEN, D))
    m["xs"] = f(np.asarray(inp["x_sample"])[i])
    m["ck"] = f(np.asarray(inp["cache_k"])[i].reshape(DEPTH, PAST, 256))
    m["cv"] = f(np.asarray(inp["cache_v"])[i].reshape(DEPTH, PAST, 256))
    m["s0f"] = f(np.asarray(inp["state_ssm_fwd"])[i])
    m["s0b"] = f(np.asarray(inp["state_ssm_bwd"])[i])
    cv = np.stack([np.asarray(inp["c_ctx"]), np.asarray(inp["c"])[i]], axis=-1)
    m["cvec"] = f(cv.reshape(8, 128, 2).transpose(1, 0, 2).reshape(128, 16))
    return m


_CACHE = {}


def kernel(**inputs):
    if "nc" not in _CACHE:
        _CACHE["nc"] = build()[0]
    nc = _CACHE["nc"]
    shared = host_prep(inputs)
    in_maps = [core_inputs(inputs, i, shared) for i in range(8)]
    res = run_bass_kernel_spmd(nc, in_maps, core_ids=list(range(8)))
    r = res.results
    y_p = np.concatenate([r[i]["yp"].reshape(NP_SEQ, P_LEN, D) for i in range(8)], 0)
    y_s = np.stack([r[i]["ys"] for i in range(8)], 0)
    nk = np.concatenate([r[i]["nk"].reshape(NP_SEQ, DEPTH, P_LEN, 8, 32) for i in range(8)], 0)
    nv = np.concatenate([r[i]["nv"].reshape(NP_SEQ, DEPTH, P_LEN, 4, 64) for i in range(8)], 0)
    nsf = np.concatenate([r[i]["nsf"].reshape(NP_SEQ, DEPTH, 8, 64, 128) for i in range(8)], 0)
    nsb = np.concatenate([r[i]["nsb"].reshape(NP_SEQ, DEPTH, 8, 64, 128) for i in range(8)], 0)
    return (y_p.astype(np.float32), y_s.astype(np.float32), nk.astype(np.float32), nv.astype(np.float32),
            nsf.astype(np.float32), nsb.astype(np.float32))
```

```python
import math
from contextlib import ExitStack

import numpy as np
import concourse.bass as bass
import concourse.mybir as mybir
from concourse.bass_utils import run_bass_kernel_spmd

F32 = mybir.dt.float32
BF16 = mybir.dt.bfloat16
AF = mybir.ActivationFunctionType
ALU = mybir.AluOpType
AX = mybir.AxisListType

ENGS = ["pe", "act", "dve", "pool", "sp"]
NDMASEM = 24

D = 1024
DEPTH = 4
NP_SEQ, P_LEN = 2, 256
S_LEN = 2048
PAST = 256
IN_COLS = 3088
ALPHA = (2 * DEPTH) ** 0.25
LN_EPS = 1e-5
RMS_EPS = 1e-6
GRID_W = 64


def _dsize(dt):
    return 2 if dt == BF16 else 4


def region(ap):
    name = ap.tensor.name
    isps = name.startswith("ps")
    if not (isps or name.startswith("sb")):
        return None
    pat = list(ap.ap)
    sz = _dsize(ap.dtype)
    pstep = pat[0][0]
    pcnt = pat[0][1]
    p0 = ap.offset // pstep
    e0 = ap.offset % pstep
    span = 1
    for st, cnt in pat[1:]:
        span += (cnt - 1) * abs(st)
    b0, b1 = e0 * sz, (e0 + span) * sz
    if isps:
        return (name, 0, 128, (b0 // 2048) * 2048, ((b1 - 1) // 2048 + 1) * 2048)
    return (name, p0, p0 + pcnt, b0, b1)


def _ov(a, b):
    return a[1] < b[2] and b[1] < a[2] and a[3] < b[4] and b[3] < a[4]


def _cov(a, b):
    return a[1] <= b[1] and a[2] >= b[2] and a[3] <= b[3] and a[4] >= b[4]


class Prog:
    def __init__(self):
        self.ops = []
        self.writes = {}
        self.reads = {}
        self.eng_ops = {e: [] for e in ENGS}

    def add(self, eng, fn, reads=(), writes=(), dma=False, extra_deps=(), cost=300.0, lat=0.0):
        oid = len(self.ops)
        deps = set(extra_deps)
        odeps = set()
        rr = list(dict.fromkeys(r for r in (region(a) for a in reads) if r is not None))
        ww = list(dict.fromkeys(r for r in (region(a) for a in writes) if r is not None))
        ww = list(dict.fromkeys(ww + [r for r in rr if r[0].startswith("ps")]))
        for r in rr:
            for (w, wid) in self.writes.get(r[0], ()):
                if _ov(r, w):
                    deps.add(wid)
        for w in ww:
            for (w2, wid) in self.writes.get(w[0], ()):
                if _ov(w, w2):
                    deps.add(wid)
            for (r2, rid) in self.reads.get(w[0], ()):
                if _ov(w, r2):
                    deps.add(rid)
        ops = self.ops
        for w in ww:
            lst = self.writes.setdefault(w[0], [])
            lst[:] = [x for x in lst if not _cov(w, x[0])]
            lst.append((w, oid))
            rl = self.reads.setdefault(w[0], [])
            rl[:] = [x for x in rl if not _cov(w, x[0])]
        for r in rr:
            rl = self.reads.setdefault(r[0], [])
            if not dma:
                keep = []
                for x in rl:
                    if x[0] == r and ops[x[1]][0] == eng and not ops[x[1]][3]:
                        odeps.add(x[1])
                    else:
                        keep.append(x)
                rl[:] = keep
            rl.append((r, oid))
        deps.discard(oid)
        self.ops.append([eng, fn, deps, dma, odeps, cost, lat, _TAG[0]])
        self.eng_ops[eng].append(oid)
        return oid

    def schedule(self, window=48):
        ops = self.ops
        n = len(ops)
        succ = [[] for _ in range(n)]
        npred = [0] * n
        for oid in range(n):
            ds = ops[oid][2] | ops[oid][4]
            npred[oid] = len(ds)
            for d in ds:
                succ[d].append(oid)
        finish = [0.0] * n
        self.start_t = [0.0] * n
        ready_t = [0.0] * n
        done = [False] * n
        pend = {e: list(self.eng_ops[e]) for e in ENGS}
        head = {e: 0 for e in ENGS}
        free_t = {e: 0.0 for e in ENGS}
        new_order = {e: [] for e in ENGS}
        order = []
        best = {e: None for e in ENGS}
        dirty = set(ENGS)
        remaining = n
        while remaining:
            for e in list(dirty):
                lst = pend[e]
                h = head[e]
                while h < len(lst) and done[lst[h]]:
                    h += 1
                head[e] = h
                b = None
                cnt = 0
                i = h
                ft = free_t[e]
                while i < len(lst) and cnt < window:
                    o = lst[i]
                    i += 1
                    if done[o]:
                        continue
                    cnt += 1
                    if npred[o] == 0:
                        st = ready_t[o] if ready_t[o] > ft else ft
                        if b is None or st < b[0]:
                            b = (st, o)
                            if st <= ft:
                                break
                best[e] = b
            dirty.clear()
            pick = None
            for e in ENGS:
                b = best[e]
                if b is not None and (pick is None or b[0] < pick[0]):
                    pick = (b[0], b[1], e)
            assert pick is not None, "scheduler deadlock"
            st, o, e = pick
            eng, fn, deps, dma, odeps, cost, lat = ops[o][:7]
            self.start_t[o] = st
            free_t[e] = st + cost
            finish[o] = st + cost + lat
            done[o] = True
            remaining -= 1
            new_order[e].append(o)
            order.append(o)
            dirty.add(e)
            for s2 in succ[o]:
                npred[s2] -= 1
                if finish[o] > ready_t[s2]:
                    ready_t[s2] = finish[o]
                dirty.add(ops[s2][0])
        self.eng_ops = new_order
        self.sched_order = order
        self.est_time = max(finish) if n else 0.0

    def plan(self):
        ops = self.ops
        n = len(ops)
        order = getattr(self, "sched_order", None) or list(range(n))
        pos = {}
        for e in ENGS:
            for i, oid in enumerate(self.eng_ops[e]):
                pos[oid] = i
        gpos = {o: i for i, o in enumerate(order)}
        clock = {e: {e2: -1 for e2 in ENGS} for e in ENGS}
        snap = {}
        dma_known = {e: set() for e in ENGS}
        waits = {}
        marked = set()
        for oid in order:
            eng, fn, deps, dma = ops[oid][:4]
            w = []
            ck = clock[eng]
            for d in sorted(deps, key=lambda x: gpos[x], reverse=True):
                de, ddma = ops[d][0], ops[d][3]
                if ddma:
                    if d in dma_known[eng]:
                        continue
                    dma_known[eng].add(d)
                    w.append(d)
                    marked.add(d)
                else:
                    if de == "pe" and eng == "pe":
                        continue
                    p = pos[d]
                    if ck[de] >= p:
                        continue
                    w.append(d)
                    marked.add(d)
                    ck[de] = p
                s = snap.get(d)
                if s is not None:
                    for e2 in ENGS:
                        if s[e2] > ck[e2]:
                            ck[e2] = s[e2]
            waits[oid] = w
            snap[oid] = tuple(ck[e2] for e2 in ENGS)
            snap[oid] = dict(zip(ENGS, snap[oid]))
        cnt = {e: 0 for e in ENGS}
        semval = {}
        dma_n = {e: 0 for e in ENGS}
        dma_prev = {}
        dma_hist = {e: [] for e in ENGS}
        for oid in order:
            eng, fn, deps, dma = ops[oid][:4]
            if dma:
                k = dma_n[eng]
                dma_n[eng] += 1
                semval[oid] = (("dma", eng, k % NDMASEM), 16 * (k // NDMASEM + 1))
                if k >= NDMASEM:
                    dma_prev[oid] = dma_hist[eng][k - NDMASEM]
                dma_hist[eng].append(oid)
            elif oid in marked:
                cnt[eng] += 1
                semval[oid] = (("eng", eng), cnt[eng])
        self.sem_counts = cnt
        self._plan = (waits, semval, dma_prev)

    def emit_engine(self, ename, eng, sems):
        waits, semval, dma_prev = self._plan
        ops = self.ops
        for oid in self.eng_ops[ename]:
            e, fn, deps, dma = ops[oid][:4]
            if dma and oid in dma_prev:
                pk, pv = semval[dma_prev[oid]]
                eng.wait_ge(sems[pk], pv)
            for d in waits[oid]:
                k, v = semval[d]
                eng.wait_ge(sems[k], v)
            ins = fn(eng)
            if oid in semval:
                k, v = semval[oid]
                ins.then_inc(sems[k], 16 if dma else 1)


Z0, XBC0, DT0, SCB0, SCC0, SCH0, Q0, K0, V0 = 0, 512, 1536, 1552, 1808, 2064, 2320, 2576, 2832


def _rope_perm():
    idx = np.arange(256).reshape(8, 2, 2, 8)
    return idx[:, :, ::-1, :].reshape(256)


def _win_groups():
    perm = _rope_perm()
    g = []
    g.append(np.arange(XBC0, XBC0 + 512))
    g.append(np.arange(XBC0 + 512, XBC0 + 1024))
    g.append(np.concatenate([Q0 + np.arange(256), Q0 + perm]))
    g.append(np.concatenate([K0 + np.arange(256), K0 + perm]))
    g.append(np.concatenate([V0 + np.arange(256), DT0 + np.arange(16), K0 + np.arange(240)]))
    g.append(np.concatenate([SCC0 + np.arange(256), SCH0 + np.arange(256)]))
    g.append(np.concatenate([SCB0 + np.arange(256), K0 + np.arange(256)]))
    g.append(np.arange(Z0, Z0 + 512))
    return np.concatenate(g)


NG = 8
C_ID, C_UF, C_LF, C_UB, C_LB, C_ONE, C_HM, C_MF, C_MB = 0, 128, 256, 384, 512, 640, 768, 772, 900
NCONST = 1028


def _consts():
    c = np.zeros((128, NCONST), np.float32)
    t = np.arange(128)[:, None]
    i = np.arange(128)[None, :]
    c[:, C_ID:C_ID + 128] = (t == i)
    c[:, C_UF:C_UF + 128] = (t <= i)
    c[:, C_LF:C_LF + 128] = (t > i)
    c[:, C_UB:C_UB + 128] = (t >= i)
    c[:, C_LB:C_LB + 128] = (t < i)
    c[:, C_ONE:C_ONE + 128] = 1.0
    for h in range(4):
        c[32 * h:32 * h + 32, C_HM + h] = 1.0
    c[:, C_MF:C_MF + 128] = (i >= t)
    c[:, C_MB:C_MB + 128] = (i <= t)
    return c


def _rope_tables():
    p = np.arange(128)
    d = p % 32
    axis, half, f = d // 16, (d % 16) // 8, d % 8
    inv = 10000.0 ** (-np.arange(8, dtype=np.float32) / 8)
    tt = np.arange(S_LEN)
    row = (tt // GRID_W).astype(np.float32)
    col = (tt % GRID_W).astype(np.float32)
    posv = np.where(axis[:, None] == 0, row[None, :], col[None, :]).astype(np.float32)
    ang = (posv * inv[f][:, None].astype(np.float32)).astype(np.float32)
    cos = np.cos(ang).astype(np.float32)
    sin = np.sin(ang).astype(np.float32)
    sgn = np.where(half == 0, -1.0, 1.0).astype(np.float32)[:, None]
    return cos, (sin * sgn).astype(np.float32)


PV_BMOD, PV_CW, PV_CB, PV_SCW, PV_L1G, PV_L1B, PV_L2G, PV_L2B, PV_NW = 0, 48, 88, 96, 102, 110, 118, 126, 134
NPV = 138
BR_DTB, BR_ALOG, BR_D, BR_DNW, BR_LAM = 0, 16, 32, 40, 104
NBR = 232


def _fm(v):
    v = np.asarray(v)
    return v.reshape(-1, 128).T


class _Stop(Exception):
    pass


_LIMIT = [10 ** 9]
_SKIP = set()


_PASSOFF = [0]
_TAG = [0]


def chk(n):
    _TAG[0] = n + _PASSOFF[0]
    if n + _PASSOFF[0] > _LIMIT[0]:
        raise _Stop()


def build():
    nc = bass.Bass("TRN2", target_bir_lowering=False)
    P = Prog()
    es = ExitStack()
    dr = lambda name, shape, kind="ExternalInput": nc.dram_tensor(name, list(shape), F32, kind=kind).ap()
    xp_d = dr("xp", [NP_SEQ * P_LEN, D])
    xs_d = dr("xs", [S_LEN, D])
    ck_d = dr("ck", [DEPTH, PAST, 256])
    cv_d = dr("cv", [DEPTH, PAST, 256])
    s0f_d = dr("s0f", [DEPTH, 8, 64, 128])
    s0b_d = dr("s0b", [DEPTH, 8, 64, 128])
    cvec_d = dr("cvec", [128, 16])
    wmod_d = dr("w_mod", [DEPTH, D, 6 * D])
    win_d = dr("w_in_r", [DEPTH, D, NG * 512])
    wout_d = dr("w_out", [DEPTH, D, D])
    wup_d = dr("w_up", [DEPTH, D, 4 * D])
    wdn_d = dr("w_down", [DEPTH, 4 * D, D])
    pvec_d = dr("pvec", [DEPTH, 128, NPV])
    brow_d = dr("brow", [DEPTH, NBR])
    const_d = dr("consts", [128, NCONST])
    cos_d = dr("ropecos", [128, S_LEN])
    sin_d = dr("ropesin", [128, S_LEN])
    yp_d = dr("yp", [NP_SEQ * P_LEN, D], "ExternalOutput")
    ys_d = dr("ys", [S_LEN, D], "ExternalOutput")
    nk_d = dr("nk", [NP_SEQ * DEPTH * P_LEN, 256], "ExternalOutput")
    nv_d = dr("nv", [NP_SEQ * DEPTH * P_LEN, 256], "ExternalOutput")
    nsf_d = dr("nsf", [NP_SEQ * DEPTH * 512, 128], "ExternalOutput")
    nsb_d = dr("nsb", [NP_SEQ * DEPTH * 512, 128], "ExternalOutput")
    out_dmas = []

    sb = lambda name, cols: es.enter_context(nc.sbuf_tensor(name, [128, cols], F32))
    TMAX = S_LEN
    sbX = sb("sbX", 8 * TMAX)
    sbU = sb("sbU", 4 * TMAX)
    sbA = sb("sbA", 4 * TMAX)
    sbB = sb("sbB", 4 * (TMAX + 8))
    sbW = sb("sbW", 4096)
    sbC = sb("sbC", C_MF)
    sbCb = sb("sbCb", NCONST // 2)
    sbPV = sb("sbPV", NPV)
    sbBR = sb("sbBR", NBR)
    sbMOD = sb("sbMOD", DEPTH * 2 * 48)
    sbDER = sb("sbDER", 2 * 48 + 16)
    sbT = sb("sbT", 2560)
    sbT2 = sb("sbT2", 1664)
    sbS = sb("sbS", 2 * 512 + 2 * 256)
    ps = es.enter_context(nc.psum_tensor("ps", [128, 8, 512], F32))

    cst = lambda off, n=128: sbC[:, off:off + n]
    cstb = lambda off, n=128: sbCb[:, :].bitcast(BF16)[:, off:off + n]

    def fsz(ap):
        return float(ap.free_size())

    def ecost(eng, out):
        f = fsz(out)
        if eng == "act":
            return 220.0 + f / 1.4
        if eng == "dve":
            return 100.0 + f / 0.96
        return 300.0 + 2.0 * f

    def mm(out, lhsT, rhs, start=True, stop=True, skip=False, tpos=None):
        passes = 4.0 if lhsT.dtype == F32 else 1.0
        c = 30.0 + fsz(rhs) * passes / 2.0
        if tpos is not None:
            c = c / 2.0 + 10.0
            fn = lambda e: e.matmul(out, lhsT, rhs, start=start, stop=stop, tile_position=tpos)
        elif skip:
            fn = lambda e: e.matmul(out, lhsT, rhs, start=start, stop=stop, skip_group_check=True)
        else:
            fn = lambda e: e.matmul(out, lhsT, rhs, start=start, stop=stop)
        return P.add("pe", fn, reads=[lhsT, rhs], writes=[out], cost=c)

    def tp(out, in_, ident):
        c = 40.0 + 128.0 * (4.0 if in_.dtype == F32 else 1.0) / 2.0
        return P.add("pe", lambda e: e.transpose(out, in_, ident), reads=[in_, ident], writes=[out], cost=c)

    def act(out, in_, func, bias=None, scale=None, accum=None):
        kw = {}
        rd = [in_]
        if bias is not None:
            kw["bias"] = bias
            if not isinstance(bias, float):
                rd.append(bias)
        if scale is not None:
            kw["scale"] = scale
            if not isinstance(scale, float):
                rd.append(scale)
        wr = [out]
        if accum is not None:
            kw["accum_out"] = accum
            wr.append(accum)
        return P.add("act", lambda e: e.activation(out=out, in_=in_, func=func, **kw), reads=rd, writes=wr, cost=ecost("act", out))

    def tt(eng, out, in0, in1, op):
        return P.add(eng, lambda e: e.tensor_tensor(out=out, in0=in0, in1=in1, op=op), reads=[in0, in1], writes=[out], cost=ecost(eng, out))

    def ts(eng, out, in0, s1, s2=None, op0=ALU.mult, op1=None):
        rd = [in0] + [s for s in (s1, s2) if s is not None and not isinstance(s, float)]
        if op1 is None:
            fn = lambda e: e.tensor_scalar(out=out, in0=in0, scalar1=s1, scalar2=None, op0=op0)
        else:
            fn = lambda e: e.tensor_scalar(out=out, in0=in0, scalar1=s1, scalar2=s2, op0=op0, op1=op1)
        return P.add(eng, fn, reads=rd, writes=[out], cost=ecost(eng, out))

    def stt(eng, out, in0, scalar, in1, op0, op1):
        rd = [in0, in1] + ([] if isinstance(scalar, float) else [scalar])
        return P.add(eng, lambda e: e.scalar_tensor_tensor(out=out, in0=in0, scalar=scalar, in1=in1, op0=op0, op1=op1),
                     reads=rd, writes=[out], cost=ecost(eng, out))

    def cp(eng, out, in_):
        if eng == "act":
            return act(out, in_, AF.Copy)
        return P.add(eng, lambda e: e.tensor_copy(out=out, in_=in_), reads=[in_], writes=[out], cost=ecost(eng, out))

    def mset(eng, out, val):
        return P.add(eng, lambda e: e.memset(out, val), writes=[out], cost=150.0)

    def dma(q, out, in_, is_out=False):
        nb_ = fsz(out) * out.partition_size() * (2 if out.dtype == BF16 else 4)
        if q == "pool":
            o = P.add(q, lambda e: e.dma_start(out=out, in_=in_), reads=[in_], writes=[out], dma=True, cost=1000.0, lat=3000.0 + nb_ / 100.0)
        else:
            o = P.add(q, lambda e: e.dma_start(out=out, in_=in_), reads=[in_], writes=[out], dma=True, cost=100.0, lat=2500.0 + nb_ / 150.0)
        if is_out:
            out_dmas.append(o)
        return o

    def recip(out, in_):
        return P.add("dve", lambda e: e.reciprocal(out=out, in_=in_), reads=[in_], writes=[out], cost=ecost("dve", out))

    def rsqrt(buf, eps):
        act(buf, buf, AF.Sqrt, bias=cst_eps[eps], scale=1.0)
        recip(buf, buf)

    bank_rr = {"A": [0, [0, 1, 2, 3]], "B": [0, [4, 5]], "C": [0, [6, 7]], "S": [0, [0, 1, 2, 3, 4, 5, 6, 7]]}
    tdcount = [0]
    pairc = [0]
    lnpar = [0]

    def bank(cls):
        st = bank_rr[cls]
        b = st[1][st[0] % len(st[1])]
        st[0] += 1
        return ps[:, b, :]

    def bf(ap):
        return ap.bitcast(BF16)

    dma("sp", sbC[:, :], const_d[:, 0:C_MF])
    dma("pool", sbCb[:, :].bitcast(BF16), const_d)
    identb = cstb(C_ID)
    identf = cst(C_ID)
    sbE = sb("sbE", 4)
    sbQ = sb("sbQ", 32)
    mset("pool", sbE[:, 0:1], LN_EPS / (ALPHA * ALPHA))
    mset("pool", sbE[:, 1:2], RMS_EPS)
    cst_eps = {"ln": sbE[:, 0:1], "rms": sbE[:, 1:2]}

    sbCS = sb("sbCS", 24 + 48)
    cvb = sbCS[:, 0:16]
    dma("sp", cvb, cvec_d)
    csb = bf(sbCS[:, 16:24])
    act(csb, cvb, AF.Silu)
    sbBM = sbCS[:, 24:72]
    MOD = sbMOD[:, :].rearrange("p (l c w) -> p l c w", l=DEPTH, c=2)
    wslot = [bf(sbW[:, 0:2048]).rearrange("p (k n) -> p k n", k=8), bf(sbW[:, 2048:4096]).rearrange("p (k n) -> p k n", k=8)]
    wcnt = [0]

    def wload(src2d, ncols):
        s = wslot[wcnt[0] % 2]
        wcnt[0] += 1
        dma("pool", s[:, :, 0:ncols], src2d.rearrange("(k p) n -> p k n", p=128))
        return s

    mslot = [bf(sbX[:, 4096 + i * 2048:4096 + (i + 1) * 2048]).rearrange("p (k n) -> p k n", k=8) for i in range(6)]
    mcnt = [0]

    def emit_mod(l):
        dma("sp", sbBM, pvec_d[l][:, PV_BMOD:PV_BMOD + 48])
        for g in range(12):
            wt = mslot[mcnt[0] % 6]
            mcnt[0] += 1
            dma("pool", wt, wmod_d[l][:, g * 512:(g + 1) * 512].rearrange("(k p) n -> p k n", p=128))
            pb = bank("A")
            for m in range(4):
                for k in range(8):
                    mm(pb[:, m * 2:m * 2 + 2], wt[:, k, m * 128:(m + 1) * 128], csb[:, k * 2:k * 2 + 2],
                       start=(k == 0), stop=(k == 7))
            for m in range(4):
                oc = g * 4 + m
                for c in range(2):
                    ts("dve", MOD[:, l, c, oc:oc + 1], pb[:, m * 2 + c:m * 2 + c + 1], sbBM[:, oc:oc + 1],
                       op0=ALU.add)

    emit_mod(0)

    def run_pass(seqs, T, x_d, y_d):
        chk(1)
        NT = T // 128
        X = sbX[:, 0:8 * T].rearrange("p (k t) -> p k t", k=8)
        U = bf(sbU[:, 0:4 * T]).rearrange("p (k t) -> p k t", k=8)
        XBC = bf(sbA[:, 0:4 * T]).rearrange("p (k t) -> p k t", k=8)
        TB = T + 4 * len(seqs)
        Bv = bf(sbB[:, 0:4 * TB]).rearrange("p (k t) -> p k t", k=8)
        for si, s in enumerate(seqs):
            s["t0"] = sum(q["n"] for q in seqs[:si])
            s["b0"] = s["t0"] + 4 * si + 2
            s["nk"] = s["n"] + (PAST if s["sample"] else 0)
        blocks = []
        for s in seqs:
            for o in range(0, s["n"], 512):
                blocks.append((s, o, min(512, s["n"] - o)))
        dblocks = []
        for (s_, o_, nb_) in blocks:
            t0_ = s_["t0"] + o_
            if dblocks and dblocks[-1][2] == s_["cond"] and dblocks[-1][0] + dblocks[-1][1] == t0_ and dblocks[-1][1] + nb_ <= 512:
                dblocks[-1] = (dblocks[-1][0], dblocks[-1][1] + nb_, s_["cond"])
            else:
                dblocks.append((t0_, nb_, s_["cond"]))
        NKMAX = max(s["nk"] for s in seqs)
        NKMAX = max(s["nk"] for s in seqs)
        Kf = bf(sbA[:, 0:NKMAX]).rearrange("p (c t) -> p c t", c=2)
        vofs = NKMAX
        Vt = bf(sbA[:, vofs:vofs + (NKMAX // 128) * 132]).rearrange("p (t h e) -> p t h e", h=4, e=66)
        AT0 = 4680
        HID = bf(sbB[:, 0:4096]).rearrange("p (b k t) -> p b k t", b=2, k=8)
        YBW = bf(sbU[:, 0:NT * 256]).rearrange("p (t c) -> p t c", c=512)
        TSETS = [sbT, sbU[:, 4096:6656]]
        EXS = [sbT2[:, 96:120], sbU[:, 6656:6680]]
        SSQS = [sbT2[:, 120:121], sbU[:, 6680:6681]]

        for ti in range(NT):
            xt = sbT[:, 0:1024]
            dma("sp", xt, x_d[ti * 128:(ti + 1) * 128, :])
            for half in range(2):
                pb = bank("A")
                for k in range(4):
                    tp(pb[:, k * 128:(k + 1) * 128], xt[:, (half * 4 + k) * 128:(half * 4 + k + 1) * 128], identf)
                cp("act" if half else "dve", X[:, half * 4:half * 4 + 4, ti * 128:(ti + 1) * 128],
                   pb.rearrange("p (k t) -> p k t", k=4))

        chk(2)

        def _layer(l):
            lam_init = 0.8 - 0.6 * math.exp(-0.3 * l)
            dma("sp", sbPV[:, :], pvec_d[l])
            dma("sp", sbBR[:, :], brow_d[l].partition_broadcast(128))
            pv = lambda off, n=1: sbPV[:, off:off + n]
            DER = sbDER[:, 0:96].rearrange("p (c w) -> p c w", c=2)
            for c in range(2):
                ts("pool", DER[:, c, 0:8], MOD[:, l, c, 8:16], 1.0, op0=ALU.add)
                ts("pool", DER[:, c, 8:16], MOD[:, l, c, 16:24], 1.0 / ALPHA, op0=ALU.mult)
                ts("pool", DER[:, c, 40:48], MOD[:, l, c, 32:40], 1.0, op0=ALU.add)
                tt("pool", DER[:, c, 16:24], DER[:, c, 40:48], pv(PV_L1G, 8), ALU.mult)
                tt("pool", DER[:, c, 24:32], DER[:, c, 40:48], pv(PV_L1B, 8), ALU.mult)
                tt("pool", DER[:, c, 24:32], DER[:, c, 24:32], MOD[:, l, c, 24:32], ALU.add)
                ts("pool", DER[:, c, 32:40], MOD[:, l, c, 40:48], 1.0 / ALPHA, op0=ALU.mult)
            LT = sbDER[:, 96:112]
            lp = sbBR[:, BR_LAM:BR_LAM + 128].rearrange("p (a d) -> p a d", a=4)
            prod = sbT2[:, 0:64].rearrange("p (a d) -> p a d", a=2)
            tt("dve", prod[:, 0, :], lp[:, 0, :], lp[:, 1, :], ALU.mult)
            tt("dve", prod[:, 1, :], lp[:, 2, :], lp[:, 3, :], ALU.mult)
            P.add("dve", lambda e: e.reduce_sum(out=LT[:, 0:2], in_=prod, axis=AX.X), reads=[prod], writes=[LT[:, 0:2]])
            act(LT[:, 2:4], LT[:, 0:2], AF.Exp)
            tt("dve", LT[:, 4:5], LT[:, 3:4], LT[:, 2:3], ALU.subtract)
            ts("dve", LT[:, 5:6], LT[:, 4:5], -lam_init, op0=ALU.add)
            NLAM = LT[:, 5:6]
            Abc = sbDER[:, 112:112] if False else None
            ABR = sbT2[:, 64:80]
            act(ABR, sbBR[:, BR_ALOG:BR_ALOG + 16], AF.Exp)
            ts("dve", ABR, ABR, -1.0, op0=ALU.mult)

            for (t0, nb, c) in dblocks:
                for k in range(8):
                    if k % 2:
                        ts("dve", U[:, k, t0:t0 + nb], X[:, k, t0:t0 + nb], DER[:, c, k:k + 1], MOD[:, l, c, k:k + 1], op0=ALU.mult, op1=ALU.add)
                    else:
                        act(U[:, k, t0:t0 + nb], X[:, k, t0:t0 + nb], AF.Identity, bias=MOD[:, l, c, k:k + 1], scale=DER[:, c, k:k + 1])

            chk(3)
            if (not seqs[0]["sample"]) and l + 1 < DEPTH:
                _t = _TAG[0]
                _TAG[0] = 0
                emit_mod(l + 1)
                _TAG[0] = _t
            wg = lambda g: wload(win_d[l][:, g * 512:(g + 1) * 512], 512)

            def proj_fm(wt, ncol0, nch, evac):
                for (s, o, nb) in blocks:
                    t0 = s["t0"] + o
                    for m in range(nch):
                        pb = bank("A")
                        for k in range(8):
                            mm(pb[:, 0:nb], wt[:, k, ncol0 + m * 128:ncol0 + (m + 1) * 128], U[:, k, t0:t0 + nb],
                               start=(k == 0), stop=(k == 7))
                        evac(s, o, nb, m, pb[:, 0:nb])

            def proj_tm(wt, ncol0, ncols, evac):
                for s in seqs:
                    for ti in range(s["n"] // 128):
                        t0 = s["t0"] + ti * 128
                        pb = bank("A")
                        for k in range(8):
                            mm(pb[:, 0:ncols], U[:, k, t0:t0 + 128], wt[:, k, ncol0:ncol0 + ncols], start=(k == 0), stop=(k == 7))
                        evac(s, ti, pb[:, 0:ncols])

            DTR = sbT2[:, 1152:1152 + NT * 16].rearrange("p (t c) -> p t c", c=16)
            for s in seqs:
                wt4 = wg(4)
                koff = PAST // 128 if s["sample"] else 0
                if s["sample"]:
                    for kt in range(2):
                        kst = bf(sbT[:, 2048:2176])
                        dma("pool", kst, ck_d[l, kt * 128:(kt + 1) * 128, :])
                        pb = bank("C")
                        pbb = pb.bitcast(BF16)
                        for c2 in range(2):
                            tp(pbb[:, c2 * 128:(c2 + 1) * 128], kst[:, c2 * 128:(c2 + 1) * 128], identb)
                        cp("act", Kf[:, :, kt * 128:(kt + 1) * 128], pbb[:, 0:256].rearrange("p (c t) -> p c t", c=2))
                        dma("pool", Vt[:, kt, :, 0:64], cv_d[l, kt * 128:(kt + 1) * 128, :].rearrange("t (h e) -> t h e", h=4))
                for ti in range(s["nk"] // 128):
                    if "A" not in _SKIP:
                        mset("pool", Vt[:, ti, :, 64:66], 1.0)
                for ti in range(s["n"] // 128):
                    t0 = s["t0"] + ti * 128
                    pb = bank("A")
                    for k in range(8):
                        mm(pb[:, 0:272], U[:, k, t0:t0 + 128], wt4[:, k, 0:272], start=(k == 0), stop=(k == 7))
                    if "B" not in _SKIP:
                        cp("act", Vt[:, koff + ti, :, 0:64], pb[:, 0:256].rearrange("p (h e) -> p h e", h=4))
                    if "C" not in _SKIP:
                        tt("dve", DTR[:, s["t0"] // 128 + ti, :], pb[:, 256:272], sbBR[:, BR_DTB:BR_DTB + 16], ALU.add)
                    if not s["sample"] and "D" not in _SKIP:
                        st = sbT[:, 1024:1280]
                        cp("dve", st, pb[:, 0:256])
                        r0 = (s["idx"] * DEPTH + l) * P_LEN + ti * 128
                        dma("sp", nv_d[r0:r0 + 128, :], st, is_out=True)
                chk(3.1)
                wt3 = wg(3)
                koffc = PAST if s["sample"] else 0
                for o in range(0, s["n"], 512):
                    nb = min(512, s["n"] - o)
                    t0 = s["t0"] + o
                    if s["sample"]:
                        cs = sbT[:, 1024:1536]
                        sn = sbT[:, 1536:2048]
                        dma("sp", cs[:, 0:nb], cos_d[:, o:o + nb])
                        dma("sp", sn[:, 0:nb], sin_d[:, o:o + nb])
                    for m in range(2):
                        pk = bank("A")
                        for k in range(8):
                            mm(pk[:, 0:nb], wt3[:, k, m * 128:(m + 1) * 128], U[:, k, t0:t0 + nb], start=(k == 0), stop=(k == 7))
                        if s["sample"]:
                            pkp = bank("A")
                            for k in range(8):
                                mm(pkp[:, 0:nb], wt3[:, k, 256 + m * 128:256 + (m + 1) * 128], U[:, k, t0:t0 + nb], start=(k == 0), stop=(k == 7))
                            t1 = sbT[:, 0:nb]
                            t2 = sbT[:, 512:512 + nb]
                            tt("dve", t1, pk[:, 0:nb], cs[:, 0:nb], ALU.mult)
                            tt("dve", t2, pkp[:, 0:nb], sn[:, 0:nb], ALU.mult)
                            tt("pool", Kf[:, m, koffc + o:koffc + o + nb], t1, t2, ALU.add)
                        else:
                            cp("act", Kf[:, m, koffc + o:koffc + o + nb], pk[:, 0:nb])
                chk(3.2)
                wt2 = wg(2)
                nkt = s["nk"] // 128
                for o in range(0, s["n"], 512):
                    nb = min(512, s["n"] - o)
                    nq = nb // 128
                    t0 = s["t0"] + o
                    QR = bf(sbA[:, AT0:AT0 + 512]).rearrange("p (c t) -> p c t", c=2)
                    if s["sample"]:
                        cs = sbT[:, 1024:1536]
                        sn = sbT[:, 1536:2048]
                        dma("sp", cs[:, 0:nb], cos_d[:, o:o + nb])
                        dma("sp", sn[:, 0:nb], sin_d[:, o:o + nb])
                    for m in range(2):
                        pq = bank("A")
                        for k in range(8):
                            mm(pq[:, 0:nb], wt2[:, k, m * 128:(m + 1) * 128], U[:, k, t0:t0 + nb], start=(k == 0), stop=(k == 7))
                        if s["sample"]:
                            pqp = bank("A")
                            for k in range(8):
                                mm(pqp[:, 0:nb], wt2[:, k, 256 + m * 128:256 + (m + 1) * 128], U[:, k, t0:t0 + nb], start=(k == 0), stop=(k == 7))
                            t1 = sbT[:, 0:nb]
                            t2 = sbT[:, 512:512 + nb]
                            tt("dve", t1, pq[:, 0:nb], cs[:, 0:nb], ALU.mult)
                            tt("dve", t2, pqp[:, 0:nb], sn[:, 0:nb], ALU.mult)
                            tt("pool", QR[:, m, 0:nb], t1, t2, ALU.add)
                        else:
                            cp("act", QR[:, m, 0:nb], pq[:, 0:nb])
                    chk(3.3)
                    YT = bf(sbA[:, AT0 + 512:AT0 + 1024]).rearrange("p (q c) -> p q c", q=4)
                    for i in range(4):
                        OS = sbA[:, AT0 + 2560:AT0 + 2560 + 520].rearrange("p (q s e) -> p q s e", q=4, s=2)
                        cch = i // 2
                        accs = [ps[:, 4 + 2 * (i % 2) + sm_, :] for sm_ in range(2)]
                        for kt in range(nkt):
                            b0 = 2 * (pairc[0] % 2)
                            pairc[0] += 1
                            pst2 = ps[:, b0:b0 + 2, :]
                            for sm in range(2):
                                hrow = (2 * i + sm) % 4
                                mm(pst2[:, sm, 0:nb], Kf[32 * hrow:32 * hrow + 32, cch, kt * 128:(kt + 1) * 128],
                                   QR[32 * hrow:32 * hrow + 32, cch, 0:nb], tpos=(32 * hrow, 0))
                            pofs = AT0 + 1536 + (pairc[0] % 2) * 512
                            PT2 = bf(sbA[:, pofs:pofs + 512]).rearrange("p (u t) -> p u t", u=2)
                            act(PT2[:, :, 0:nb], pst2[:, :, 0:nb], AF.Exp, scale=float(32 ** -0.5))
                            for sm in range(2):
                                for qt in range(nq):
                                    mm(accs[sm][:, qt * 65:(qt + 1) * 65], PT2[:, sm, qt * 128:(qt + 1) * 128], Vt[:, kt, i, 0:65],
                                       start=(kt == 0 and qt == 0), stop=(kt == nkt - 1), skip=True)
                        for sm in range(2):
                            cp("act" if sm else "dve", OS[:, 0:nq, sm, :], accs[sm][:, 0:nq * 65].rearrange("p (q e) -> p q e", q=nq))
                        chk(3.4)
                        RD = sbT2[:, 80:88].rearrange("p (q s) -> p q s", q=4)
                        recip(RD[:, 0:nq, :], OS[:, 0:nq, :, 64])
                        ON = sbT[:, 0:512].rearrange("p (q s e) -> p q s e", q=4, s=2)
                        tt("dve", ON[:, 0:nq], OS[:, 0:nq, :, 0:64], RD[:, 0:nq, :].unsqueeze(3).to_broadcast([128, nq, 2, 64]), ALU.mult)
                        DF = sbT[:, 512:768].rearrange("p (q e) -> p q e", q=4)
                        stt("dve", DF[:, 0:nq], ON[:, 0:nq, 1, :], NLAM, ON[:, 0:nq, 0, :], ALU.mult, ALU.add)
                        SQ = sbT[:, 768:1024].rearrange("p (q e) -> p q e", q=4)
                        tt("pool", SQ[:, 0:nq], DF[:, 0:nq], DF[:, 0:nq], ALU.mult)
                        SSA = sbQ[:, 16:32].rearrange("p (q h) -> p q h", h=4)
                        SS = SSA[:, 0:nq, i]
                        P.add("dve", lambda e, SS=SS, SQ=SQ, nq=nq: e.reduce_sum(out=SS, in_=SQ[:, 0:nq], axis=AX.X),
                              reads=[SQ[:, 0:nq]], writes=[SS])
                        stt("dve", YT[:, 0:nq, i * 64:(i + 1) * 64], DF[:, 0:nq], float(1.0 - lam_init),
                            sbBR[:, BR_DNW:BR_DNW + 64].unsqueeze(1).to_broadcast([128, nq, 64]), ALU.mult, ALU.mult)
                    chk(3.5)
                    SSA = sbQ[:, 16:32].rearrange("p (q h) -> p q h", h=4)
                    ts("dve", SSA[:, 0:nq, :], SSA[:, 0:nq, :], 1.0 / 64, op0=ALU.mult)
                    rsqrt(SSA[:, 0:nq, :], "rms")
                    YT4 = YT.rearrange("p q (h e) -> p q h e", h=4)
                    tt("dve", YT4[:, 0:nq], YT4[:, 0:nq], SSA[:, 0:nq, :].unsqueeze(3).to_broadcast([128, nq, 4, 64]), ALU.mult)
                    for qt in range(nq):
                        pb = bank("C")
                        pbb = pb.bitcast(BF16)
                        for c2 in range(2):
                            tp(pbb[:, c2 * 128:(c2 + 1) * 128], YT[:, qt, c2 * 128:(c2 + 1) * 128], identb)
                        cc = s["b0"] + o + qt * 128
                        cp("act", Bv[:, 6:8, cc:cc + 128], pbb[:, 0:256].rearrange("p (c t) -> p c t", c=2))

            chk(4)
            for s in seqs:
                mset("pool", Bv[:, 0:4, s["b0"] - 2:s["b0"]], 0.0)
                mset("pool", Bv[:, 0:4, s["b0"] + s["n"]:s["b0"] + s["n"] + 2], 0.0)

            DG = bf(sbT2[:, 128:128 + 640]).rearrange("p (r k c) -> p r k c", r=2, k=5)
            for half in range(2):
                wt = wg(half)
                proj_fm(wt, 0, 4, lambda s, o, nb, m, pb: cp("act" if m % 2 else "dve",
                        Bv[:, m, s["b0"] + o:s["b0"] + o + nb], pb))
                for m in range(4):
                    ch = half * 4 + m
                    dg = DG[:, ch % 2]
                    for k in range(5):
                        ts("dve", dg[:, k, :], identb, pv(PV_CW + k * 8 + ch), op0=ALU.mult)
                    for (s, o, nb) in blocks:
                        pb = bank("A")
                        for k in range(5):
                            c0 = s["b0"] + o + k - 2
                            mm(pb[:, 0:nb], dg[:, k, :], Bv[:, m, c0:c0 + nb], start=(k == 0), stop=(k == 4))
                        act(XBC[:, ch, s["t0"] + o:s["t0"] + o + nb], pb[:, 0:nb], AF.Silu, bias=pv(PV_CB + ch))

            chk(5)
            for s in seqs:
                mset("pool", Bv[:, 0:2, s["b0"] - 2:s["b0"]], 0.0)
                mset("pool", Bv[:, 0:2, s["b0"] + s["n"]:s["b0"] + s["n"] + 2], 0.0)
            wt = wg(5)
            for (s, o, nb) in blocks:
                t0 = s["t0"] + o
                for m in range(2):
                    pc = bank("A")
                    ph = bank("A")
                    for k in range(8):
                        mm(pc[:, 0:nb], wt[:, k, m * 128:(m + 1) * 128], U[:, k, t0:t0 + nb], start=(k == 0), stop=(k == 7))
                    for k in range(8):
                        mm(ph[:, 0:nb], wt[:, k, 256 + m * 128:256 + (m + 1) * 128], U[:, k, t0:t0 + nb], start=(k == 0), stop=(k == 7))
                    tmp = sbT[:, 0:nb]
                    cp("act", tmp, ph[:, 0:nb])
                    tt("dve", Bv[:, m, s["b0"] + o:s["b0"] + o + nb], pc[:, 0:nb], tmp, ALU.mult)
            wt6 = wg(6)
            DG3 = bf(sbT2[:, 768:768 + 384]).rearrange("p (m k c) -> p m k c", m=2, k=3)
            for m in range(2):
                for k in range(3):
                    ts("dve", DG3[:, m, k, :], identb, pv(PV_SCW + k * 2 + m), op0=ALU.mult)
            for (s, o, nb) in blocks:
                t0 = s["t0"] + o
                for m in range(2):
                    pbb = bank("A")
                    pcc = bank("A")
                    for k in range(8):
                        mm(pbb[:, 0:nb], wt6[:, k, m * 128:(m + 1) * 128], U[:, k, t0:t0 + nb], start=(k == 0), stop=(k == 7))
                    for k in range(3):
                        c0 = s["b0"] + o + k - 1
                        mm(pcc[:, 0:nb], DG3[:, m, k, :], Bv[:, m, c0:c0 + nb], start=(k == 0), stop=(k == 2))
                    tmp = sbT[:, 512:512 + nb]
                    cp("act", tmp, pcc[:, 0:nb])
                    tt("dve", Bv[:, 4 + m, s["b0"] + o:s["b0"] + o + nb], pbb[:, 0:nb], tmp, ALU.mult)
            for s in seqs:
                if s["sample"]:
                    continue
                for ti in range(s["n"] // 128):
                    t0 = s["t0"] + ti * 128
                    pb = bank("A")
                    for k in range(8):
                        mm(pb[:, 0:256], U[:, k, t0:t0 + 128], wt6[:, k, 256:512], start=(k == 0), stop=(k == 7))
                    st = sbT[:, 1024:1280]
                    cp("act", st, pb[:, 0:256])
                    r0 = (s["idx"] * DEPTH + l) * P_LEN + ti * 128
                    dma("sp", nk_d[r0:r0 + 128, :], st, is_out=True)

            chk(6)
            wt7 = wg(7)

            def zs_view(s, ti):
                cc = s["b0"] + ti * 128
                return Bv[:, 0:4, cc:cc + 128]

            proj_tm(wt7, 0, 512, lambda s, ti, pb: act(zs_view(s, ti), pb.rearrange("p (c t) -> p c t", c=4), AF.Silu))

            DTf = sbT2[:, 1152:1152 + NT * 16]
            act(DTf, DTf, AF.Exp)
            ts("dve", DTf, DTf, 1.0, op0=ALU.add)
            act(DTf, DTf, AF.Ln)
            DAt = sbDER[:, 0:0] if False else None
            DA = sbS[:, 0:0] if False else None
            DAv = sbT2[:, 1408:1408 + NT * 16].rearrange("p (t c) -> p t c", c=16)
            tt("dve", DAv, DTR, ABR.unsqueeze(1).to_broadcast([128, NT, 16]), ALU.mult)

            chk(7)
            Sst = [sbS[:, 0:512], sbS[:, 512:1024]]
            Sbf = [bf(sbS[:, 1024:1280]), bf(sbS[:, 1280:1536])]
            for s in seqs:
                ntile = s["n"] // 128
                for d_ in range(2):
                    if s["sample"]:
                        src = (s0f_d if d_ == 0 else s0b_d)[l]
                        stg = sbT[:, 0:512].rearrange("p (a n) -> p a n", a=4)
                        dma("sp", stg, src.rearrange("(a h2) p n -> (h2 p) a n", h2=2))
                        for a in range(4):
                            pb = bank("C")
                            tp(pb[:, 0:128], stg[:, a, :], identf)
                            cp("act", Sst[d_][:, a * 128:(a + 1) * 128], pb[:, 0:128])
                        cp("dve", Sbf[d_], Sst[d_])
                    else:
                        mset("pool", Sst[d_], 0.0)
                        mset("pool", Sbf[d_], 0.0)
                order = []
                for k_ in range(ntile):
                    order.append((1, ntile - 1 - k_, k_ >= ntile // 2))
                    order.append((0, k_, k_ >= ntile // 2))
                for (d_, ti, comb) in order:
                    par = tdcount[0] % 2
                    tdcount[0] += 1
                    Tq = TSETS[par]
                    gti = s["t0"] // 128 + ti
                    tc0 = s["t0"] + ti * 128
                    da = DAv[:, gti, d_ * 8:(d_ + 1) * 8]
                    dt = DTR[:, gti, d_ * 8:(d_ + 1) * 8]
                    U_, L_, M_ = (C_UF, C_LF, C_MF) if d_ == 0 else (C_UB, C_LB, C_MB)
                    pe_ = bank("S")
                    mm(pe_[:, 0:8], cst(U_), da)
                    mm(pe_[:, 8:16], cst(L_), da)
                    mm(pe_[:, 16:24], cst(C_ONE), da)
                    EX = EXS[par]
                    act(EX, pe_[:, 0:24], AF.Exp)
                    pxs = bank("S")
                    pxsb = pxs.bitcast(BF16)
                    for c4 in range(4):
                        tp(pxsb[:, c4 * 128:(c4 + 1) * 128], XBC[:, c4, tc0:tc0 + 128], identb)
                    xsv = pxsb[:, 0:512].rearrange("p (h e) -> p h e", h=8)
                    XDT = bf(Tq[:, 0:256]).rearrange("p (h e) -> p h e", h=8)
                    XDE = bf(Tq[:, 256:512]).rearrange("p (h e) -> p h e", h=8)
                    tt("dve", XDT, xsv, dt.unsqueeze(2).to_broadcast([128, 8, 64]), ALU.mult)
                    tt("dve", XDE, XDT, EX[:, 8:16].unsqueeze(2).to_broadcast([128, 8, 64]), ALU.mult)
                    if comb:
                        XSD = Tq[:, 512:1024].rearrange("p (h e) -> p h e", h=8)
                        tt("dve", XSD, xsv, sbBR[:, BR_D:BR_D + 8].unsqueeze(2).to_broadcast([128, 8, 64]), ALU.mult)
                    pbt = bank("S")
                    pbtb = pbt.bitcast(BF16)
                    for g in range(2):
                        tp(pbtb[:, g * 128:(g + 1) * 128], XBC[:, 4 + g, tc0:tc0 + 128], identb)
                    BT = bf(Tq[:, 1024:1152])
                    cp("act", BT, pbtb[:, 0:256])
                    pcb = bank("S")
                    for g in range(2):
                        mm(pcb[:, g * 128:(g + 1) * 128], XBC[:, 4 + g, tc0:tc0 + 128], XBC[:, 6 + g, tc0:tc0 + 128])
                    CBM = bf(Tq[:, 1152:1280]).rearrange("p (g i) -> p g i", g=2)
                    tt("dve", CBM, pcb[:, 0:256].rearrange("p (g i) -> p g i", g=2),
                       cstb(M_).unsqueeze(1).to_broadcast([128, 2, 128]), ALU.mult)
                    MT = bf(Tq[:, 1536:2048]).rearrange("p (h i) -> p h i", h=8)
                    for g in range(2):
                        LD = bf(Tq[:, 2048 + g * 256:2048 + (g + 1) * 256]).rearrange("p (h j) -> p h j", h=4)
                        for h4 in range(4):
                            act(LD[:, h4, :], cstb(L_), AF.Copy, scale=da[:, g * 4 + h4:g * 4 + h4 + 1])
                        psg = bank("S")
                        for h4 in range(4):
                            mm(psg[:, h4 * 128:(h4 + 1) * 128], LD[:, h4, :], cstb(U_))
                        EE = bf(Tq[:, 1280:1536]).rearrange("p (h i) -> p h i", h=4)
                        act(EE, psg.rearrange("p (h i) -> p h i", h=4), AF.Exp)
                        tt("dve" if g == 0 else "pool", MT[:, g * 4:(g + 1) * 4, :], EE, CBM[:, g, :].unsqueeze(1).to_broadcast([128, 4, 128]), ALU.mult)
                    if comb:
                        XSDY = bf(Tq[:, 1280:1536])
                        tt("pool", XSDY, Tq[:, 512:1024], YBW[:, gti, :], ALU.add)
                    pyd = bank("S")
                    for h in range(8):
                        mm(pyd[:, h * 64:(h + 1) * 64], MT[:, h, :], XDT[:, h, :], start=(h == 0), stop=(not comb), skip=True)
                    if comb:
                        mm(pyd[:, 0:512], identb, XSDY, start=False, stop=True, skip=True)
                    pyo = bank("S")
                    for g in range(2):
                        mm(pyo[:, g * 256:(g + 1) * 256], XBC[:, 6 + g, tc0:tc0 + 128], Sbf[d_][:, g * 256:(g + 1) * 256])
                    YTMP = Tq[:, 2048:2560].rearrange("p (h e) -> p h e", h=8)
                    tt("dve", YTMP, pyo.rearrange("p (h e) -> p h e", h=8), EX[:, 0:8].unsqueeze(2).to_broadcast([128, 8, 64]), ALU.mult)
                    pst_ = bank("S")
                    for g in range(2):
                        mm(pst_[:, g * 256:(g + 1) * 256], BT[:, g * 128:(g + 1) * 128], bf(Tq[:, 256:512])[:, g * 256:(g + 1) * 256])
                    Sv = Sst[d_].rearrange("p (h e) -> p h e", h=8)
                    tt("pool", Sv, Sv, EX[:, 16:24].unsqueeze(2).to_broadcast([128, 8, 64]), ALU.mult)
                    tt("dve", Sst[d_], Sst[d_], pst_, ALU.add)
                    cp("act", Sbf[d_], Sst[d_])
                    ycol = YBW[:, gti, :]
                    if not comb:
                        tt("dve", ycol, pyd, YTMP.rearrange("p h e -> p (h e)"), ALU.add)
                    else:
                        YS = Tq[:, 2048:2560]
                        tt("dve", YS, pyd, YTMP.rearrange("p h e -> p (h e)"), ALU.add)
                        tt("dve", YS.rearrange("p (c t) -> p c t", c=4), YS.rearrange("p (c t) -> p c t", c=4), zs_view(s, ti), ALU.mult)
                        SSQ = sbQ[:, gti:gti + 1]
                        JK = Tq[:, 1536:2048]
                        mset("pool", SSQ, 0.0)
                        act(JK, YS, AF.Square, accum=SSQ)
                        cp("pool", zs_view(s, ti), YS.rearrange("p (c t) -> p c t", c=4))
                g0 = s["t0"] // 128
                RQ = sbQ[:, g0:g0 + ntile]
                ts("dve", RQ, RQ, 1.0 / 512, op0=ALU.mult)
                rsqrt(RQ, "rms")
                for ti in range(ntile):
                    YN = bf(sbT[:, (ti % 2) * 256:(ti % 2) * 256 + 256]).rearrange("p (c t) -> p c t", c=4)
                    ts("dve", YN, zs_view(s, ti), sbQ[:, g0 + ti:g0 + ti + 1], op0=ALU.mult)
                    pyt = bank("S")
                    pytb = pyt.bitcast(BF16)
                    for c4 in range(4):
                        tp(pytb[:, c4 * 128:(c4 + 1) * 128], YN[:, c4, :], identb)
                    for c4 in range(4):
                        if c4 % 2:
                            ts("dve", zs_view(s, ti)[:, c4, :], pytb[:, c4 * 128:(c4 + 1) * 128], pv(PV_NW + c4), op0=ALU.mult)
                        else:
                            act(zs_view(s, ti)[:, c4, :], pytb[:, c4 * 128:(c4 + 1) * 128], AF.Identity, scale=pv(PV_NW + c4))
                if not s["sample"]:
                    for d_ in range(2):
                        dst = (nsf_d if d_ == 0 else nsb_d)
                        r0 = (s["idx"] * DEPTH + l) * 512
                        for a in range(4):
                            pb = bank("C")
                            tp(pb[:, 0:128], Sst[d_][:, a * 128:(a + 1) * 128], identf)
                            stg = sbT[:, 0:128]
                            cp("act", stg, pb[:, 0:128])
                            dma("sp", dst[r0 + a * 128:r0 + (a + 1) * 128, :], stg, is_out=True)

            chk(8)
            WO = bf(sbU[:, 0:4096]).rearrange("p (k n) -> p k n", k=8) if False else None
            for half in range(2):
                wt = wload(wout_d[l][:, half * 512:(half + 1) * 512], 512)
                for (s, o, nb) in blocks:
                    t0 = s["t0"] + o
                    c = s["cond"]
                    for m in range(4):
                        mo = half * 4 + m
                        pb = bank("A")
                        for ki, k in enumerate((4, 5, 6, 7, 0, 1, 2, 3)):
                            mm(pb[:, 0:nb], wt[:, k, m * 128:(m + 1) * 128], Bv[:, k, s["b0"] + o:s["b0"] + o + nb],
                               start=(ki == 0), stop=(ki == 7))
                        stt("dve", X[:, mo, t0:t0 + nb], pb[:, 0:nb], DER[:, c, 8 + mo:9 + mo], X[:, mo, t0:t0 + nb], ALU.mult, ALU.add)

            def layer_norm(goff, boff, gh, bh, want_h):
                for (t0, nb, c) in dblocks:
                    VB = bf(sbB[:, 4096:4096 + 2048]).rearrange("p (k t) -> p k t", k=8)
                    pm = bank("C")
                    for k in range(8):
                        mm(pm[:, 0:nb], cst(C_ONE), X[:, k, t0:t0 + nb], start=(k == 0), stop=(k == 7))
                    for k in range(8):
                        act(VB[:, k, 0:nb], X[:, k, t0:t0 + nb], AF.Square)
                    pq = bank("C")
                    for k in range(8):
                        mm(pq[:, 0:nb], cstb(C_ONE), VB[:, k, 0:nb], start=(k == 0), stop=(k == 7))
                    lnpar[0] += 1
                    if lnpar[0] % 2:
                        MEAN, RSTD, NMR = sbT[:, 0:nb], sbT[:, 512:512 + nb], sbT[:, 1024:1024 + nb]
                    else:
                        MEAN, RSTD, NMR = sbB[:, 6144:6144 + nb], sbB[:, 6656:6656 + nb], sbB[:, 7168:7168 + nb]
                    act(MEAN, pm[:, 0:nb], AF.Identity, scale=1.0 / D)
                    tt("pool", NMR, MEAN, MEAN, ALU.mult)
                    stt("dve", RSTD, pq[:, 0:nb], 1.0 / D, NMR, ALU.mult, ALU.subtract)
                    rsqrt(RSTD, "ln")
                    stt("dve", NMR, MEAN, -1.0, RSTD, ALU.mult, ALU.mult)
                    for k in range(8):
                        TT = sbT[:, 1536 + (k % 2) * 512:1536 + (k % 2) * 512 + nb]
                        tt("dve", TT, X[:, k, t0:t0 + nb], RSTD, ALU.mult)
                        tt("pool" if k % 8 in (0, 2, 3, 5, 6) else "dve", TT, TT, NMR, ALU.add)
                        if want_h or k % 4 != 3:
                            act(X[:, k, t0:t0 + nb], TT, AF.Identity, bias=pv(boff + k), scale=pv(goff + k))
                        else:
                            ts("dve", X[:, k, t0:t0 + nb], TT, pv(goff + k), pv(boff + k), op0=ALU.mult, op1=ALU.add)
                        if want_h:
                            if k % 4 == 0:
                                act(U[:, k, t0:t0 + nb], TT, AF.Identity, bias=DER[:, c, bh + k:bh + k + 1], scale=DER[:, c, gh + k:gh + k + 1])
                            else:
                                ts("dve", U[:, k, t0:t0 + nb], TT, DER[:, c, gh + k:gh + k + 1], DER[:, c, bh + k:bh + k + 1], op0=ALU.mult, op1=ALU.add)

            chk(9)
            layer_norm(PV_L1G, PV_L1B, 16, 24, True)
            chk(10)

            for q in range(8):
                wu = wload(wup_d[l][:, q * 512:(q + 1) * 512], 512)
                wd = wload(wdn_d[l][q * 512:(q + 1) * 512, :], 1024) if False else None
                WD = bf(sbA[:, (q % 2) * 2048:(q % 2) * 2048 + 2048]).rearrange("p (k n) -> p k n", k=4)
                dma("pool", WD, wdn_d[l][q * 512:(q + 1) * 512, :].rearrange("(k p) n -> p k n", p=128))
                for bi, (t0, nb, c) in enumerate(dblocks):
                    Hd = HID[:, bi % 2]
                    for hc in range(4):
                        pb = bank("A")
                        for k in range(8):
                            mm(pb[:, 0:nb], wu[:, k, hc * 128:(hc + 1) * 128], U[:, k, t0:t0 + nb], start=(k == 0), stop=(k == 7))
                        RL = sbT[:, (hc % 2) * 512:(hc % 2) * 512 + nb]
                        act(RL, pb[:, 0:nb], AF.Relu)
                        tt("pool", Hd[:, hc, 0:nb], RL, RL, ALU.mult)
                    for mo in range(8):
                        pb = bank("A")
                        for hc in range(4):
                            mm(pb[:, 0:nb], WD[:, hc, mo * 128:(mo + 1) * 128], Hd[:, hc, 0:nb], start=(hc == 0), stop=(hc == 3))
                        stt("dve", X[:, mo, t0:t0 + nb], pb[:, 0:nb], DER[:, c, 32 + mo:33 + mo], X[:, mo, t0:t0 + nb], ALU.mult, ALU.add)

            chk(11)
            layer_norm(PV_L2G, PV_L2B, 0, 0, False)
            chk(12)

        stopped = False
        try:
            for l in range(DEPTH):
                _layer(l)
        except _Stop:
            stopped = True
        for ti in range(NT):
            yt = sbT[:, 0:1024]
            for half in range(2):
                pb = bank("A")
                for k in range(4):
                    tp(pb[:, k * 128:(k + 1) * 128], X[:, half * 4 + k, ti * 128:(ti + 1) * 128], identf)
                cp("act" if half else "dve", yt[:, half * 512:(half + 1) * 512], pb)
            dma("sp", y_d[ti * 128:(ti + 1) * 128, :], yt, is_out=True)
        if stopped:
            raise _Stop()

    _PASSOFF[0] = 0
    try:
        pseqs = [dict(n=P_LEN, cond=0, sample=False, idx=i) for i in range(NP_SEQ)]
        run_pass(pseqs, NP_SEQ * P_LEN, xp_d, yp_d)
        sseqs = [dict(n=S_LEN, cond=1, sample=True, idx=0)]
        _PASSOFF[0] = 100
        run_pass(sseqs, S_LEN, xs_d, ys_d)
    except _Stop:
        pass

    P.add("sp", lambda e: e.nop(), extra_deps=out_dmas, cost=50.0)
    P.schedule(window=1024)
    P.plan()
    sems = {}
    keys = [("eng", e) for e in ENGS] + [("dma", e, s) for e in ("sp", "pool") for s in range(NDMASEM)]
    for k in keys:
        sems[k] = es.enter_context(nc.semaphore("_".join(map(str, k))))
    block = es.enter_context(nc.Block())

    @block.tensor
    def _(e):
        P.emit_engine("pe", e, sems)

    @block.scalar
    def _(e):
        P.emit_engine("act", e, sems)

    @block.vector
    def _(e):
        P.emit_engine("dve", e, sems)

    @block.gpsimd
    def _(e):
        P.emit_engine("pool", e, sems)

    @block.sync
    def _(e):
        P.emit_engine("sp", e, sems)

    es.close()
    return nc, P


def host_prep(inp):
    f = lambda a: np.ascontiguousarray(np.asarray(a, dtype=np.float32))
    cols = _win_groups()
    w_in_r = f(np.asarray(inp["w_in"])[:, :, cols])
    L = DEPTH
    pvec = np.zeros((L, 128, NPV), np.float32)
    brow = np.zeros((L, NBR), np.float32)
    for l in range(L):
        pvec[l, :, PV_BMOD:PV_BMOD + 48] = _fm(inp["b_mod"][l])
        cw = np.asarray(inp["ssd_conv_w"][l])
        for k in range(5):
            pvec[l, :, PV_CW + k * 8:PV_CW + k * 8 + 8] = _fm(cw[k])
        pvec[l, :, PV_CB:PV_CB + 8] = _fm(inp["ssd_conv_b"][l])
        sw = np.asarray(inp["sc_conv_w"][l])
        for k in range(3):
            pvec[l, :, PV_SCW + k * 2:PV_SCW + k * 2 + 2] = _fm(sw[k])
        pvec[l, :, PV_L1G:PV_L1G + 8] = _fm(inp["ln1_g"][l])
        pvec[l, :, PV_L1B:PV_L1B + 8] = _fm(inp["ln1_b"][l])
        pvec[l, :, PV_L2G:PV_L2G + 8] = _fm(inp["ln2_g"][l])
        pvec[l, :, PV_L2B:PV_L2B + 8] = _fm(inp["ln2_b"][l])
        brow[l, BR_DTB:BR_DTB + 16] = np.asarray(inp["ssd_dt_bias"][l]).reshape(16)
        brow[l, BR_ALOG:BR_ALOG + 16] = np.asarray(inp["ssd_a_log"][l]).reshape(16)
        brow[l, BR_D:BR_D + 8] = np.asarray(inp["ssd_d"][l])
        pvec[l, :, PV_NW:PV_NW + 4] = _fm(inp["ssd_norm_w"][l])
        brow[l, BR_DNW:BR_DNW + 64] = np.asarray(inp["diff_norm_w"][l])
        brow[l, BR_LAM:BR_LAM + 128] = np.asarray(inp["diff_lambda"][l]).reshape(128)
    cos, sin = _rope_tables()
    shared = {
        "w_mod": f(inp["w_mod"]), "w_in_r": w_in_r, "w_out": f(inp["w_out"]), "w_up": f(inp["w_up"]),
        "w_down": f(inp["w_down"]), "pvec": pvec, "brow": brow, "consts": _consts(),
        "ropecos": f(cos), "ropesin": f(sin),
    }
    return shared


def core_inputs(inp, i, shared):
    f = lambda a: np.ascontiguousarray(np.asarray(a, dtype=np.float32))
    m = dict(shared)
    m["xp"] = f(np.asarray(inp["x_prompt"])[2 * i:2 * i + 2].reshape(NP_SEQ * P_LEN, D))
    m["xs"] = f(np.asarray(inp["x_sample"])[i])
    m["ck"] = f(np.asarray(inp["cache_k"])[i].reshape(DEPTH, PAST, 256))
    m["cv"] = f(np.asarray(inp["cache_v"])[i].reshape(DEPTH, PAST, 256))
    m["s0f"] = f(np.asarray(inp["state_ssm_fwd"])[i])
    m["s0b"] = f(np.asarray(inp["state_ssm_bwd"])[i])
    cv = np.stack([np.asarray(inp["c_ctx"]), np.asarray(inp["c"])[i]], axis=-1)
    m["cvec"] = f(cv.reshape(8, 128, 2).transpose(1, 0, 2).reshape(128, 16))
    return m


_CACHE = {}


def kernel(**inputs):
    if "nc" not in _CACHE:
        _CACHE["nc"] = build()[0]
    nc = _CACHE["nc"]
    shared = host_prep(inputs)
    in_maps = [core_inputs(inputs, i, shared) for i in range(8)]
    res = run_bass_kernel_spmd(nc, in_maps, core_ids=list(range(8)))
    r = res.results
    y_p = np.concatenate([r[i]["yp"].reshape(NP_SEQ, P_LEN, D) for i in range(8)], 0)
    y_s = np.stack([r[i]["ys"] for i in range(8)], 0)
    nk = np.concatenate([r[i]["nk"].reshape(NP_SEQ, DEPTH, P_LEN, 8, 32) for i in range(8)], 0)
    nv = np.concatenate([r[i]["nv"].reshape(NP_SEQ, DEPTH, P_LEN, 4, 64) for i in range(8)], 0)
    nsf = np.concatenate([r[i]["nsf"].reshape(NP_SEQ, DEPTH, 8, 64, 128) for i in range(8)], 0)
    nsb = np.concatenate([r[i]["nsb"].reshape(NP_SEQ, DEPTH, 8, 64, 128) for i in range(8)], 0)
    return (y_p.astype(np.float32), y_s.astype(np.float32), nk.astype(np.float32), nv.astype(np.float32),
            nsf.astype(np.float32), nsb.astype(np.float32))
```

```python
import math
from contextlib import ExitStack

import numpy as np
import concourse.bass as bass
import concourse.mybir as mybir
from concourse.bass_utils import run_bass_kernel_spmd

F32 = mybir.dt.float32
BF16 = mybir.dt.bfloat16
AF = mybir.ActivationFunctionType
ALU = mybir.AluOpType
AX = mybir.AxisListType

ENGS = ["pe", "act", "dve", "pool", "sp"]
NDMASEM = 24

D = 1024
DEPTH = 4
NP_SEQ, P_LEN = 2, 256
S_LEN = 2048
PAST = 256
IN_COLS = 3088
ALPHA = (2 * DEPTH) ** 0.25
LN_EPS = 1e-5
RMS_EPS = 1e-6
GRID_W = 64


def _dsize(dt):
    return 2 if dt == BF16 else 4


def region(ap):
    name = ap.tensor.name
    isps = name.startswith("ps")
    if not (isps or name.startswith("sb")):
        return None
    pat = list(ap.ap)
    sz = _dsize(ap.dtype)
    pstep = pat[0][0]
    pcnt = pat[0][1]
    p0 = ap.offset // pstep
    e0 = ap.offset % pstep
    span = 1
    for st, cnt in pat[1:]:
        span += (cnt - 1) * abs(st)
    b0, b1 = e0 * sz, (e0 + span) * sz
    if isps:
        return (name, 0, 128, (b0 // 2048) * 2048, ((b1 - 1) // 2048 + 1) * 2048)
    return (name, p0, p0 + pcnt, b0, b1)


def _ov(a, b):
    return a[1] < b[2] and b[1] < a[2] and a[3] < b[4] and b[3] < a[4]


def _cov(a, b):
    return a[1] <= b[1] and a[2] >= b[2] and a[3] <= b[3] and a[4] >= b[4]


class Prog:
    def __init__(self):
        self.ops = []
        self.writes = {}
        self.reads = {}
        self.eng_ops = {e: [] for e in ENGS}

    def add(self, eng, fn, reads=(), writes=(), dma=False, extra_deps=(), cost=300.0, lat=0.0):
        oid = len(self.ops)
        deps = set(extra_deps)
        odeps = set()
        rr = list(dict.fromkeys(r for r in (region(a) for a in reads) if r is not None))
        ww = list(dict.fromkeys(r for r in (region(a) for a in writes) if r is not None))
        ww = list(dict.fromkeys(ww + [r for r in rr if r[0].startswith("ps")]))
        for r in rr:
            for (w, wid) in self.writes.get(r[0], ()):
                if _ov(r, w):
                    deps.add(wid)
        for w in ww:
            for (w2, wid) in self.writes.get(w[0], ()):
                if _ov(w, w2):
                    deps.add(wid)
            for (r2, rid) in self.reads.get(w[0], ()):
                if _ov(w, r2):
                    deps.add(rid)
        ops = self.ops
        for w in ww:
            lst = self.writes.setdefault(w[0], [])
            lst[:] = [x for x in lst if not _cov(w, x[0])]
            lst.append((w, oid))
            rl = self.reads.setdefault(w[0], [])
            rl[:] = [x for x in rl if not _cov(w, x[0])]
        for r in rr:
            rl = self.reads.setdefault(r[0], [])
            if not dma:
                keep = []
                for x in rl:
                    if x[0] == r and ops[x[1]][0] == eng and not ops[x[1]][3]:
                        odeps.add(x[1])
                    else:
                        keep.append(x)
                rl[:] = keep
            rl.append((r, oid))
        deps.discard(oid)
        self.ops.append([eng, fn, deps, dma, odeps, cost, lat, _TAG[0]])
        self.eng_ops[eng].append(oid)
        return oid

    def schedule(self, window=48):
        ops = self.ops
        n = len(ops)
        succ = [[] for _ in range(n)]
        npred = [0] * n
        for oid in range(n):
            ds = ops[oid][2] | ops[oid][4]
            npred[oid] = len(ds)
            for d in ds:
                succ[d].append(oid)
        finish = [0.0] * n
        self.start_t = [0.0] * n
        ready_t = [0.0] * n
        done = [False] * n
        pend = {e: list(self.eng_ops[e]) for e in ENGS}
        head = {e: 0 for e in ENGS}
        free_t = {e: 0.0 for e in ENGS}
        new_order = {e: [] for e in ENGS}
        order = []
        best = {e: None for e in ENGS}
        dirty = set(ENGS)
        remaining = n
        while remaining:
            for e in list(dirty):
                lst = pend[e]
                h = head[e]
                while h < len(lst) and done[lst[h]]:
                    h += 1
                head[e] = h
                b = None
                cnt = 0
                i = h
                ft = free_t[e]
                while i < len(lst) and cnt < window:
                    o = lst[i]
                    i += 1
                    if done[o]:
                        continue
                    cnt += 1
                    if npred[o] == 0:
                        st = ready_t[o] if ready_t[o] > ft else ft
                        if b is None or st < b[0]:
                            b = (st, o)
                            if st <= ft:
                                break
                best[e] = b
            dirty.clear()
            pick = None
            for e in ENGS:
                b = best[e]
                if b is not None and (pick is None or b[0] < pick[0]):
                    pick = (b[0], b[1], e)
            assert pick is not None, "scheduler deadlock"
            st, o, e = pick
            eng, fn, deps, dma, odeps, cost, lat = ops[o][:7]
            self.start_t[o] = st
            free_t[e] = st + cost
            finish[o] = st + cost + lat
            done[o] = True
            remaining -= 1
            new_order[e].append(o)
            order.append(o)
            dirty.add(e)
            for s2 in succ[o]:
                npred[s2] -= 1
                if finish[o] > ready_t[s2]:
                    ready_t[s2] = finish[o]
                dirty.add(ops[s2][0])
        self.eng_ops = new_order
        self.sched_order = order
        self.est_time = max(finish) if n else 0.0

    def plan(self):
        ops = self.ops
        n = len(ops)
        order = getattr(self, "sched_order", None) or list(range(n))
        pos = {}
        for e in ENGS:
            for i, oid in enumerate(self.eng_ops[e]):
                pos[oid] = i
        gpos = {o: i for i, o in enumerate(order)}
        clock = {e: {e2: -1 for e2 in ENGS} for e in ENGS}
        snap = {}
        dma_known = {e: set() for e in ENGS}
        waits = {}
        marked = set()
        for oid in order:
            eng, fn, deps, dma = ops[oid][:4]
            w = []
            ck = clock[eng]
            for d in sorted(deps, key=lambda x: gpos[x], reverse=True):
                de, ddma = ops[d][0], ops[d][3]
                if ddma:
                    if d in dma_known[eng]:
                        continue
                    dma_known[eng].add(d)
                    w.append(d)
                    marked.add(d)
                else:
                    if de == "pe" and eng == "pe":
                        continue
                    p = pos[d]
                    if ck[de] >= p:
                        continue
                    w.append(d)
                    marked.add(d)
                    ck[de] = p
                s = snap.get(d)
                if s is not None:
                    for e2 in ENGS:
                        if s[e2] > ck[e2]:
                            ck[e2] = s[e2]
            waits[oid] = w
            snap[oid] = tuple(ck[e2] for e2 in ENGS)
            snap[oid] = dict(zip(ENGS, snap[oid]))
        cnt = {e: 0 for e in ENGS}
        semval = {}
        dma_n = {e: 0 for e in ENGS}
        dma_prev = {}
        dma_hist = {e: [] for e in ENGS}
        for oid in order:
            eng, fn, deps, dma = ops[oid][:4]
            if dma:
                k = dma_n[eng]
                dma_n[eng] += 1
                semval[oid] = (("dma", eng, k % NDMASEM), 16 * (k // NDMASEM + 1))
                if k >= NDMASEM:
                    dma_prev[oid] = dma_hist[eng][k - NDMASEM]
                dma_hist[eng].append(oid)
            elif oid in marked:
                cnt[eng] += 1
                semval[oid] = (("eng", eng), cnt[eng])
        self.sem_counts = cnt
        self._plan = (waits, semval, dma_prev)

    def emit_engine(self, ename, eng, sems):
        waits, semval, dma_prev = self._plan
        ops = self.ops
        for oid in self.eng_ops[ename]:
            e, fn, deps, dma = ops[oid][:4]
            if dma and oid in dma_prev:
                pk, pv = semval[dma_prev[oid]]
                eng.wait_ge(sems[pk], pv)
            for d in waits[oid]:
                k, v = semval[d]
                eng.wait_ge(sems[k], v)
            ins = fn(eng)
            if oid in semval:
                k, v = semval[oid]
                ins.then_inc(sems[k], 16 if dma else 1)


Z0, XBC0, DT0, SCB0, SCC0, SCH0, Q0, K0, V0 = 0, 512, 1536, 1552, 1808, 2064, 2320, 2576, 2832


def _rope_perm():
    idx = np.arange(256).reshape(8, 2, 2, 8)
    return idx[:, :, ::-1, :].reshape(256)


def _win_groups():
    perm = _rope_perm()
    g = []
    g.append(np.arange(XBC0, XBC0 + 512))
    g.append(np.arange(XBC0 + 512, XBC0 + 1024))
    g.append(np.concatenate([Q0 + np.arange(256), Q0 + perm]))
    g.append(np.concatenate([K0 + np.arange(256), K0 + perm]))
    g.append(np.concatenate([V0 + np.arange(256), DT0 + np.arange(16), K0 + np.arange(240)]))
    g.append(np.concatenate([SCC0 + np.arange(256), SCH0 + np.arange(256)]))
    g.append(np.concatenate([SCB0 + np.arange(256), K0 + np.arange(256)]))
    g.append(np.arange(Z0, Z0 + 512))
    return np.concatenate(g)


NG = 8
C_ID, C_UF, C_LF, C_UB, C_LB, C_ONE, C_HM, C_MF, C_MB = 0, 128, 256, 384, 512, 640, 768, 772, 900
NCONST = 1028


def _consts():
    c = np.zeros((128, NCONST), np.float32)
    t = np.arange(128)[:, None]
    i = np.arange(128)[None, :]
    c[:, C_ID:C_ID + 128] = (t == i)
    c[:, C_UF:C_UF + 128] = (t <= i)
    c[:, C_LF:C_LF + 128] = (t > i)
    c[:, C_UB:C_UB + 128] = (t >= i)
    c[:, C_LB:C_LB + 128] = (t < i)
    c[:, C_ONE:C_ONE + 128] = 1.0
    for h in range(4):
        c[32 * h:32 * h + 32, C_HM + h] = 1.0
    c[:, C_MF:C_MF + 128] = (i >= t)
    c[:, C_MB:C_MB + 128] = (i <= t)
    return c


def _rope_tables():
    p = np.arange(128)
    d = p % 32
    axis, half, f = d // 16, (d % 16) // 8, d % 8
    inv = 10000.0 ** (-np.arange(8, dtype=np.float32) / 8)
    tt = np.arange(S_LEN)
    row = (tt // GRID_W).astype(np.float32)
    col = (tt % GRID_W).astype(np.float32)
    posv = np.where(axis[:, None] == 0, row[None, :], col[None, :]).astype(np.float32)
    ang = (posv * inv[f][:, None].astype(np.float32)).astype(np.float32)
    cos = np.cos(ang).astype(np.float32)
    sin = np.sin(ang).astype(np.float32)
    sgn = np.where(half == 0, -1.0, 1.0).astype(np.float32)[:, None]
    return cos, (sin * sgn).astype(np.float32)


PV_BMOD, PV_CW, PV_CB, PV_SCW, PV_L1G, PV_L1B, PV_L2G, PV_L2B, PV_NW = 0, 48, 88, 96, 102, 110, 118, 126, 134
NPV = 138
BR_DTB, BR_ALOG, BR_D, BR_DNW, BR_LAM = 0, 16, 32, 40, 104
NBR = 232


def _fm(v):
    v = np.asarray(v)
    return v.reshape(-1, 128).T


class _Stop(Exception):
    pass


_LIMIT = [10 ** 9]
_SKIP = set()


_PASSOFF = [0]
_TAG = [0]


def chk(n):
    _TAG[0] = n + _PASSOFF[0]
    if n + _PASSOFF[0] > _LIMIT[0]:
        raise _Stop()


def build():
    nc = bass.Bass("TRN2", target_bir_lowering=False)
    P = Prog()
    es = ExitStack()
    dr = lambda name, shape, kind="ExternalInput": nc.dram_tensor(name, list(shape), F32, kind=kind).ap()
    xp_d = dr("xp", [NP_SEQ * P_LEN, D])
    xs_d = dr("xs", [S_LEN, D])
    ck_d = dr("ck", [DEPTH, PAST, 256])
    cv_d = dr("cv", [DEPTH, PAST, 256])
    s0f_d = dr("s0f", [DEPTH, 8, 64, 128])
    s0b_d = dr("s0b", [DEPTH, 8, 64, 128])
    cvec_d = dr("cvec", [128, 16])
    wmod_d = dr("w_mod", [DEPTH, D, 6 * D])
    win_d = dr("w_in_r", [DEPTH, D, NG * 512])
    wout_d = dr("w_out", [DEPTH, D, D])
    wup_d = dr("w_up", [DEPTH, D, 4 * D])
    wdn_d = dr("w_down", [DEPTH, 4 * D, D])
    pvec_d = dr("pvec", [DEPTH, 128, NPV])
    brow_d = dr("brow", [DEPTH, NBR])
    const_d = dr("consts", [128, NCONST])
    cos_d = dr("ropecos", [128, S_LEN])
    sin_d = dr("ropesin", [128, S_LEN])
    yp_d = dr("yp", [NP_SEQ * P_LEN, D], "ExternalOutput")
    ys_d = dr("ys", [S_LEN, D], "ExternalOutput")
    nk_d = dr("nk", [NP_SEQ * DEPTH * P_LEN, 256], "ExternalOutput")
    nv_d = dr("nv", [NP_SEQ * DEPTH * P_LEN, 256], "ExternalOutput")
    nsf_d = dr("nsf", [NP_SEQ * DEPTH * 512, 128], "ExternalOutput")
    nsb_d = dr("nsb", [NP_SEQ * DEPTH * 512, 128], "ExternalOutput")
    out_dmas = []

    sb = lambda name, cols: es.enter_context(nc.sbuf_tensor(name, [128, cols], F32))
    TMAX = S_LEN
    sbX = sb("sbX", 8 * TMAX)
    sbU = sb("sbU", 4 * TMAX)
    sbA = sb("sbA", 4 * TMAX)
    sbB = sb("sbB", 4 * (TMAX + 8))
    sbW = sb("sbW", 4096)
    sbC = sb("sbC", C_MF)
    sbCb = sb("sbCb", NCONST // 2)
    sbPV = sb("sbPV", NPV)
    sbBR = sb("sbBR", NBR)
    sbMOD = sb("sbMOD", DEPTH * 2 * 48)
    sbDER = sb("sbDER", 2 * 48 + 16)
    sbT = sb("sbT", 2560)
    sbT2 = sb("sbT2", 1664)
    sbS = sb("sbS", 2 * 512 + 2 * 256)
    ps = es.enter_context(nc.psum_tensor("ps", [128, 8, 512], F32))

    cst = lambda off, n=128: sbC[:, off:off + n]
    cstb = lambda off, n=128: sbCb[:, :].bitcast(BF16)[:, off:off + n]

    def fsz(ap):
        return float(ap.free_size())

    def ecost(eng, out):
        f = fsz(out)
        if eng == "act":
            return 220.0 + f / 1.4
        if eng == "dve":
            return 100.0 + f / 0.96
        return 300.0 + 2.0 * f

    def mm(out, lhsT, rhs, start=True, stop=True, skip=False, tpos=None):
        passes = 4.0 if lhsT.dtype == F32 else 1.0
        c = 30.0 + fsz(rhs) * passes / 2.0
        if tpos is not None:
            c = c / 2.0 + 10.0
            fn = lambda e: e.matmul(out, lhsT, rhs, start=start, stop=stop, tile_position=tpos)
        elif skip:
            fn = lambda e: e.matmul(out, lhsT, rhs, start=start, stop=stop, skip_group_check=True)
        else:
            fn = lambda e: e.matmul(out, lhsT, rhs, start=start, stop=stop)
        return P.add("pe", fn, reads=[lhsT, rhs], writes=[out], cost=c)

    def tp(out, in_, ident):
        c = 40.0 + 128.0 * (4.0 if in_.dtype == F32 else 1.0) / 2.0
        return P.add("pe", lambda e: e.transpose(out, in_, ident), reads=[in_, ident], writes=[out], cost=c)

    def act(out, in_, func, bias=None, scale=None, accum=None):
        kw = {}
        rd = [in_]
        if bias is not None:
            kw["bias"] = bias
            if not isinstance(bias, float):
                rd.append(bias)
        if scale is not None:
            kw["scale"] = scale
            if not isinstance(scale, float):
                rd.append(scale)
        wr = [out]
        if accum is not None:
            kw["accum_out"] = accum
            wr.append(accum)
        return P.add("act", lambda e: e.activation(out=out, in_=in_, func=func, **kw), reads=rd, writes=wr, cost=ecost("act", out))

    def tt(eng, out, in0, in1, op):
        return P.add(eng, lambda e: e.tensor_tensor(out=out, in0=in0, in1=in1, op=op), reads=[in0, in1], writes=[out], cost=ecost(eng, out))

    def ts(eng, out, in0, s1, s2=None, op0=ALU.mult, op1=None):
        rd = [in0] + [s for s in (s1, s2) if s is not None and not isinstance(s, float)]
        if op1 is None:
            fn = lambda e: e.tensor_scalar(out=out, in0=in0, scalar1=s1, scalar2=None, op0=op0)
        else:
            fn = lambda e: e.tensor_scalar(out=out, in0=in0, scalar1=s1, scalar2=s2, op0=op0, op1=op1)
        return P.add(eng, fn, reads=rd, writes=[out], cost=ecost(eng, out))

    def stt(eng, out, in0, scalar, in1, op0, op1):
        rd = [in0, in1] + ([] if isinstance(scalar, float) else [scalar])
        return P.add(eng, lambda e: e.scalar_tensor_tensor(out=out, in0=in0, scalar=scalar, in1=in1, op0=op0, op1=op1),
                     reads=rd, writes=[out], cost=ecost(eng, out))

    def cp(eng, out, in_):
        if eng == "act":
            return act(out, in_, AF.Copy)
        return P.add(eng, lambda e: e.tensor_copy(out=out, in_=in_), reads=[in_], writes=[out], cost=ecost(eng, out))

    def mset(eng, out, val):
        return P.add(eng, lambda e: e.memset(out, val), writes=[out], cost=150.0)

    def dma(q, out, in_, is_out=False):
        nb_ = fsz(out) * out.partition_size() * (2 if out.dtype == BF16 else 4)
        if q == "pool":
            o = P.add(q, lambda e: e.dma_start(out=out, in_=in_), reads=[in_], writes=[out], dma=True, cost=1000.0, lat=3000.0 + nb_ / 100.0)
        else:
            o = P.add(q, lambda e: e.dma_start(out=out, in_=in_), reads=[in_], writes=[out], dma=True, cost=100.0, lat=2500.0 + nb_ / 150.0)
        if is_out:
            out_dmas.append(o)
        return o

    def recip(out, in_):
        return P.add("dve", lambda e: e.reciprocal(out=out, in_=in_), reads=[in_], writes=[out], cost=ecost("dve", out))

    def rsqrt(buf, eps):
        act(buf, buf, AF.Sqrt, bias=cst_eps[eps], scale=1.0)
        recip(buf, buf)

    bank_rr = {"A": [0, [0, 1, 2, 3]], "B": [0, [4, 5]], "C": [0, [6, 7]], "S": [0, [0, 1, 2, 3, 4, 5, 6, 7]]}
    tdcount = [0]
    pairc = [0]
    lnpar = [0]

    def bank(cls):
        st = bank_rr[cls]
        b = st[1][st[0] % len(st[1])]
        st[0] += 1
        return ps[:, b, :]

    def bf(ap):
        return ap.bitcast(BF16)

    dma("sp", sbC[:, :], const_d[:, 0:C_MF])
    dma("pool", sbCb[:, :].bitcast(BF16), const_d)
    identb = cstb(C_ID)
    identf = cst(C_ID)
    sbE = sb("sbE", 4)
    sbQ = sb("sbQ", 32)
    mset("pool", sbE[:, 0:1], LN_EPS / (ALPHA * ALPHA))
    mset("pool", sbE[:, 1:2], RMS_EPS)
    cst_eps = {"ln": sbE[:, 0:1], "rms": sbE[:, 1:2]}

    sbCS = sb("sbCS", 24 + 48)
    cvb = sbCS[:, 0:16]
    dma("sp", cvb, cvec_d)
    csb = bf(sbCS[:, 16:24])
    act(csb, cvb, AF.Silu)
    sbBM = sbCS[:, 24:72]
    MOD = sbMOD[:, :].rearrange("p (l c w) -> p l c w", l=DEPTH, c=2)
    wslot = [bf(sbW[:, 0:2048]).rearrange("p (k n) -> p k n", k=8), bf(sbW[:, 2048:4096]).rearrange("p (k n) -> p k n", k=8)]
    wcnt = [0]

    def wload(src2d, ncols):
        s = wslot[wcnt[0] % 2]
        wcnt[0] += 1
        dma("pool", s[:, :, 0:ncols], src2d.rearrange("(k p) n -> p k n", p=128))
        return s

    mslot = [bf(sbX[:, 4096 + i * 2048:4096 + (i + 1) * 2048]).rearrange("p (k n) -> p k n", k=8) for i in range(6)]
    mcnt = [0]

    def emit_mod(l):
        dma("sp", sbBM, pvec_d[l][:, PV_BMOD:PV_BMOD + 48])
        for g in range(12):
            wt = mslot[mcnt[0] % 6]
            mcnt[0] += 1
            dma("pool", wt, wmod_d[l][:, g * 512:(g + 1) * 512].rearrange("(k p) n -> p k n", p=128))
            pb = bank("A")
            for m in range(4):
                for k in range(8):
                    mm(pb[:, m * 2:m * 2 + 2], wt[:, k, m * 128:(m + 1) * 128], csb[:, k * 2:k * 2 + 2],
                       start=(k == 0), stop=(k == 7))
            for m in range(4):
                oc = g * 4 + m
                for c in range(2):
                    ts("dve", MOD[:, l, c, oc:oc + 1], pb[:, m * 2 + c:m * 2 + c + 1], sbBM[:, oc:oc + 1],
                       op0=ALU.add)

    emit_mod(0)

    def run_pass(seqs, T, x_d, y_d):
        chk(1)
        NT = T // 128
        X = sbX[:, 0:8 * T].rearrange("p (k t) -> p k t", k=8)
        U = bf(sbU[:, 0:4 * T]).rearrange("p (k t) -> p k t", k=8)
        XBC = bf(sbA[:, 0:4 * T]).rearrange("p (k t) -> p k t", k=8)
        TB = T + 4 * len(seqs)
        Bv = bf(sbB[:, 0:4 * TB]).rearrange("p (k t) -> p k t", k=8)
        for si, s in enumerate(seqs):
            s["t0"] = sum(q["n"] for q in seqs[:si])
            s["b0"] = s["t0"] + 4 * si + 2
            s["nk"] = s["n"] + (PAST if s["sample"] else 0)
        blocks = []
        for s in seqs:
            for o in range(0, s["n"], 512):
                blocks.append((s, o, min(512, s["n"] - o)))
        dblocks = []
        for (s_, o_, nb_) in blocks:
            t0_ = s_["t0"] + o_
            if dblocks and dblocks[-1][2] == s_["cond"] and dblocks[-1][0] + dblocks[-1][1] == t0_ and dblocks[-1][1] + nb_ <= 512:
                dblocks[-1] = (dblocks[-1][0], dblocks[-1][1] + nb_, s_["cond"])
            else:
                dblocks.append((t0_, nb_, s_["cond"]))
        NKMAX = max(s["nk"] for s in seqs)
        NKMAX = max(s["nk"] for s in seqs)
        Kf = bf(sbA[:, 0:NKMAX]).rearrange("p (c t) -> p c t", c=2)
        vofs = NKMAX
        Vt = bf(sbA[:, vofs:vofs + (NKMAX // 128) * 132]).rearrange("p (t h e) -> p t h e", h=4, e=66)
        AT0 = 4680
        HID = bf(sbB[:, 0:4096]).rearrange("p (b k t) -> p b k t", b=2, k=8)
        YBW = bf(sbU[:, 0:NT * 256]).rearrange("p (t c) -> p t c", c=512)
        TSETS = [sbT, sbU[:, 4096:6656]]
        EXS = [sbT2[:, 96:120], sbU[:, 6656:6680]]
        SSQS = [sbT2[:, 120:121], sbU[:, 6680:6681]]

        for ti in range(NT):
            xt = sbT[:, 0:1024]
            dma("sp", xt, x_d[ti * 128:(ti + 1) * 128, :])
            for half in range(2):
                pb = bank("A")
                for k in range(4):
                    tp(pb[:, k * 128:(k + 1) * 128], xt[:, (half * 4 + k) * 128:(half * 4 + k + 1) * 128], identf)
                cp("act" if half else "dve", X[:, half * 4:half * 4 + 4, ti * 128:(ti + 1) * 128],
                   pb.rearrange("p (k t) -> p k t", k=4))

        chk(2)

        def _layer(l):
            lam_init = 0.8 - 0.6 * math.exp(-0.3 * l)
            dma("sp", sbPV[:, :], pvec_d[l])
            dma("sp", sbBR[:, :], brow_d[l].partition_broadcast(128))
            pv = lambda off, n=1: sbPV[:, off:off + n]
            DER = sbDER[:, 0:96].rearrange("p (c w) -> p c w", c=2)
            for c in range(2):
                ts("pool", DER[:, c, 0:8], MOD[:, l, c, 8:16], 1.0, op0=ALU.add)
                ts("pool", DER[:, c, 8:16], MOD[:, l, c, 16:24], 1.0 / ALPHA, op0=ALU.mult)
                ts("pool", DER[:, c, 40:48], MOD[:, l, c, 32:40], 1.0, op0=ALU.add)
                tt("pool", DER[:, c, 16:24], DER[:, c, 40:48], pv(PV_L1G, 8), ALU.mult)
                tt("pool", DER[:, c, 24:32], DER[:, c, 40:48], pv(PV_L1B, 8), ALU.mult)
                tt("pool", DER[:, c, 24:32], DER[:, c, 24:32], MOD[:, l, c, 24:32], ALU.add)
                ts("pool", DER[:, c, 32:40], MOD[:, l, c, 40:48], 1.0 / ALPHA, op0=ALU.mult)
            LT = sbDER[:, 96:112]
            lp = sbBR[:, BR_LAM:BR_LAM + 128].rearrange("p (a d) -> p a d", a=4)
            prod = sbT2[:, 0:64].rearrange("p (a d) -> p a d", a=2)
            tt("dve", prod[:, 0, :], lp[:, 0, :], lp[:, 1, :], ALU.mult)
            tt("dve", prod[:, 1, :], lp[:, 2, :], lp[:, 3, :], ALU.mult)
            P.add("dve", lambda e: e.reduce_sum(out=LT[:, 0:2], in_=prod, axis=AX.X), reads=[prod], writes=[LT[:, 0:2]])
            act(LT[:, 2:4], LT[:, 0:2], AF.Exp)
            tt("dve", LT[:, 4:5], LT[:, 3:4], LT[:, 2:3], ALU.subtract)
            ts("dve", LT[:, 5:6], LT[:, 4:5], -lam_init, op0=ALU.add)
            NLAM = LT[:, 5:6]
            Abc = sbDER[:, 112:112] if False else None
            ABR = sbT2[:, 64:80]
            act(ABR, sbBR[:, BR_ALOG:BR_ALOG + 16], AF.Exp)
            ts("dve", ABR, ABR, -1.0, op0=ALU.mult)

            for (t0, nb, c) in dblocks:
                for k in range(8):
                    if k % 2:
                        ts("dve", U[:, k, t0:t0 + nb], X[:, k, t0:t0 + nb], DER[:, c, k:k + 1], MOD[:, l, c, k:k + 1], op0=ALU.mult, op1=ALU.add)
                    else:
                        act(U[:, k, t0:t0 + nb], X[:, k, t0:t0 + nb], AF.Identity, bias=MOD[:, l, c, k:k + 1], scale=DER[:, c, k:k + 1])

            chk(3)
            if (not seqs[0]["sample"]) and l + 1 < DEPTH:
                _t = _TAG[0]
                _TAG[0] = 0
                emit_mod(l + 1)
                _TAG[0] = _t
            wg = lambda g: wload(win_d[l][:, g * 512:(g + 1) * 512], 512)

            def proj_fm(wt, ncol0, nch, evac):
                for (s, o, nb) in blocks:
                    t0 = s["t0"] + o
                    for m in range(nch):
                        pb = bank("A")
                        for k in range(8):
                            mm(pb[:, 0:nb], wt[:, k, ncol0 + m * 128:ncol0 + (m + 1) * 128], U[:, k, t0:t0 + nb],
                               start=(k == 0), stop=(k == 7))
                        evac(s, o, nb, m, pb[:, 0:nb])

            def proj_tm(wt, ncol0, ncols, evac):
                for s in seqs:
                    for ti in range(s["n"] // 128):
                        t0 = s["t0"] + ti * 128
                        pb = bank("A")
                        for k in range(8):
                            mm(pb[:, 0:ncols], U[:, k, t0:t0 + 128], wt[:, k, ncol0:ncol0 + ncols], start=(k == 0), stop=(k == 7))
                        evac(s, ti, pb[:, 0:ncols])

            DTR = sbT2[:, 1152:1152 + NT * 16].rearrange("p (t c) -> p t c", c=16)
            for s in seqs:
                wt4 = wg(4)
                koff = PAST // 128 if s["sample"] else 0
                if s["sample"]:
                    for kt in range(2):
                        kst = bf(sbT[:, 2048:2176])
                        dma("pool", kst, ck_d[l, kt * 128:(kt + 1) * 128, :])
                        pb = bank("C")
                        pbb = pb.bitcast(BF16)
                        for c2 in range(2):
                            tp(pbb[:, c2 * 128:(c2 + 1) * 128], kst[:, c2 * 128:(c2 + 1) * 128], identb)
                        cp("act", Kf[:, :, kt * 128:(kt + 1) * 128], pbb[:, 0:256].rearrange("p (c t) -> p c t", c=2))
                        dma("pool", Vt[:, kt, :, 0:64], cv_d[l, kt * 128:(kt + 1) * 128, :].rearrange("t (h e) -> t h e", h=4))
                for ti in range(s["nk"] // 128):
                    if "A" not in _SKIP:
                        mset("pool", Vt[:, ti, :, 64:66], 1.0)
                for ti in range(s["n"] // 128):
                    t0 = s["t0"] + ti * 128
                    pb = bank("A")
                    for k in range(8):
                        mm(pb[:, 0:272], U[:, k, t0:t0 + 128], wt4[:, k, 0:272], start=(k == 0), stop=(k == 7))
                    if "B" not in _SKIP:
                        cp("act", Vt[:, koff + ti, :, 0:64], pb[:, 0:256].rearrange("p (h e) -> p h e", h=4))
                    if "C" not in _SKIP:
                        tt("dve", DTR[:, s["t0"] // 128 + ti, :], pb[:, 256:272], sbBR[:, BR_DTB:BR_DTB + 16], ALU.add)
                    if not s["sample"] and "D" not in _SKIP:
                        st = sbT[:, 1024:1280]
                        cp("dve", st, pb[:, 0:256])
                        r0 = (s["idx"] * DEPTH + l) * P_LEN + ti * 128
                        dma("sp", nv_d[r0:r0 + 128, :], st, is_out=True)
                chk(3.1)
                wt3 = wg(3)
                koffc = PAST if s["sample"] else 0
                for o in range(0, s["n"], 512):
                    nb = min(512, s["n"] - o)
                    t0 = s["t0"] + o
                    if s["sample"]:
                        cs = sbT[:, 1024:1536]
                        sn = sbT[:, 1536:2048]
                        dma("sp", cs[:, 0:nb], cos_d[:, o:o + nb])
                        dma("sp", sn[:, 0:nb], sin_d[:, o:o + nb])
                    for m in range(2):
                        pk = bank("A")
                        for k in range(8):
                            mm(pk[:, 0:nb], wt3[:, k, m * 128:(m + 1) * 128], U[:, k, t0:t0 + nb], start=(k == 0), stop=(k == 7))
                        if s["sample"]:
                            pkp = bank("A")
                            for k in range(8):
                                mm(pkp[:, 0:nb], wt3[:, k, 256 + m * 128:256 + (m + 1) * 128], U[:, k, t0:t0 + nb], start=(k == 0), stop=(k == 7))
                            t1 = sbT[:, 0:nb]
                            t2 = sbT[:, 512:512 + nb]
                            tt("dve", t1, pk[:, 0:nb], cs[:, 0:nb], ALU.mult)
                            tt("dve", t2, pkp[:, 0:nb], sn[:, 0:nb], ALU.mult)
                            tt("pool", Kf[:, m, koffc + o:koffc + o + nb], t1, t2, ALU.add)
                        else:
                            cp("act", Kf[:, m, koffc + o:koffc + o + nb], pk[:, 0:nb])
                chk(3.2)
                wt2 = wg(2)
                nkt = s["nk"] // 128
                for o in range(0, s["n"], 512):
                    nb = min(512, s["n"] - o)
                    nq = nb // 128
                    t0 = s["t0"] + o
                    QR = bf(sbA[:, AT0:AT0 + 512]).rearrange("p (c t) -> p c t", c=2)
                    if s["sample"]:
                        cs = sbT[:, 1024:1536]
                        sn = sbT[:, 1536:2048]
                        dma("sp", cs[:, 0:nb], cos_d[:, o:o + nb])
                        dma("sp", sn[:, 0:nb], sin_d[:, o:o + nb])
                    for m in range(2):
                        pq = bank("A")
                        for k in range(8):
                            mm(pq[:, 0:nb], wt2[:, k, m * 128:(m + 1) * 128], U[:, k, t0:t0 + nb], start=(k == 0), stop=(k == 7))
                        if s["sample"]:
                            pqp = bank("A")
                            for k in range(8):
                                mm(pqp[:, 0:nb], wt2[:, k, 256 + m * 128:256 + (m + 1) * 128], U[:, k, t0:t0 + nb], start=(k == 0), stop=(k == 7))
                            t1 = sbT[:, 0:nb]
                            t2 = sbT[:, 512:512 + nb]
                            tt("dve", t1, pq[:, 0:nb], cs[:, 0:nb], ALU.mult)
                            tt("dve", t2, pqp[:, 0:nb], sn[:, 0:nb], ALU.mult)
                            tt("pool", QR[:, m, 0:nb], t1, t2, ALU.add)
                        else:
                            cp("act", QR[:, m, 0:nb], pq[:, 0:nb])
                    chk(3.3)
                    YT = bf(sbA[:, AT0 + 512:AT0 + 1024]).rearrange("p (q c) -> p q c", q=4)
                    for i in range(4):
                        OS = sbA[:, AT0 + 2560:AT0 + 2560 + 520].rearrange("p (q s e) -> p q s e", q=4, s=2)
                        cch = i // 2
                        accs = [ps[:, 4 + 2 * (i % 2) + sm_, :] for sm_ in range(2)]
                        for kt in range(nkt):
                            b0 = 2 * (pairc[0] % 2)
                            pairc[0] += 1
                            pst2 = ps[:, b0:b0 + 2, :]
                            for sm in range(2):
                                hrow = (2 * i + sm) % 4
                                mm(pst2[:, sm, 0:nb], Kf[32 * hrow:32 * hrow + 32, cch, kt * 128:(kt + 1) * 128],
                                   QR[32 * hrow:32 * hrow + 32, cch, 0:nb], tpos=(32 * hrow, 0))
                            pofs = AT0 + 1536 + (pairc[0] % 2) * 512
                            PT2 = bf(sbA[:, pofs:pofs + 512]).rearrange("p (u t) -> p u t", u=2)
                            act(PT2[:, :, 0:nb], pst2[:, :, 0:nb], AF.Exp, scale=float(32 ** -0.5))
                            for sm in range(2):
                                for qt in range(nq):
                                    mm(accs[sm][:, qt * 65:(qt + 1) * 65], PT2[:, sm, qt * 128:(qt + 1) * 128], Vt[:, kt, i, 0:65],
                                       start=(kt == 0 and qt == 0), stop=(kt == nkt - 1), skip=True)
                        for sm in range(2):
                            cp("act" if sm else "dve", OS[:, 0:nq, sm, :], accs[sm][:, 0:nq * 65].rearrange("p (q e) -> p q e", q=nq))
                        chk(3.4)
                        RD = sbT2[:, 80:88].rearrange("p (q s) -> p q s", q=4)
                        recip(RD[:, 0:nq, :], OS[:, 0:nq, :, 64])
                        ON = sbT[:, 0:512].rearrange("p (q s e) -> p q s e", q=4, s=2)
                        tt("dve", ON[:, 0:nq], OS[:, 0:nq, :, 0:64], RD[:, 0:nq, :].unsqueeze(3).to_broadcast([128, nq, 2, 64]), ALU.mult)
                        DF = sbT[:, 512:768].rearrange("p (q e) -> p q e", q=4)
                        stt("dve", DF[:, 0:nq], ON[:, 0:nq, 1, :], NLAM, ON[:, 0:nq, 0, :], ALU.mult, ALU.add)
                        SQ = sbT[:, 768:1024].rearrange("p (q e) -> p q e", q=4)
                        tt("pool", SQ[:, 0:nq], DF[:, 0:nq], DF[:, 0:nq], ALU.mult)
                        SSA = sbQ[:, 16:32].rearrange("p (q h) -> p q h", h=4)
                        SS = SSA[:, 0:nq, i]
                        P.add("dve", lambda e, SS=SS, SQ=SQ, nq=nq: e.reduce_sum(out=SS, in_=SQ[:, 0:nq], axis=AX.X),
                              reads=[SQ[:, 0:nq]], writes=[SS])
                        stt("dve", YT[:, 0:nq, i * 64:(i + 1) * 64], DF[:, 0:nq], float(1.0 - lam_init),
                            sbBR[:, BR_DNW:BR_DNW + 64].unsqueeze(1).to_broadcast([128, nq, 64]), ALU.mult, ALU.mult)
                    chk(3.5)
                    SSA = sbQ[:, 16:32].rearrange("p (q h) -> p q h", h=4)
                    ts("dve", SSA[:, 0:nq, :], SSA[:, 0:nq, :], 1.0 / 64, op0=ALU.mult)
                    rsqrt(SSA[:, 0:nq, :], "rms")
                    YT4 = YT.rearrange("p q (h e) -> p q h e", h=4)
                    tt("dve", YT4[:, 0:nq], YT4[:, 0:nq], SSA[:, 0:nq, :].unsqueeze(3).to_broadcast([128, nq, 4, 64]), ALU.mult)
                    for qt in range(nq):
                        pb = bank("C")
                        pbb = pb.bitcast(BF16)
                        for c2 in range(2):
                            tp(pbb[:, c2 * 128:(c2 + 1) * 128], YT[:, qt, c2 * 128:(c2 + 1) * 128], identb)
                        cc = s["b0"] + o + qt * 128
                        cp("act", Bv[:, 6:8, cc:cc + 128], pbb[:, 0:256].rearrange("p (c t) -> p c t", c=2))

            chk(4)
            for s in seqs:
                mset("pool", Bv[:, 0:4, s["b0"] - 2:s["b0"]], 0.0)
                mset("pool", Bv[:, 0:4, s["b0"] + s["n"]:s["b0"] + s["n"] + 2], 0.0)

            DG = bf(sbT2[:, 128:128 + 640]).rearrange("p (r k c) -> p r k c", r=2, k=5)
            for half in range(2):
                wt = wg(half)
                proj_fm(wt, 0, 4, lambda s, o, nb, m, pb: cp("act" if m % 2 else "dve",
                        Bv[:, m, s["b0"] + o:s["b0"] + o + nb], pb))
                for m in range(4):
                    ch = half * 4 + m
                    dg = DG[:, ch % 2]
                    for k in range(5):
                        ts("dve", dg[:, k, :], identb, pv(PV_CW + k * 8 + ch), op0=ALU.mult)
                    for (s, o, nb) in blocks:
                        pb = bank("A")
                        for k in range(5):
                            c0 = s["b0"] + o + k - 2
                            mm(pb[:, 0:nb], dg[:, k, :], Bv[:, m, c0:c0 + nb], start=(k == 0), stop=(k == 4))
                        act(XBC[:, ch, s["t0"] + o:s["t0"] + o + nb], pb[:, 0:nb], AF.Silu, bias=pv(PV_CB + ch))

            chk(5)
            for s in seqs:
                mset("pool", Bv[:, 0:2, s["b0"] - 2:s["b0"]], 0.0)
                mset("pool", Bv[:, 0:2, s["b0"] + s["n"]:s["b0"] + s["n"] + 2], 0.0)
            wt = wg(5)
            for (s, o, nb) in blocks:
                t0 = s["t0"] + o
                for m in range(2):
                    pc = bank("A")
                    ph = bank("A")
                    for k in range(8):
                        mm(pc[:, 0:nb], wt[:, k, m * 128:(m + 1) * 128], U[:, k, t0:t0 + nb], start=(k == 0), stop=(k == 7))
                    for k in range(8):
                        mm(ph[:, 0:nb], wt[:, k, 256 + m * 128:256 + (m + 1) * 128], U[:, k, t0:t0 + nb], start=(k == 0), stop=(k == 7))
                    tmp = sbT[:, 0:nb]
                    cp("act", tmp, ph[:, 0:nb])
                    tt("dve", Bv[:, m, s["b0"] + o:s["b0"] + o + nb], pc[:, 0:nb], tmp, ALU.mult)
            wt6 = wg(6)
            DG3 = bf(sbT2[:, 768:768 + 384]).rearrange("p (m k c) -> p m k c", m=2, k=3)
            for m in range(2):
                for k in range(3):
                    ts("dve", DG3[:, m, k, :], identb, pv(PV_SCW + k * 2 + m), op0=ALU.mult)
            for (s, o, nb) in blocks:
                t0 = s["t0"] + o
                for m in range(2):
                    pbb = bank("A")
                    pcc = bank("A")
                    for k in range(8):
                        mm(pbb[:, 0:nb], wt6[:, k, m * 128:(m + 1) * 128], U[:, k, t0:t0 + nb], start=(k == 0), stop=(k == 7))
                    for k in range(3):
                        c0 = s["b0"] + o + k - 1
                        mm(pcc[:, 0:nb], DG3[:, m, k, :], Bv[:, m, c0:c0 + nb], start=(k == 0), stop=(k == 2))
                    tmp = sbT[:, 512:512 + nb]
                    cp("act", tmp, pcc[:, 0:nb])
                    tt("dve", Bv[:, 4 + m, s["b0"] + o:s["b0"] + o + nb], pbb[:, 0:nb], tmp, ALU.mult)
            for s in seqs:
                if s["sample"]:
                    continue
                for ti in range(s["n"] // 128):
                    t0 = s["t0"] + ti * 128
                    pb = bank("A")
                    for k in range(8):
                        mm(pb[:, 0:256], U[:, k, t0:t0 + 128], wt6[:, k, 256:512], start=(k == 0), stop=(k == 7))
                    st = sbT[:, 1024:1280]
                    cp("act", st, pb[:, 0:256])
                    r0 = (s["idx"] * DEPTH + l) * P_LEN + ti * 128
                    dma("sp", nk_d[r0:r0 + 128, :], st, is_out=True)

            chk(6)
            wt7 = wg(7)

            def zs_view(s, ti):
                cc = s["b0"] + ti * 128
                return Bv[:, 0:4, cc:cc + 128]

            proj_tm(wt7, 0, 512, lambda s, ti, pb: act(zs_view(s, ti), pb.rearrange("p (c t) -> p c t", c=4), AF.Silu))

            DTf = sbT2[:, 1152:1152 + NT * 16]
            act(DTf, DTf, AF.Exp)
            ts("dve", DTf, DTf, 1.0, op0=ALU.add)
            act(DTf, DTf, AF.Ln)
            DAt = sbDER[:, 0:0] if False else None
            DA = sbS[:, 0:0] if False else None
            DAv = sbT2[:, 1408:1408 + NT * 16].rearrange("p (t c) -> p t c", c=16)
            tt("dve", DAv, DTR, ABR.unsqueeze(1).to_broadcast([128, NT, 16]), ALU.mult)

            chk(7)
            Sst = [sbS[:, 0:512], sbS[:, 512:1024]]
            Sbf = [bf(sbS[:, 1024:1280]), bf(sbS[:, 1280:1536])]
            for s in seqs:
                ntile = s["n"] // 128
                for d_ in range(2):
                    if s["sample"]:
                        src = (s0f_d if d_ == 0 else s0b_d)[l]
                        stg = sbT[:, 0:512].rearrange("p (a n) -> p a n", a=4)
                        dma("sp", stg, src.rearrange("(a h2) p n -> (h2 p) a n", h2=2))
                        for a in range(4):
                            pb = bank("C")
                            tp(pb[:, 0:128], stg[:, a, :], identf)
                            cp("act", Sst[d_][:, a * 128:(a + 1) * 128], pb[:, 0:128])
                        cp("dve", Sbf[d_], Sst[d_])
                    else:
                        mset("pool", Sst[d_], 0.0)
                        mset("pool", Sbf[d_], 0.0)
                order = []
                for k_ in range(ntile):
                    order.append((1, ntile - 1 - k_, k_ >= ntile // 2))
                    order.append((0, k_, k_ >= ntile // 2))
                for (d_, ti, comb) in order:
                    par = tdcount[0] % 2
                    tdcount[0] += 1
                    Tq = TSETS[par]
                    gti = s["t0"] // 128 + ti
                    tc0 = s["t0"] + ti * 128
                    da = DAv[:, gti, d_ * 8:(d_ + 1) * 8]
                    dt = DTR[:, gti, d_ * 8:(d_ + 1) * 8]
                    U_, L_, M_ = (C_UF, C_LF, C_MF) if d_ == 0 else (C_UB, C_LB, C_MB)
                    pe_ = bank("S")
                    mm(pe_[:, 0:8], cst(U_), da)
                    mm(pe_[:, 8:16], cst(L_), da)
                    mm(pe_[:, 16:24], cst(C_ONE), da)
                    EX = EXS[par]
                    act(EX, pe_[:, 0:24], AF.Exp)
                    pxs = bank("S")
                    pxsb = pxs.bitcast(BF16)
                    for c4 in range(4):
                        tp(pxsb[:, c4 * 128:(c4 + 1) * 128], XBC[:, c4, tc0:tc0 + 128], identb)
                    xsv = pxsb[:, 0:512].rearrange("p (h e) -> p h e", h=8)
                    XDT = bf(Tq[:, 0:256]).rearrange("p (h e) -> p h e", h=8)
                    XDE = bf(Tq[:, 256:512]).rearrange("p (h e) -> p h e", h=8)
                    tt("dve", XDT, xsv, dt.unsqueeze(2).to_broadcast([128, 8, 64]), ALU.mult)
                    tt("dve", XDE, XDT, EX[:, 8:16].unsqueeze(2).to_broadcast([128, 8, 64]), ALU.mult)
                    if comb:
                        XSD = Tq[:, 512:1024].rearrange("p (h e) -> p h e", h=8)
                        tt("dve", XSD, xsv, sbBR[:, BR_D:BR_D + 8].unsqueeze(2).to_broadcast([128, 8, 64]), ALU.mult)
                    pbt = bank("S")
                    pbtb = pbt.bitcast(BF16)
                    for g in range(2):
                        tp(pbtb[:, g * 128:(g + 1) * 128], XBC[:, 4 + g, tc0:tc0 + 128], identb)
                    BT = bf(Tq[:, 1024:1152])
                    cp("act", BT, pbtb[:, 0:256])
                    pcb = bank("S")
                    for g in range(2):
                        mm(pcb[:, g * 128:(g + 1) * 128], XBC[:, 4 + g, tc0:tc0 + 128], XBC[:, 6 + g, tc0:tc0 + 128])
                    CBM = bf(Tq[:, 1152:1280]).rearrange("p (g i) -> p g i", g=2)
                    tt("dve", CBM, pcb[:, 0:256].rearrange("p (g i) -> p g i", g=2),
                       cstb(M_).unsqueeze(1).to_broadcast([128, 2, 128]), ALU.mult)
                    MT = bf(Tq[:, 1536:2048]).rearrange("p (h i) -> p h i", h=8)
                    for g in range(2):
                        LD = bf(Tq[:, 2048 + g * 256:2048 + (g + 1) * 256]).rearrange("p (h j) -> p h j", h=4)
                        for h4 in range(4):
                            act(LD[:, h4, :], cstb(L_), AF.Copy, scale=da[:, g * 4 + h4:g * 4 + h4 + 1])
                        psg = bank("S")
                        for h4 in range(4):
                            mm(psg[:, h4 * 128:(h4 + 1) * 128], LD[:, h4, :], cstb(U_))
                        EE = bf(Tq[:, 1280:1536]).rearrange("p (h i) -> p h i", h=4)
                        act(EE, psg.rearrange("p (h i) -> p h i", h=4), AF.Exp)
                        tt("dve" if g == 0 else "pool", MT[:, g * 4:(g + 1) * 4, :], EE, CBM[:, g, :].unsqueeze(1).to_broadcast([128, 4, 128]), ALU.mult)
                    if comb:
                        XSDY = bf(Tq[:, 1280:1536])
                        tt("pool", XSDY, Tq[:, 512:1024], YBW[:, gti, :], ALU.add)
                    pyd = bank("S")
                    for h in range(8):
                        mm(pyd[:, h * 64:(h + 1) * 64], MT[:, h, :], XDT[:, h, :], start=(h == 0), stop=(not comb), skip=True)
                    if comb:
                        mm(pyd[:, 0:512], identb, XSDY, start=False, stop=True, skip=True)
                    pyo = bank("S")
                    for g in range(2):
                        mm(pyo[:, g * 256:(g + 1) * 256], XBC[:, 6 + g, tc0:tc0 + 128], Sbf[d_][:, g * 256:(g + 1) * 256])
                    YTMP = Tq[:, 2048:2560].rearrange("p (h e) -> p h e", h=8)
                    tt("dve", YTMP, pyo.rearrange("p (h e) -> p h e", h=8), EX[:, 0:8].unsqueeze(2).to_broadcast([128, 8, 64]), ALU.mult)
                    pst_ = bank("S")
                    for g in range(2):
                        mm(pst_[:, g * 256:(g + 1) * 256], BT[:, g * 128:(g + 1) * 128], bf(Tq[:, 256:512])[:, g * 256:(g + 1) * 256])
                    Sv = Sst[d_].rearrange("p (h e) -> p h e", h=8)
                    tt("pool", Sv, Sv, EX[:, 16:24].unsqueeze(2).to_broadcast([128, 8, 64]), ALU.mult)
                    tt("dve", Sst[d_], Sst[d_], pst_, ALU.add)
                    cp("act", Sbf[d_], Sst[d_])
                    ycol = YBW[:, gti, :]
                    if not comb:
                        tt("dve", ycol, pyd, YTMP.rearrange("p h e -> p (h e)"), ALU.add)
                    else:
                        YS = Tq[:, 2048:2560]
                        tt("dve", YS, pyd, YTMP.rearrange("p h e -> p (h e)"), ALU.add)
                        tt("dve", YS.rearrange("p (c t) -> p c t", c=4), YS.rearrange("p (c t) -> p c t", c=4), zs_view(s, ti), ALU.mult)
                        SSQ = sbQ[:, gti:gti + 1]
                        JK = Tq[:, 1536:2048]
                        mset("pool", SSQ, 0.0)
                        act(JK, YS, AF.Square, accum=SSQ)
                        cp("pool", zs_view(s, ti), YS.rearrange("p (c t) -> p c t", c=4))
                g0 = s["t0"] // 128
                RQ = sbQ[:, g0:g0 + ntile]
                ts("dve", RQ, RQ, 1.0 / 512, op0=ALU.mult)
                rsqrt(RQ, "rms")
                for ti in range(ntile):
                    YN = bf(sbT[:, (ti % 2) * 256:(ti % 2) * 256 + 256]).rearrange("p (c t) -> p c t", c=4)
                    ts("dve", YN, zs_view(s, ti), sbQ[:, g0 + ti:g0 + ti + 1], op0=ALU.mult)
                    pyt = bank("S")
                    pytb = pyt.bitcast(BF16)
                    for c4 in range(4):
                        tp(pytb[:, c4 * 128:(c4 + 1) * 128], YN[:, c4, :], identb)
                    for c4 in range(4):
                        act(zs_view(s, ti)[:, c4, :], pytb[:, c4 * 128:(c4 + 1) * 128], AF.Identity, scale=pv(PV_NW + c4))
                if not s["sample"]:
                    for d_ in range(2):
                        dst = (nsf_d if d_ == 0 else nsb_d)
                        r0 = (s["idx"] * DEPTH + l) * 512
                        for a in range(4):
                            pb = bank("C")
                            tp(pb[:, 0:128], Sst[d_][:, a * 128:(a + 1) * 128], identf)
                            stg = sbT[:, 0:128]
                            cp("act", stg, pb[:, 0:128])
                            dma("sp", dst[r0 + a * 128:r0 + (a + 1) * 128, :], stg, is_out=True)

            chk(8)
            WO = bf(sbU[:, 0:4096]).rearrange("p (k n) -> p k n", k=8) if False else None
            for half in range(2):
                wt = wload(wout_d[l][:, half * 512:(half + 1) * 512], 512)
                for (s, o, nb) in blocks:
                    t0 = s["t0"] + o
                    c = s["cond"]
                    for m in range(4):
                        mo = half * 4 + m
                        pb = bank("A")
                        for ki, k in enumerate((4, 5, 6, 7, 0, 1, 2, 3)):
                            mm(pb[:, 0:nb], wt[:, k, m * 128:(m + 1) * 128], Bv[:, k, s["b0"] + o:s["b0"] + o + nb],
                               start=(ki == 0), stop=(ki == 7))
                        stt("dve", X[:, mo, t0:t0 + nb], pb[:, 0:nb], DER[:, c, 8 + mo:9 + mo], X[:, mo, t0:t0 + nb], ALU.mult, ALU.add)

            def layer_norm(goff, boff, gh, bh, want_h):
                for (t0, nb, c) in dblocks:
                    VB = bf(sbB[:, 4096:4096 + 2048]).rearrange("p (k t) -> p k t", k=8)
                    pm = bank("C")
                    for k in range(8):
                        mm(pm[:, 0:nb], cst(C_ONE), X[:, k, t0:t0 + nb], start=(k == 0), stop=(k == 7))
                    for k in range(8):
                        act(VB[:, k, 0:nb], X[:, k, t0:t0 + nb], AF.Square)
                    pq = bank("C")
                    for k in range(8):
                        mm(pq[:, 0:nb], cstb(C_ONE), VB[:, k, 0:nb], start=(k == 0), stop=(k == 7))
                    lnpar[0] += 1
                    if lnpar[0] % 2:
                        MEAN, RSTD, NMR = sbT[:, 0:nb], sbT[:, 512:512 + nb], sbT[:, 1024:1024 + nb]
                    else:
                        MEAN, RSTD, NMR = sbB[:, 6144:6144 + nb], sbB[:, 6656:6656 + nb], sbB[:, 7168:7168 + nb]
                    act(MEAN, pm[:, 0:nb], AF.Identity, scale=1.0 / D)
                    tt("pool", NMR, MEAN, MEAN, ALU.mult)
                    stt("dve", RSTD, pq[:, 0:nb], 1.0 / D, NMR, ALU.mult, ALU.subtract)
                    rsqrt(RSTD, "ln")
                    stt("dve", NMR, MEAN, -1.0, RSTD, ALU.mult, ALU.mult)
                    for k in range(8):
                        TT = sbT[:, 1536 + (k % 2) * 512:1536 + (k % 2) * 512 + nb]
                        tt("dve", TT, X[:, k, t0:t0 + nb], RSTD, ALU.mult)
                        tt("pool" if k % 8 in (0, 2, 3, 5, 6) else "dve", TT, TT, NMR, ALU.add)
                        if want_h or k % 4 != 3:
                            act(X[:, k, t0:t0 + nb], TT, AF.Identity, bias=pv(boff + k), scale=pv(goff + k))
                        else:
                            ts("dve", X[:, k, t0:t0 + nb], TT, pv(goff + k), pv(boff + k), op0=ALU.mult, op1=ALU.add)
                        if want_h:
                            if k % 4 == 0:
                                act(U[:, k, t0:t0 + nb], TT, AF.Identity, bias=DER[:, c, bh + k:bh + k + 1], scale=DER[:, c, gh + k:gh + k + 1])
                            else:
                                ts("dve", U[:, k, t0:t0 + nb], TT, DER[:, c, gh + k:gh + k + 1], DER[:, c, bh + k:bh + k + 1], op0=ALU.mult, op1=ALU.add)

            chk(9)
            layer_norm(PV_L1G, PV_L1B, 16, 24, True)
            chk(10)

            for q in range(8):
                wu = wload(wup_d[l][:, q * 512:(q + 1) * 512], 512)
                wd = wload(wdn_d[l][q * 512:(q + 1) * 512, :], 1024) if False else None
                WD = bf(sbA[:, (q % 2) * 2048:(q % 2) * 2048 + 2048]).rearrange("p (k n) -> p k n", k=4)
                dma("pool", WD, wdn_d[l][q * 512:(q + 1) * 512, :].rearrange("(k p) n -> p k n", p=128))
                for bi, (t0, nb, c) in enumerate(dblocks):
                    Hd = HID[:, bi % 2]
                    for hc in range(4):
                        pb = bank("A")
                        for k in range(8):
                            mm(pb[:, 0:nb], wu[:, k, hc * 128:(hc + 1) * 128], U[:, k, t0:t0 + nb], start=(k == 0), stop=(k == 7))
                        RL = sbT[:, (hc % 2) * 512:(hc % 2) * 512 + nb]
                        act(RL, pb[:, 0:nb], AF.Relu)
                        tt("pool", Hd[:, hc, 0:nb], RL, RL, ALU.mult)
                    for mo in range(8):
                        pb = bank("A")
                        for hc in range(4):
                            mm(pb[:, 0:nb], WD[:, hc, mo * 128:(mo + 1) * 128], Hd[:, hc, 0:nb], start=(hc == 0), stop=(hc == 3))
                        stt("dve", X[:, mo, t0:t0 + nb], pb[:, 0:nb], DER[:, c, 32 + mo:33 + mo], X[:, mo, t0:t0 + nb], ALU.mult, ALU.add)

            chk(11)
            layer_norm(PV_L2G, PV_L2B, 0, 0, False)
            chk(12)

        stopped = False
        try:
            for l in range(DEPTH):
                _layer(l)
        except _Stop:
            stopped = True
        for ti in range(NT):
            yt = sbT[:, 0:1024]
            for half in range(2):
                pb = bank("A")
                for k in range(4):
                    tp(pb[:, k * 128:(k + 1) * 128], X[:, half * 4 + k, ti * 128:(ti + 1) * 128], identf)
                cp("act" if half else "dve", yt[:, half * 512:(half + 1) * 512], pb)
            dma("sp", y_d[ti * 128:(ti + 1) * 128, :], yt, is_out=True)
        if stopped:
            raise _Stop()

    _PASSOFF[0] = 0
    try:
        pseqs = [dict(n=P_LEN, cond=0, sample=False, idx=i) for i in range(NP_SEQ)]
        run_pass(pseqs, NP_SEQ * P_LEN, xp_d, yp_d)
        sseqs = [dict(n=S_LEN, cond=1, sample=True, idx=0)]
        _PASSOFF[0] = 100
        run_pass(sseqs, S_LEN, xs_d, ys_d)
    except _Stop:
        pass

    P.add("sp", lambda e: e.nop(), extra_deps=out_dmas, cost=50.0)
    P.schedule(window=1024)
    P.plan()
    sems = {}
    keys = [("eng", e) for e in ENGS] + [("dma", e, s) for e in ("sp", "pool") for s in range(NDMASEM)]
    for k in keys:
        sems[k] = es.enter_context(nc.semaphore("_".join(map(str, k))))
    block = es.enter_context(nc.Block())

    @block.tensor
    def _(e):
        P.emit_engine("pe", e, sems)

    @block.scalar
    def _(e):
        P.emit_engine("act", e, sems)

    @block.vector
    def _(e):
        P.emit_engine("dve", e, sems)

    @block.gpsimd
    def _(e):
        P.emit_engine("pool", e, sems)

    @block.sync
    def _(e):
        P.emit_engine("sp", e, sems)

    es.close()
    return nc, P


def host_prep(inp):
    f = lambda a: np.ascontiguousarray(np.asarray(a, dtype=np.float32))
    cols = _win_groups()
    w_in_r = f(np.asarray(inp["w_in"])[:, :, cols])
    L = DEPTH
    pvec = np.zeros((L, 128, NPV), np.float32)
    brow = np.zeros((L, NBR), np.float32)
    for l in range(L):
        pvec[l, :, PV_BMOD:PV_BMOD + 48] = _fm(inp["b_mod"][l])
        cw = np.asarray(inp["ssd_conv_w"][l])
        for k in range(5):
            pvec[l, :, PV_CW + k * 8:PV_CW + k * 8 + 8] = _fm(cw[k])
        pvec[l, :, PV_CB:PV_CB + 8] = _fm(inp["ssd_conv_b"][l])
        sw = np.asarray(inp["sc_conv_w"][l])
        for k in range(3):
            pvec[l, :, PV_SCW + k * 2:PV_SCW + k * 2 + 2] = _fm(sw[k])
        pvec[l, :, PV_L1G:PV_L1G + 8] = _fm(inp["ln1_g"][l])
        pvec[l, :, PV_L1B:PV_L1B + 8] = _fm(inp["ln1_b"][l])
        pvec[l, :, PV_L2G:PV_L2G + 8] = _fm(inp["ln2_g"][l])
        pvec[l, :, PV_L2B:PV_L2B + 8] = _fm(inp["ln2_b"][l])
        brow[l, BR_DTB:BR_DTB + 16] = np.asarray(inp["ssd_dt_bias"][l]).reshape(16)
        brow[l, BR_ALOG:BR_ALOG + 16] = np.asarray(inp["ssd_a_log"][l]).reshape(16)
        brow[l, BR_D:BR_D + 8] = np.asarray(inp["ssd_d"][l])
        pvec[l, :, PV_NW:PV_NW + 4] = _fm(inp["ssd_norm_w"][l])
        brow[l, BR_DNW:BR_DNW + 64] = np.asarray(inp["diff_norm_w"][l])
        brow[l, BR_LAM:BR_LAM + 128] = np.asarray(inp["diff_lambda"][l]).reshape(128)
    cos, sin = _rope_tables()
    shared = {
        "w_mod": f(inp["w_mod"]), "w_in_r": w_in_r, "w_out": f(inp["w_out"]), "w_up": f(inp["w_up"]),
        "w_down": f(inp["w_down"]), "pvec": pvec, "brow": brow, "consts": _consts(),
        "ropecos": f(cos), "ropesin": f(sin),
    }
    return shared


def core_inputs(inp, i, shared):
    f = lambda a: np.ascontiguousarray(np.asarray(a, dtype=np.float32))
    m = dict(shared)
    m["xp"] = f(np.asarray(inp["x_prompt"])[2 * i:2 * i + 2].reshape(NP_SEQ * P_LEN, D))
    m["xs"] = f(np.asarray(inp["x_sample"])[i])
    m["ck"] = f(np.asarray(inp["cache_k"])[i].reshape(DEPTH, PAST, 256))
    m["cv"] = f(np.asarray(inp["cache_v"])[i].reshape(DEPTH, PAST, 256))
    m["s0f"] = f(np.asarray(inp["state_ssm_fwd"])[i])
    m["s0b"] = f(np.asarray(inp["state_ssm_bwd"])[i])
    cv = np.stack([np.asarray(inp["c_ctx"]), np.asarray(inp["c"])[i]], axis=-1)
    m["cvec"] = f(cv.reshape(8, 128, 2).transpose(1, 0, 2).reshape(128, 16))
    return m


_CACHE = {}


def kernel(**inputs):
    if "nc" not in _CACHE:
        _CACHE["nc"] = build()[0]
    nc = _CACHE["nc"]
    shared = host_prep(inputs)
    in_maps = [core_inputs(inputs, i, shared) for i in range(8)]
    res = run_bass_kernel_spmd(nc, in_maps, core_ids=list(range(8)))
    r = res.results
    y_p = np.concatenate([r[i]["yp"].reshape(NP_SEQ, P_LEN, D) for i in range(8)], 0)
    y_s = np.stack([r[i]["ys"] for i in range(8)], 0)
    nk = np.concatenate([r[i]["nk"].reshape(NP_SEQ, DEPTH, P_LEN, 8, 32) for i in range(8)], 0)
    nv = np.concatenate([r[i]["nv"].reshape(NP_SEQ, DEPTH, P_LEN, 4, 64) for i in range(8)], 0)
    nsf = np.concatenate([r[i]["nsf"].reshape(NP_SEQ, DEPTH, 8, 64, 128) for i in range(8)], 0)
    nsb = np.concatenate([r[i]["nsb"].reshape(NP_SEQ, DEPTH, 8, 64, 128) for i in range(8)], 0)
    return (y_p.astype(np.float32), y_s.astype(np.float32), nk.astype(np.float32), nv.astype(np.float32),
            nsf.astype(np.float32), nsb.astype(np.float32))
```
